# Optimizing a Trainium2 kernel written in Bass

```python
import math
import jax
import jax.numpy as jnp
from jax import lax
import numpy as np

D_MODEL = 2048
BATCH = 8
SEQ = 2048
DEPTH = 2

MLA_HEADS = 8
QK_NOPE = 128
QK_ROPE = 64
V_HEAD = 128
Q_LORA = 512
KV_LORA = 512
ROPE_THETA = 10000.0
Q_BLOCK = 128
GDN_HEADS = 8
GDN_DK = 128
GDN_DV = 128
CONV_WIDTH = 4
CHUNK = 64
GDN_QK = GDN_HEADS * GDN_DK
GDN_V = GDN_HEADS * GDN_DV
CONV_CH = 2 * GDN_QK + GDN_V
D_FF = ((8 * D_MODEL // 3 + 255) // 256) * 256
EPS = 1e-6
IN_SIZES = (Q_LORA, KV_LORA, QK_ROPE, GDN_QK, GDN_QK, GDN_V, GDN_V, GDN_HEADS, GDN_HEADS, 2 * D_MODEL)
IN_WIDTH = Q_LORA + KV_LORA + QK_ROPE + 2 * GDN_QK + 2 * GDN_V + 2 * GDN_HEADS + 2 * D_MODEL

kernel_name = 'hybrid_mla_gdn_adaln_block'


def _rmsnorm(x, w):
    xf = x.astype(jnp.float32)
    y = xf * lax.rsqrt(jnp.mean(xf * xf, axis=-1, keepdims=True) + EPS)
    return (y * w.astype(jnp.float32)).astype(x.dtype)


def _split_cols(p):
    outs, off = [], 0
    for n in IN_SIZES:
        outs.append(p[..., off:off + n])
        off += n
    return outs


def _rope_tables(positions):
    inv_freq = 1.0 / (ROPE_THETA ** (jnp.arange(0, QK_ROPE, 2, dtype=jnp.float32) / QK_ROPE))
    ang = positions.astype(jnp.float32)[..., None] * inv_freq
    return jnp.cos(ang), jnp.sin(ang)


def _rope(x, cos, sin):
    xf = x.astype(jnp.float32)
    x1, x2 = jnp.split(xf, 2, axis=-1)
    return jnp.concatenate([x1 * cos - x2 * sin, x2 * cos + x1 * sin], axis=-1).astype(x.dtype)


def _mla_branch(c_q, c_kv, k_pe, q_norm, kv_norm, w_uq, w_ukv, cos, sin):
    B, T, _ = c_q.shape
    q = (_rmsnorm(c_q, q_norm) @ w_uq).reshape(B, T, MLA_HEADS, QK_NOPE + QK_ROPE)
    q_nope, q_pe = q[..., :QK_NOPE], q[..., QK_NOPE:]
    q_pe = _rope(q_pe, cos[:, :, None, :], sin[:, :, None, :])
    kv = (_rmsnorm(c_kv, kv_norm) @ w_ukv).reshape(B, T, MLA_HEADS, QK_NOPE + V_HEAD)
    k_nope, v = kv[..., :QK_NOPE], kv[..., QK_NOPE:]
    k_pe = _rope(k_pe, cos, sin)
    scale = (QK_NOPE + QK_ROPE) ** -0.5
    outs = []
    for i in range(T // Q_BLOCK):
        q0, k_end = i * Q_BLOCK, (i + 1) * Q_BLOCK
        s = (jnp.einsum('bqhd,bkhd->bhqk', q_nope[:, q0:k_end], k_nope[:, :k_end])
             + jnp.einsum('bqhr,bkr->bhqk', q_pe[:, q0:k_end], k_pe[:, :k_end]))
        s = s.astype(jnp.float32) * scale
        mask = jnp.arange(k_end)[None, :] <= (q0 + jnp.arange(Q_BLOCK))[:, None]
        p = jax.nn.softmax(jnp.where(mask, s, -jnp.inf), axis=-1).astype(v.dtype)
        outs.append(jnp.einsum('bhqk,bkhd->bqhd', p, v[:, :k_end]))
    return jnp.concatenate(outs, axis=1).reshape(B, T, MLA_HEADS * V_HEAD)


def _causal_conv_silu(u, w):
    kern = w[:, None, :].astype(u.dtype)
    y = lax.conv_general_dilated(u, kern, window_strides=(1,), padding=[(CONV_WIDTH - 1, 0)],
                                 dimension_numbers=('NWC', 'WIO', 'NWC'),
                                 feature_group_count=u.shape[-1])
    return jax.nn.silu(y)


def _l2norm(x):
    xf = x.astype(jnp.float32)
    return xf * lax.rsqrt(jnp.sum(xf * xf, axis=-1, keepdims=True) + EPS)


def _gated_delta_chunked(q, k, v, beta, g):
    B, T, H, DK = q.shape
    DV = v.shape[-1]
    N = T // CHUNK
    to_chunks = lambda a: a.reshape(B, N, CHUNK, H, -1).transpose(0, 3, 1, 2, 4)
    q, k, v = to_chunks(q), to_chunks(k), to_chunks(v)
    beta = beta.reshape(B, N, CHUNK, H).transpose(0, 3, 1, 2)
    g = g.reshape(B, N, CHUNK, H).transpose(0, 3, 1, 2)
    G = jnp.cumsum(g, axis=-1)
    idx = jnp.arange(CHUNK)
    lower = idx[:, None] >= idx[None, :]
    strict = idx[:, None] > idx[None, :]
    diff = G[..., :, None] - G[..., None, :]
    decay = jnp.where(lower, jnp.exp(jnp.where(lower, diff, 0.0)), 0.0)
    kb = k * beta[..., None]
    Lmat = jnp.where(strict, jnp.einsum('bhncd,bhnsd->bhncs', kb, k) * decay, 0.0)
    A = Lmat + jnp.eye(CHUNK, dtype=jnp.float32)
    rhs = jnp.concatenate([v * beta[..., None], kb * jnp.exp(G)[..., None]], axis=-1)
    sol = lax.linalg.triangular_solve(A, rhs, left_side=True, lower=True, unit_diagonal=True)
    u, w = sol[..., :DV], sol[..., DV:]
    attn = jnp.where(lower, jnp.einsum('bhncd,bhnsd->bhncs', q, k) * decay, 0.0)

    def step(S, inp):
        q_c, k_c, u_c, w_c, G_c, a_c = inp
        v_new = u_c - jnp.einsum('bhcd,bhde->bhce', w_c, S)
        o = (jnp.einsum('bhcd,bhde->bhce', q_c * jnp.exp(G_c)[..., None], S)
             + jnp.einsum('bhcs,bhse->bhce', a_c, v_new))
        G_last = G_c[..., -1]
        k_dec = k_c * jnp.exp(G_last[..., None] - G_c)[..., None]
        S = S * jnp.exp(G_last)[..., None, None] + jnp.einsum('bhcd,bhce->bhde', k_dec, v_new)
        return S, o

    xs = tuple(jnp.moveaxis(a, 2, 0) for a in (q, k, u, w, G, attn))
    S0 = jnp.zeros((B, H, DK, DV), jnp.float32)
    _, o = lax.scan(step, S0, xs)
    return o.transpose(1, 0, 3, 2, 4).reshape(B, T, H, DV)


def _gdn_branch(qkv, z, b_logit, a_logit, conv_w, A_log, dt_bias, gdn_norm):
    B, T, _ = qkv.shape
    dtype = qkv.dtype
    qkv = _causal_conv_silu(qkv, conv_w)
    q = _l2norm(qkv[..., :GDN_QK].reshape(B, T, GDN_HEADS, GDN_DK)) * (GDN_DK ** -0.5)
    k = _l2norm(qkv[..., GDN_QK:2 * GDN_QK].reshape(B, T, GDN_HEADS, GDN_DK))
    v = qkv[..., 2 * GDN_QK:].reshape(B, T, GDN_HEADS, GDN_DV).astype(jnp.float32)
    beta = jax.nn.sigmoid(b_logit.astype(jnp.float32))
    g = -jnp.exp(A_log.astype(jnp.float32)) * jax.nn.softplus(a_logit.astype(jnp.float32) + dt_bias.astype(jnp.float32))
    o = _gated_delta_chunked(q, k, v, beta, g)
    o = o * lax.rsqrt(jnp.mean(o * o, axis=-1, keepdims=True) + EPS) * gdn_norm.astype(jnp.float32)
    o = o * jax.nn.silu(z.reshape(B, T, GDN_HEADS, GDN_DV).astype(jnp.float32))
    return o.reshape(B, T, GDN_V).astype(dtype)


def setup_inputs(seed: int = 0) -> dict:
    key = jax.random.key(seed)
    ks = jax.random.split(key, 24)
    L, D = DEPTH, D_MODEL
    f32 = jnp.float32

    def nrm(k, shape, fan_in, gain=1.0):
        return gain * fan_in ** -0.5 * jax.random.normal(k, shape, f32)

    def gain(k, shape):
        return 1.0 + 0.02 * jax.random.normal(k, shape, f32)

    x = jax.random.normal(ks[0], (BATCH, SEQ, D), f32)
    c = jax.random.normal(ks[1], (BATCH, D), f32)
    positions = (jnp.arange(SEQ, dtype=jnp.int32)[None, :]
                 + jax.random.randint(ks[2], (BATCH, 1), 0, 1024, dtype=jnp.int32))
    dt = jnp.exp(jax.random.uniform(ks[15], (L, GDN_HEADS), f32, math.log(1e-3), math.log(1e-1)))
    return {
        'x': x,
        'c': c,
        'positions': positions,
        'w_ada': nrm(ks[3], (L, D, 6 * D), D, 0.5),
        'b_ada': 0.01 * jax.random.normal(ks[4], (L, 6 * D), f32),
        'norm_mix': gain(ks[5], (L, D)),
        'norm_ffn': gain(ks[6], (L, D)),
        'w_in': nrm(ks[7], (L, D, IN_WIDTH), D),
        'q_a_norm': gain(ks[8], (L, Q_LORA)),
        'kv_a_norm': gain(ks[9], (L, KV_LORA)),
        'w_uq': nrm(ks[10], (L, Q_LORA, MLA_HEADS * (QK_NOPE + QK_ROPE)), Q_LORA),
        'w_ukv': nrm(ks[11], (L, KV_LORA, MLA_HEADS * (QK_NOPE + V_HEAD)), KV_LORA),
        'w_o_mla': nrm(ks[12], (L, MLA_HEADS * V_HEAD, D), MLA_HEADS * V_HEAD),
        'conv_w': nrm(ks[13], (L, CONV_WIDTH, CONV_CH), CONV_WIDTH),
        'A_log': jnp.log(jax.random.uniform(ks[14], (L, GDN_HEADS), f32, 1.0, 16.0)),
        'dt_bias': dt + jnp.log(-jnp.expm1(-dt)),
        'gdn_norm': gain(ks[16], (L, GDN_DV)),
        'w_o_gdn': nrm(ks[17], (L, GDN_V, D), GDN_V),
        'w_o': nrm(ks[18], (L, D, D), D),
        'w_gate_up': nrm(ks[19], (L, D, 2 * D_FF), D),
        'w_down': nrm(ks[20], (L, D_FF, D), D_FF),
        'final_norm': gain(ks[21], (D,)),
    }


def reference(x, c, positions, w_ada, b_ada, norm_mix, norm_ffn, w_in, q_a_norm, kv_a_norm,
              w_uq, w_ukv, w_o_mla, conv_w, A_log, dt_bias, gdn_norm, w_o_gdn, w_o,
              w_gate_up, w_down, final_norm):
    cos, sin = _rope_tables(positions)
    c_act = jax.nn.silu(c)
    for l in range(DEPTH):
        mod = c_act @ w_ada[l] + b_ada[l]
        sh_a, sc_a, gt_a, sh_f, sc_f, gt_f = [m[:, None, :] for m in jnp.split(mod, 6, axis=-1)]
        h = _rmsnorm(x, norm_mix[l]) * (1.0 + sc_a) + sh_a
        p = h @ w_in[l]
        c_q, c_kv, k_pe, q_g, k_g, v_g, z, b_logit, a_logit, gate_logits = _split_cols(p)
        y_a = _mla_branch(c_q, c_kv, k_pe, q_a_norm[l], kv_a_norm[l], w_uq[l], w_ukv[l], cos, sin) @ w_o_mla[l]
        qkv = jnp.concatenate([q_g, k_g, v_g], axis=-1)
        y_b = _gdn_branch(qkv, z, b_logit, a_logit, conv_w[l], A_log[l], dt_bias[l], gdn_norm[l]) @ w_o_gdn[l]
        g_a, g_b = jnp.split(jax.nn.sigmoid(gate_logits), 2, axis=-1)
        mix = (g_a * y_a + g_b * y_b) @ w_o[l]
        x = x + gt_a * mix
        h = _rmsnorm(x, norm_ffn[l]) * (1.0 + sc_f) + sh_f
        gate, up = jnp.split(h @ w_gate_up[l], 2, axis=-1)
        x = x + gt_f * ((jax.nn.silu(gate) * up) @ w_down[l])
    return _rmsnorm(x, final_norm)
```

```python
import numpy as np
from contextlib import ExitStack
import concourse.bass as bass
import concourse.mybir as mybir
from concourse.bass_utils import run_bass_kernel_spmd

F32 = mybir.dt.float32
BF16 = mybir.dt.bfloat16
I32 = mybir.dt.int32
AF = mybir.ActivationFunctionType
ALU = mybir.AluOpType
AX = mybir.AxisListType


class Cfg:
    def __init__(self, D=2048, T=2048, L=2, H=8, QL=512, KVL=512, GH=8, DFF=5632):
        self.D, self.T, self.L, self.H, self.QL, self.KVL, self.GH, self.DFF = D, T, L, H, QL, KVL, GH, DFF
        self.KD = D // 128
        self.TB = min(512, T)
        self.NTB = T // self.TB
        self.NT128 = T // 128
        self.NCH = T // 64
        self.NB = min(256, T)
        self.NLq, self.NLk = QL // 128, KVL // 128
        self.o_cq = 0
        self.o_ckv = QL
        self.o_kpe = QL + KVL
        self.o_q = self.o_kpe + 64
        self.o_k = self.o_q + GH * 128
        self.o_v = self.o_k + GH * 128
        self.o_z = self.o_v + GH * 128
        self.o_b = self.o_z + GH * 128
        self.o_a = self.o_b + GH
        self.o_g = self.o_a + GH
        self.INW = self.o_g + 2 * D
        self.EPS = 1e-6
        self.HG = min(4, GH)
        nff = DFF // 128
        self.FQ = max(d for d in range(1, 12) if nff % d == 0)
        self.NFP = nff // self.FQ


class Ev:
    __slots__ = ("sem", "val")

    def __init__(self, sem, val=None):
        self.sem = sem
        self.val = val


class Buf:
    __slots__ = ("name", "w", "r")

    def __init__(self, name="b"):
        self.name = name
        self.w = None
        self.r = {}


class Eng:
    def __init__(self, name, eng, sem):
        self.name, self.eng, self.sem = name, eng, sem
        self.cnt = 0
        self.seen = {}
        self.pending = []
        self.nwait = 0
        self.nins = 0


class MK:
    def __init__(self, nc, stack, n_dma_sems=24):
        self.nc = nc
        ec = stack.enter_context
        self.pe = Eng("pe", nc.tensor, ec(nc.semaphore("s_pe")))
        self.act = Eng("act", nc.scalar, ec(nc.semaphore("s_act")))
        self.dve = Eng("dve", nc.vector, ec(nc.semaphore("s_dve")))
        self.pool = Eng("pool", nc.gpsimd, ec(nc.semaphore("s_pool")))
        self.sp = Eng("sp", nc.sync, ec(nc.semaphore("s_sp")))
        self.engs = [self.pe, self.act, self.dve, self.pool, self.sp]
        self.dsems = {}
        for q in (self.sp, self.pool):
            self.dsems[q.name] = [[ec(nc.semaphore(f"d_{q.name}{i}")), 0, None] for i in range(n_dma_sems)]
        self.drr = {"sp": 0, "pool": 0}

    def _wait(self, E, evs):
        best = {}
        for ev in evs:
            if ev is None:
                continue
            if ev.val is None:
                assert ev.sem is E.sem and E is self.pe, f"unresolved event waited by {E.name}"
                continue
            k = id(ev.sem)
            if k not in best or best[k].val < ev.val:
                best[k] = ev
        for k, ev in best.items():
            if E.seen.get(k, 0) < ev.val:
                E.eng.wait_ge(ev.sem, ev.val)
                E.seen[k] = ev.val
                E.nwait += 1

    @staticmethod
    def _deps(reads, writes):
        need = []
        for b in reads:
            need.append(b.w)
        for b in writes:
            need.append(b.w)
            need.extend(b.r.values())
        return need

    @staticmethod
    def _commit(ev, reads, writes):
        for b in reads:
            b.r[id(ev.sem)] = ev
        for b in writes:
            b.w = ev
            b.r = {}

    def op(self, E, fn, reads=(), writes=(), signal=True):
        self._wait(E, self._deps(reads, writes))
        ins = fn(E.eng)
        E.nins += 1
        ev = Ev(E.sem)
        E.pending.append(ev)
        if signal:
            E.cnt += 1
            ins.then_inc(E.sem, 1)
            for p in E.pending:
                p.val = E.cnt
            E.pending = []
        self._commit(ev, reads, writes)
        return ev

    def dma(self, Q, out_ap, in_ap, reads=(), writes=()):
        pool = self.dsems[Q.name]
        i = self.drr[Q.name]
        self.drr[Q.name] = (i + 1) % len(pool)
        slot = pool[i]
        need = self._deps(reads, writes)
        need.append(slot[2])
        self._wait(Q, need)
        slot[1] += 16
        ev = Ev(slot[0], slot[1])
        Q.eng.dma_start(out=out_ap, in_=in_ap).then_inc(slot[0], 16)
        Q.nins += 1
        slot[2] = ev
        self._commit(ev, reads, writes)
        return ev

    def barrier(self):
        for Q in (self.sp, self.pool):
            self._wait(Q, [s[2] for s in self.dsems[Q.name]])
            assert not Q.pending
            Q.cnt += 1
            Q.eng.sem_inc(Q.sem, 1)
        assert not self.pe.pending
        evs = [Ev(E.sem, E.cnt) for E in self.engs if E.cnt > 0]
        for E in self.engs:
            self._wait(E, evs)


class Ring:
    def __init__(self, items):
        self.items = items
        self.i = 0

    def next(self):
        it = self.items[self.i]
        self.i = (self.i + 1) % len(self.items)
        return it


def make_consts():
    c = np.zeros((128, 640), np.float32)
    c[:, 0:128] = np.eye(128)
    k = np.arange(64)
    c[0:64, 128:192] = (k[:, None] <= k[None, :])
    c[0:64, 192:256] = (k[:, None] > k[None, :])
    c[0:64, 256:320] = (k[:, None] >= k[None, :])
    c[:, 320:448] = 1.0
    p = np.arange(128)
    c[:, 448:576] = (p[:, None] <= p[None, :])
    inv = (1.0 / (10000.0 ** (np.arange(0, 64, 2, dtype=np.float32) / 64))).astype(np.float32)
    c[0:64, 576] = np.concatenate([inv, inv])
    c[0:32, 577] = -1.0
    c[32:64, 577] = 1.0
    return c


class Builder:
    def __init__(self, cfg, nlayers=None, debug_dump=False):
        self.cfg = cfg
        self.nl = cfg.L if nlayers is None else nlayers
        self.nc = bass.Bass("TRN2", target_bir_lowering=False)
        self._uid = 0

    def I(self, E, name, *a, R=(), W=(), sig=True, **kw):
        return self.mk.op(E, lambda e: getattr(e, name)(*a, **kw), reads=R, writes=W, signal=sig)

    def MM(self, out, lhsT, rhs, start, stop, R, W, sig):
        return self.mk.op(self.mk.pe, lambda e: e.matmul(out, lhsT=lhsT, rhs=rhs, start=start, stop=stop),
                          reads=R, writes=W, signal=sig)

    def TR(self, out, in_, ident, R, W, sig=True):
        return self.mk.op(self.mk.pe, lambda e: e.transpose(out, in_, ident), reads=R, writes=W, signal=sig)

    def ACT(self, out, in_, func, R, W, bias=None, scale=1.0):
        if bias is None:
            return self.mk.op(self.mk.act, lambda e: e.activation(out, in_, func, scale=scale), reads=R, writes=W)
        return self.mk.op(self.mk.act, lambda e: e.activation(out, in_, func, bias=bias, scale=scale), reads=R, writes=W)

    def sb(self, shape, dtype, name=None):
        self._uid += 1
        t = self.scopes[-1].enter_context(self.nc.sbuf_tensor(f"{name or 't'}_{self._uid}", list(shape), dtype))
        return t

    def push(self):
        st = ExitStack()
        st.__enter__()
        self.scopes.append(st)

    def pop(self):
        self.mk.barrier()
        st = self.scopes.pop()
        st.__exit__(None, None, None)

    def dram(self, name, shape, dtype, kind):
        return self.nc.dram_tensor(name, list(shape), dtype, kind=kind).ap()

    def build(self):
        cfg, nc = self.cfg, self.nc
        D, T, L, KD = cfg.D, cfg.T, cfg.L, cfg.KD
        dr = self.dram
        self.x_d = dr("x", [T, D], F32, "ExternalInput")
        self.c_d = dr("c", [KD, 128], F32, "ExternalInput")
        self.pos_d = dr("pos", [1, T], I32, "ExternalInput")
        self.cst_d = dr("consts", [128, 640], F32, "ExternalInput")
        self.w_ada = dr("w_ada", [L, D, 6 * D], F32, "ExternalInput")
        self.b_ada = dr("b_ada", [L, 6 * KD, 128], F32, "ExternalInput")
        self.norm_mix = dr("norm_mix", [L, KD, 128], F32, "ExternalInput")
        self.norm_ffn = dr("norm_ffn", [L, KD, 128], F32, "ExternalInput")
        self.w_in = dr("w_in", [L, D, cfg.INW], F32, "ExternalInput")
        self.q_a_norm = dr("q_a_norm", [L, cfg.NLq, 128], F32, "ExternalInput")
        self.kv_a_norm = dr("kv_a_norm", [L, cfg.NLk, 128], F32, "ExternalInput")
        self.w_uq = dr("w_uq", [L, cfg.QL, cfg.H * 192], F32, "ExternalInput")
        self.w_ukv = dr("w_ukv", [L, cfg.KVL, cfg.H * 256], F32, "ExternalInput")
        self.w_o_mla = dr("w_o_mla", [L, cfg.H * 128, D], F32, "ExternalInput")
        self.conv_w = dr("conv_w", [L, 4, 3 * cfg.GH * 128], F32, "ExternalInput")
        self.A_log = dr("A_log", [L, 1, cfg.GH], F32, "ExternalInput")
        self.dt_bias = dr("dt_bias", [L, 1, cfg.GH], F32, "ExternalInput")
        self.gdn_norm = dr("gdn_norm", [L, 1, 128], F32, "ExternalInput")
        self.w_o_gdn = dr("w_o_gdn", [L, cfg.GH * 128, D], F32, "ExternalInput")
        self.w_o = dr("w_o", [L, D, D], F32, "ExternalInput")
        self.w_gate_up = dr("w_gate_up", [L, D, 2 * cfg.DFF], F32, "ExternalInput")
        self.w_down = dr("w_down", [L, cfg.DFF, D], F32, "ExternalInput")
        self.final_norm = dr("final_norm", [KD, 128], F32, "ExternalInput")
        self.out_d = dr("out", [T, D], F32, "ExternalOutput")
        self.xT = dr("s_xT", [KD, 128, T], F32, "Internal")
        self.s_lat = dr("s_lat", [cfg.NLq + cfg.NLk, 128, T], F32, "Internal")
        self.s_kpe = dr("s_kpe", [2, 64, T], F32, "Internal")
        self.s_qkv = dr("s_qkv", [3 * cfg.GH, 128, T], F32, "Internal")
        self.s_z = dr("s_z", [cfg.GH, 128, T], F32, "Internal")
        self.s_ba = dr("s_ba", [2 * cfg.GH, T], F32, "Internal")
        self.s_gate = dr("s_gate", [2 * KD, 128, T], BF16, "Internal")
        self.s_ma = dr("s_ma", [KD, 128, T], BF16, "Internal")
        self.xTB = {}
        self.outB = Buf("out")

        with ExitStack() as top:
            self.mk = MK(nc, top)
            self.scopes = [top]
            mk = self.mk
            self.ps = [top.enter_context(nc.psum_tensor(f"ps{i}", [128, 512], F32)) for i in range(8)]
            self.pb = [Buf(f"ps{i}") for i in range(8)]
            stop = getattr(self, "stop_after", None)
            order = ["globals", "x0", "ada", "rope", "norm0", "proj", "mla", "gdn", "norm1", "ffn", "final"]
            lim = order.index(stop) if stop else len(order)
            on = lambda nm: order.index(nm) <= lim
            self.setup_globals()
            if on("x0"):
                self.phase_x0()
            if on("ada"):
                self.phase_ada()
            if on("rope"):
                self.phase_rope()
            for l in range(self.nl):
                if on("norm0"):
                    self.push()
                    hT = self.sb([128, KD, T], BF16, "hT")
                    hB = Buf("hT")
                    self.phase_norm(l, 0, hT, hB)
                    if on("proj"):
                        self.phase_proj(l, hT, hB)
                    self.pop()
                if on("mla"):
                    self.phase_mla(l)
                if on("gdn"):
                    self.phase_gdn(l)
                if on("norm1"):
                    self.push()
                    hT = self.sb([128, KD, T], BF16, "hT2")
                    hB = Buf("hT2")
                    self.phase_norm(l, 1, hT, hB)
                    if on("ffn"):
                        self.phase_ffn(l, hT, hB)
                    self.pop()
            if on("final"):
                self.phase_final()
            mk._wait(mk.sp, [self.outB.w] + list(self.outB.r.values()))
            mk.barrier()
            self.stats = {e.name: (e.nins, e.nwait) for e in mk.engs}
        return nc

    def setup_globals(self):
        cfg, mk = self.cfg, self.mk
        KD, L = cfg.KD, cfg.L
        self.cst = self.sb([128, 640], F32, "cst")
        self.cstB = Buf("cst")
        mk.dma(mk.sp, self.cst[:], self.cst_d, writes=[self.cstB])
        c = self.cst
        self.identF = c[:, 0:128]
        self.U64 = c[0:64, 128:192]
        self.SL64 = c[0:64, 192:256]
        self.LINC = c[0:64, 256:320]
        self.onesF = c[:, 320:448]
        self.invf = c[0:64, 576:577]
        self.sgn = c[0:64, 577:578]
        self.cb = self.sb([128, 384], BF16, "cb")
        self.cbB = Buf("cb")
        self.identB = self.cb[:, 0:128]
        self.onesB = self.cb[:, 128:256]
        self.causB = self.cb[:, 256:384]
        self.I(mk.dve, "tensor_copy", self.cb[:, 0:128], c[:, 0:128], R=[self.cstB], W=[self.cbB])
        self.I(mk.dve, "tensor_copy", self.cb[:, 128:256], c[:, 320:448], R=[self.cstB], W=[self.cbB])
        self.I(mk.dve, "tensor_copy", self.cb[:, 256:384], c[:, 448:576], R=[self.cstB], W=[self.cbB])
        self.cc = self.sb([128, 4], F32, "cc")
        self.ccB = Buf("cc")
        self.I(mk.dve, "memset", self.cc[:, 0:1], cfg.EPS, W=[self.ccB])
        self.I(mk.dve, "memset", self.cc[:, 1:2], 1.0, W=[self.ccB])
        self.I(mk.dve, "memset", self.cc[:, 2:3], 0.0, W=[self.ccB])
        self.epsc = self.cc[:, 0:1]
        self.onec = self.cc[:, 1:2]
        self.par = self.sb([128, L, 8 * KD + 16], F32, "par")
        self.parB = Buf("par")
        self.mod = self.sb([128, L, 8 * KD], F32, "mod")
        self.modB = Buf("mod")
        self.cw = self.sb([128, L, 3 * cfg.GH, 4], F32, "cw")
        self.cwB = Buf("cw")
        self.fn = self.sb([128, KD], F32, "fn")
        self.fnB = Buf("fn")
        self.cact = self.sb([128, KD], BF16, "cact")
        self.cactB = Buf("cact")
        self.grow = self.sb([64, L, 2, cfg.GH], F32, "grow")
        self.growB = Buf("grow")
        self.push()
        rows = self.sb([128, 128], F32, "rows")
        rowsB = Buf("rows")
        pst, pstB = self.ps[0], self.pb[0]

        def load_cols(dst, dstB, src, n):
            mk.dma(mk.sp, rows[0:n, :], src, writes=[rowsB])
            self.TR(pst[:, 0:n], rows[0:n, :], self.identF[0:n, 0:n], R=[rowsB, self.cstB], W=[pstB])
            self.I(mk.dve, "tensor_copy", dst, pst[:, 0:n], R=[pstB], W=[dstB])
        for l in range(L):
            o = 0
            load_cols(self.par[:, l, o:o + KD], self.parB, self.norm_mix[l], KD); o += KD
            load_cols(self.par[:, l, o:o + KD], self.parB, self.norm_ffn[l], KD); o += KD
            for j0 in range(0, 6 * KD, 128):
                n = min(128, 6 * KD - j0)
                load_cols(self.par[:, l, o + j0:o + j0 + n], self.parB, self.b_ada[l, j0:j0 + n, :], n)
            o += 6 * KD
            load_cols(self.par[:, l, o:o + cfg.NLq], self.parB, self.q_a_norm[l], cfg.NLq)
            load_cols(self.par[:, l, o + 4:o + 4 + cfg.NLk], self.parB, self.kv_a_norm[l], cfg.NLk)
            load_cols(self.par[:, l, o + 8:o + 9], self.parB, self.gdn_norm[l], 1)
            for j in range(3 * cfg.GH):
                mk.dma(mk.sp, rows[0:4, :], self.conv_w[l, :, j * 128:(j + 1) * 128], writes=[rowsB])
                self.TR(pst[:, 0:4], rows[0:4, :], self.identF[0:4, 0:4], R=[rowsB, self.cstB], W=[pstB])
                self.I(mk.dve, "tensor_copy", self.cw[:, l, j, :], pst[:, 0:4], R=[pstB], W=[self.cwB])
            mk.dma(mk.sp, self.grow[:, l, 0, :], self.A_log[l].partition_broadcast(64), writes=[self.growB])
            mk.dma(mk.sp, self.grow[:, l, 1, :], self.dt_bias[l].partition_broadcast(64), writes=[self.growB])
            self.ACT(self.grow[:, l, 0, :], self.grow[:, l, 0, :], AF.Exp, R=[self.growB], W=[self.growB])
            self.I(mk.dve, "tensor_scalar", self.grow[:, l, 0, :], self.grow[:, l, 0, :], -1.0, None, op0=ALU.mult,
                   R=[self.growB], W=[self.growB])
        load_cols(self.fn[:, :], self.fnB, self.final_norm, KD)
        ctmp = self.sb([128, KD], F32, "ctmp")
        ctB = Buf("ctmp")
        load_cols(ctmp[:, :], ctB, self.c_d, KD)
        self.ACT(self.cact[:, :], ctmp[:, :], AF.Silu, R=[ctB], W=[self.cactB])
        self.pop()
        self.o_nmix, self.o_nffn, self.o_bada, self.o_qan, self.o_kvan, self.o_gn = 0, KD, 2 * KD, 8 * KD, 8 * KD + 4, 8 * KD + 8

    def phase_x0(self):
        cfg, mk = self.cfg, self.mk
        KD = cfg.KD
        self.push()
        xin = Ring([(self.sb([128, cfg.D], F32, "xin"), Buf("xin")) for _ in range(2)])
        stg = Ring([(self.sb([128, KD, 128], F32, "xst"), Buf("xst")) for _ in range(2)])
        banks = Ring(list(range(8)))
        gsz = min(4, KD)
        xTv = self.xT.rearrange("k p t -> p k t")
        for i in range(cfg.NT128):
            xt, xtB = xin.next()
            mk.dma(mk.sp, xt[:], self.x_d[i * 128:(i + 1) * 128, :], writes=[xtB])
            st, stB = stg.next()
            for k0 in range(0, KD, gsz):
                b = banks.next()
                for j in range(gsz):
                    self.TR(self.ps[b][:, j * 128:(j + 1) * 128], xt[:, (k0 + j) * 128:(k0 + j + 1) * 128], self.identF,
                            R=[xtB, self.cstB], W=[self.pb[b]], sig=(j == gsz - 1))
                self.I(mk.dve if (k0 // gsz) % 2 == 0 else mk.act, "tensor_copy" if (k0 // gsz) % 2 == 0 else "copy",
                       st[:, k0:k0 + gsz, :], self.ps[b][:, 0:gsz * 128].rearrange("p (k t) -> p k t", k=gsz),
                       R=[self.pb[b]], W=[stB])
            mk.dma(mk.sp, xTv[:, :, i * 128:(i + 1) * 128], st[:], reads=[stB])
        self.pop()

    def phase_ada(self):
        cfg, mk = self.cfg, self.mk
        KD = cfg.KD
        self.push()
        slots = Ring([(self.sb([128, KD, 512], BF16, "wada"), Buf("wada")) for _ in range(3)])
        psA, psAB = self.ps[0], self.pb[0]
        for l in range(self.nl):
            nblk = 6 * cfg.D // 512
            for jb in range(nblk):
                sl, slB = slots.next()
                mk.dma(mk.pool, sl[:], self.w_ada[l][:, jb * 512:(jb + 1) * 512].rearrange("(k p) n -> p k n", p=128), writes=[slB])
                for jj in range(4):
                    j = jb * 4 + jj
                    for k in range(KD):
                        self.MM(psA[:, j:j + 1], sl[:, k, jj * 128:(jj + 1) * 128], self.cact[:, k:k + 1], k == 0, k == KD - 1,
                                R=[slB, self.cactB], W=[psAB], sig=(k == KD - 1))
            m = self.mod
            self.I(mk.dve, "tensor_tensor", m[:, l, 0:6 * KD], psA[:, 0:6 * KD], self.par[:, l, self.o_bada:self.o_bada + 6 * KD],
                   op=ALU.add, R=[psAB, self.parB], W=[self.modB])
            self.I(mk.dve, "scalar_tensor_tensor", m[:, l, 6 * KD:7 * KD], m[:, l, KD:2 * KD], 1.0, self.par[:, l, 0:KD],
                   op0=ALU.add, op1=ALU.mult, R=[self.modB, self.parB], W=[self.modB])
            self.I(mk.dve, "scalar_tensor_tensor", m[:, l, 7 * KD:8 * KD], m[:, l, 4 * KD:5 * KD], 1.0, self.par[:, l, KD:2 * KD],
                   op0=ALU.add, op1=ALU.mult, R=[self.modB, self.parB], W=[self.modB])
        self.pop()

    def phase_rope(self):
        cfg, mk = self.cfg, self.mk
        T = cfg.T
        self.COS = self.sb([64, T], F32, "cos")
        self.SINS = self.sb([64, T], F32, "sins")
        self.ropeB = Buf("rope")
        self.push()
        pi_ = self.sb([64, T], I32, "posi")
        t0 = self.sb([64, T], F32, "t0")
        t1 = self.sb([64, T], F32, "t1")
        t2 = self.sb([64, T], F32, "t2")
        B = Buf("ropetmp")
        mk.dma(mk.sp, pi_[:], self.pos_d.partition_broadcast(64), writes=[B])
        V = lambda name, *a, **kw: self.I(mk.dve, name, *a, R=[B, self.cstB], W=[B], **kw)
        V("tensor_copy", t0[:], pi_[:])
        V("tensor_scalar", t0[:], t0[:], self.invf, None, op0=ALU.mult)
        V("tensor_scalar", t0[:], t0[:], float(1.0 / (2 * np.pi)), None, op0=ALU.mult)
        for which in range(2):
            dst = self.SINS if which == 0 else self.COS
            if which == 1:
                V("tensor_scalar", t0[:], t0[:], 0.25, None, op0=ALU.add)
            V("tensor_copy", pi_[:], t0[:])
            V("tensor_copy", t1[:], pi_[:])
            V("tensor_tensor", t1[:], t0[:], t1[:], op=ALU.subtract)
            V("tensor_scalar", t2[:], t1[:], 0.5, None, op0=ALU.is_ge)
            V("tensor_tensor", t1[:], t1[:], t2[:], op=ALU.subtract)
            V("tensor_scalar", t2[:], t1[:], -0.5, None, op0=ALU.is_lt)
            V("tensor_tensor", t1[:], t1[:], t2[:], op=ALU.add)
            self.ACT(dst[:], t1[:], AF.Sin, R=[B], W=[self.ropeB], scale=float(2 * np.pi))
        self.I(mk.dve, "tensor_scalar", self.SINS[:], self.SINS[:], self.sgn, None, op0=ALU.mult, R=[self.ropeB, self.cstB], W=[self.ropeB])
        self.pop()

    def phase_norm(self, l, which, hT, hB):
        cfg, mk = self.cfg, self.mk
        KD, NB = cfg.KD, cfg.NB
        a_off = (6 * KD) if which == 0 else (7 * KD)
        sh_off = 0 if which == 0 else 3 * KD
        self.push()
        xb = Ring([(self.sb([128, KD, NB], F32, "xb"), Buf("xb")) for _ in range(2)])
        sq = Ring([(self.sb([128, KD, NB], BF16, "sq"), Buf("sq")) for _ in range(2)])
        rs = Ring([(self.sb([128, NB], F32, "rs"), Buf("rs")) for _ in range(2)])
        tmp = Ring([(self.sb([128, NB], F32, "tmp"), Buf("tmp")) for _ in range(4)])
        banks = Ring([0, 1])
        xTv = self.xT.rearrange("k p t -> p k t")
        for blk in range(cfg.T // NB):
            sl = slice(blk * NB, (blk + 1) * NB)
            x_, xB = xb.next()
            rd = [self.xTB[key] for key in self.xTB]
            mk.dma(mk.sp, x_[:], xTv[:, :, sl], reads=rd, writes=[xB])
            s_, sB = sq.next()
            self.ACT(s_[:], x_[:], AF.Square, R=[xB], W=[sB])
            b = banks.next()
            for k in range(KD):
                self.MM(self.ps[b][:, 0:NB], self.onesB, s_[:, k, :], k == 0, k == KD - 1, R=[self.cbB, sB], W=[self.pb[b]], sig=(k == KD - 1))
            r_, rB = rs.next()
            self.ACT(r_[:], self.ps[b][:, 0:NB], AF.Sqrt, R=[self.pb[b], self.ccB], W=[rB], bias=self.epsc, scale=1.0 / cfg.D)
            self.I(mk.dve, "reciprocal", r_[:], r_[:], R=[rB], W=[rB])
            for k in range(KD):
                t_, tB = tmp.next()
                self.I(mk.dve, "scalar_tensor_tensor", t_[:], x_[:, k, :], self.mod[:, l, a_off + k:a_off + k + 1], r_[:],
                       op0=ALU.mult, op1=ALU.mult, R=[xB, self.modB, rB], W=[tB])
                self.ACT(hT[:, k, sl], t_[:], AF.Identity, R=[tB, self.modB], W=[hB], bias=self.mod[:, l, sh_off + k:sh_off + k + 1])
        self.pop()

    def stream_mm(self, W2d, blocks, rhs, rhsB, KC, evac, slots, extraR=()):
        cfg, mk = self.cfg, self.mk
        TB, NTB = cfg.TB, cfg.NTB
        nsets = 8 // NTB
        Wv = W2d.rearrange("(k p) n -> p k n", p=128)
        for segs, groups in blocks:
            sl, slB = slots.next()
            off = 0
            for (c0, n) in segs:
                mk.dma(mk.pool, sl[:, 0:KC, off:off + n], Wv[:, :, c0:c0 + n], writes=[slB])
                off += n
            for (goff, M, gid) in groups:
                s = self._pset
                self._pset = (self._pset + 1) % nsets
                for k in range(KC):
                    for tb in range(NTB):
                        b = s * NTB + tb
                        self.MM(self.ps[b][0:M, 0:TB], sl[:, k, goff:goff + M], rhs[:, k, tb * TB:(tb + 1) * TB], k == 0, k == KC - 1,
                                R=[slB, rhsB] + list(extraR), W=[self.pb[b]], sig=(k == KC - 1))
                for tb in range(NTB):
                    b = s * NTB + tb
                    evac(gid, M, tb, self.ps[b][0:M, 0:TB], self.pb[b])

    def col_blocks(self, c_lo, ncols, gid0=0):
        blocks = []
        g = gid0
        for c0 in range(c_lo, c_lo + ncols, 512):
            n = min(512, c_lo + ncols - c0)
            groups = []
            for o in range(0, n, 128):
                groups.append((o, min(128, n - o), g))
                g += 1
            blocks.append(([(c0, n)], groups))
        return blocks

    def phase_proj(self, l, hT, hB):
        cfg, mk = self.cfg, self.mk
        KD, TB = cfg.KD, cfg.TB
        self._pset = 0
        slots = Ring([(self.sb([128, KD, 512], BF16, "win"), Buf("win")) for _ in range(3)])
        stg = Ring([(self.sb([128, TB], F32, "pstg"), Buf("pstg")) for _ in range(6)])
        stgb = Ring([(self.sb([128, TB], BF16, "pstgb"), Buf("pstgb")) for _ in range(4)])
        flip = [0]

        def mk_evac(dst_fn, sigm=False):
            def evac(gid, M, tb, psap, psB):
                if sigm:
                    st, stB = stgb.next()
                    self.ACT(st[0:M, :], psap, AF.Sigmoid, R=[psB], W=[stB])
                else:
                    st, stB = stg.next()
                    flip[0] ^= 1
                    if flip[0]:
                        self.ACT(st[0:M, :], psap, AF.Copy, R=[psB], W=[stB])
                    else:
                        self.I(mk.dve, "tensor_copy", st[0:M, :], psap, R=[psB], W=[stB])
                mk.dma(mk.sp, dst_fn(gid, M, tb), st[0:M, :], reads=[stB])
            return evac
        W = self.w_in[l]
        tsl = lambda tb: slice(tb * TB, (tb + 1) * TB)
        self.stream_mm(W, self.col_blocks(0, cfg.QL + cfg.KVL), hT, hB, KD,
                       mk_evac(lambda g, M, tb: self.s_lat[g, :, tsl(tb)]), slots)
        o = cfg.o_kpe
        blocks = [([(o, 64), (o + 32, 32), (o, 32)], [(0, 64, 0), (64, 64, 1)])]
        self.stream_mm(W, blocks, hT, hB, KD, mk_evac(lambda g, M, tb: self.s_kpe[g, :, tsl(tb)]), slots)
        self.stream_mm(W, self.col_blocks(cfg.o_q, 3 * cfg.GH * 128), hT, hB, KD,
                       mk_evac(lambda g, M, tb: self.s_qkv[g, :, tsl(tb)]), slots)
        self.stream_mm(W, self.col_blocks(cfg.o_z, cfg.GH * 128), hT, hB, KD,
                       mk_evac(lambda g, M, tb: self.s_z[g, :, tsl(tb)]), slots)
        blocks = [([(cfg.o_b, 2 * cfg.GH)], [(0, 2 * cfg.GH, 0)])]
        self.stream_mm(W, blocks, hT, hB, KD, mk_evac(lambda g, M, tb: self.s_ba[:, tsl(tb)]), slots)
        self.stream_mm(W, self.col_blocks(cfg.o_g, 2 * cfg.D), hT, hB, KD,
                       mk_evac(lambda g, M, tb: self.s_gate[g, :, tsl(tb)], sigm=True), slots)

    def phase_mla(self, l):
        cfg, mk = self.cfg, self.mk
        T, TB, NTB, H = cfg.T, cfg.TB, cfg.NTB, cfg.H
        NLq, NLk = cfg.NLq, cfg.NLk
        NL = NLq + NLk
        self.push()
        cqn = self.sb([128, NLq, T], BF16, "cqn"); cqnB = Buf("cqn")
        ckvn = self.sb([128, NLk, T], BF16, "ckvn"); ckvnB = Buf("ckvn")
        kpeR = self.sb([64, T], BF16, "kpeR"); kpeB = Buf("kpeR")
        aT = self.sb([128, H, T], BF16, "aT"); aTB = Buf("aT")
        self.push()
        lat = Ring([(self.sb([128, NL, TB], F32, "lat"), Buf("lat")) for _ in range(2)])
        sq = Ring([(self.sb([128, NL, TB], BF16, "lsq"), Buf("lsq")) for _ in range(1)])
        rq = Ring([(self.sb([128, 2, TB], F32, "lrs"), Buf("lrs")) for _ in range(2)])
        latv = self.s_lat.rearrange("k p t -> p k t")
        for tb in range(NTB):
            sl = slice(tb * TB, (tb + 1) * TB)
            la, laB = lat.next()
            mk.dma(mk.sp, la[:], latv[:, :, sl], writes=[laB])
            s_, sB = sq.next()
            self.ACT(s_[:], la[:], AF.Square, R=[laB], W=[sB])
            r_, rB = rq.next()
            for which, (k0, nk, dim) in enumerate([(0, NLq, cfg.QL), (NLq, NLk, cfg.KVL)]):
                b = which
                for k in range(nk):
                    self.MM(self.ps[b][:, 0:TB], self.onesB, s_[:, k0 + k, :], k == 0, k == nk - 1, R=[self.cbB, sB], W=[self.pb[b]], sig=(k == nk - 1))
                self.ACT(r_[:, which, :], self.ps[b][:, 0:TB], AF.Sqrt, R=[self.pb[b], self.ccB], W=[rB], bias=self.epsc, scale=1.0 / dim)
            self.I(mk.dve, "reciprocal", r_[:], r_[:], R=[rB], W=[rB])
            for k in range(NLq):
                self.I(mk.dve, "scalar_tensor_tensor", cqn[:, k, sl], la[:, k, :], self.par[:, l, self.o_qan + k:self.o_qan + k + 1], r_[:, 0, :],
                       op0=ALU.mult, op1=ALU.mult, R=[laB, self.parB, rB], W=[cqnB])
            for k in range(NLk):
                self.I(mk.dve, "scalar_tensor_tensor", ckvn[:, k, sl], la[:, NLq + k, :], self.par[:, l, self.o_kvan + k:self.o_kvan + k + 1], r_[:, 1, :],
                       op0=ALU.mult, op1=ALU.mult, R=[laB, self.parB, rB], W=[ckvnB])
        kp = self.sb([64, 2, T], F32, "kp"); kpB = Buf("kp")
        mk.dma(mk.sp, kp[:], self.s_kpe.rearrange("g p t -> p g t"), writes=[kpB])
        self.I(mk.dve, "tensor_tensor", kp[:, 0, :], kp[:, 0, :], self.COS[:], op=ALU.mult, R=[kpB, self.ropeB], W=[kpB])
        self.I(mk.dve, "tensor_tensor", kp[:, 1, :], kp[:, 1, :], self.SINS[:], op=ALU.mult, R=[kpB, self.ropeB], W=[kpB])
        self.I(mk.dve, "tensor_tensor", kpeR[:], kp[:, 0, :], kp[:, 1, :], op=ALU.add, R=[kpB], W=[kpeB])
        self.pop()
        wq = self.sb([128, NLq, H * 192], BF16, "wq"); wqB = Buf("wq")
        wqs = self.sb([128, NLq, H, 64], BF16, "wqs"); wqsB = Buf("wqs")
        wkv = self.sb([128, NLk, H * 256], BF16, "wkv"); wkvB = Buf("wkv")
        mk.dma(mk.pool, wq[:], self.w_uq[l].rearrange("(k p) n -> p k n", p=128), writes=[wqB])
        wq4 = self.w_uq[l].rearrange("(k p) (h e) -> p k h e", p=128, e=192)
        for k in range(NLq):
            mk.dma(mk.pool, wqs[:, k, :, 0:32], wq4[:, k, :, 160:192], writes=[wqsB])
            mk.dma(mk.pool, wqs[:, k, :, 32:64], wq4[:, k, :, 128:160], writes=[wqsB])
        mk.dma(mk.pool, wkv[:], self.w_ukv[l].rearrange("(k p) n -> p k n", p=128), writes=[wkvB])
        knT = Ring([(self.sb([128, T], BF16, "knT"), Buf("knT")) for _ in range(2)])
        qnT = Ring([(self.sb([128, T], BF16, "qnT"), Buf("qnT")) for _ in range(2)])
        qpR = Ring([(self.sb([64, T], BF16, "qpR"), Buf("qpR")) for _ in range(2)])
        vtk = Ring([(self.sb([128, cfg.NT128, 128], BF16, "vtk"), Buf("vtk")) for _ in range(2)])
        rt = Ring([(self.sb([64, 2, TB], F32, "rt"), Buf("rt")) for _ in range(2)])
        Pt = Ring([(self.sb([128, TB], BF16, "Pt"), Buf("Pt")) for _ in range(3)])
        rec = Ring([(self.sb([128, TB], F32, "rec"), Buf("rec")) for _ in range(2)])
        pj = Ring([6, 7])
        sc = Ring([0, 1])
        od = Ring([(2, 3), (4, 5)])
        scale = float(192 ** -0.5)
        KT = TB // 128
        for h in range(H):
            kn, knB = knT.next(); qn, qnB = qnT.next(); qp, qpB = qpR.next(); vt, vtB = vtk.next()
            for tb in range(NTB):
                sl = slice(tb * TB, (tb + 1) * TB)
                b = pj.next()
                for k in range(NLk):
                    self.MM(self.ps[b][:, 0:TB], wkv[:, k, h * 256:h * 256 + 128], ckvn[:, k, sl], k == 0, k == NLk - 1, R=[wkvB, ckvnB], W=[self.pb[b]], sig=(k == NLk - 1))
                self.ACT(kn[:, sl], self.ps[b][:, 0:TB], AF.Copy, R=[self.pb[b]], W=[knB])
                b = pj.next()
                for k in range(NLq):
                    self.MM(self.ps[b][:, 0:TB], wq[:, k, h * 192:h * 192 + 128], cqn[:, k, sl], k == 0, k == NLq - 1, R=[wqB, cqnB], W=[self.pb[b]], sig=(k == NLq - 1))
                self.I(mk.dve, "tensor_copy", qn[:, sl], self.ps[b][:, 0:TB], R=[self.pb[b]], W=[qnB])
                b1 = pj.next()
                for k in range(NLq):
                    self.MM(self.ps[b1][0:64, 0:TB], wq[:, k, h * 192 + 128:h * 192 + 192], cqn[:, k, sl], k == 0, k == NLq - 1, R=[wqB, cqnB], W=[self.pb[b1]], sig=(k == NLq - 1))
                b2 = pj.next()
                for k in range(NLq):
                    self.MM(self.ps[b2][0:64, 0:TB], wqs[:, k, h, :], cqn[:, k, sl], k == 0, k == NLq - 1, R=[wqsB, cqnB], W=[self.pb[b2]], sig=(k == NLq - 1))
                r_, rB = rt.next()
                self.I(mk.dve, "tensor_tensor", r_[:, 0, :], self.ps[b1][0:64, 0:TB], self.COS[:, sl], op=ALU.mult, R=[self.pb[b1], self.ropeB], W=[rB])
                self.I(mk.dve, "tensor_tensor", r_[:, 1, :], self.ps[b2][0:64, 0:TB], self.SINS[:, sl], op=ALU.mult, R=[self.pb[b2], self.ropeB], W=[rB])
                self.I(mk.dve, "tensor_tensor", qp[:, sl], r_[:, 0, :], r_[:, 1, :], op=ALU.add, R=[rB], W=[qpB])
            for i0 in range(0, cfg.NT128, 4):
                n4 = min(4, cfg.NT128 - i0)
                b = pj.next()
                for ii in range(n4):
                    i = i0 + ii
                    for k in range(NLk):
                        self.MM(self.ps[b][:, ii * 128:(ii + 1) * 128], ckvn[:, k, i * 128:(i + 1) * 128], wkv[:, k, h * 256 + 128:h * 256 + 256],
                                k == 0, k == NLk - 1, R=[wkvB, ckvnB], W=[self.pb[b]], sig=(ii == n4 - 1 and k == NLk - 1))
                self.ACT(vt[:, i0:i0 + n4, :], self.ps[b][:, 0:n4 * 128].rearrange("p (i d) -> p i d", i=n4), AF.Copy, R=[self.pb[b]], W=[vtB])
            for qb in range(NTB):
                bo, bd = od.next()
                nkt = (qb + 1) * KT
                pend = None

                def emit_pv(item):
                    kt, q0, P_, PB = item
                    self.MM(self.ps[bo][:, q0:TB], vt[:, kt, :], P_[:, q0:TB], kt == 0, kt == nkt - 1, R=[vtB, PB], W=[self.pb[bo]], sig=False)
                    self.MM(self.ps[bd][:, q0:TB], self.onesB, P_[:, q0:TB], kt == 0, kt == nkt - 1, R=[self.cbB, PB], W=[self.pb[bd]], sig=True)
                for kt in range(nkt):
                    j = kt - qb * KT
                    q0 = max(0, j) * 128
                    bs = sc.next()
                    qsl = slice(qb * TB + q0, (qb + 1) * TB)
                    ksl = slice(kt * 128, (kt + 1) * 128)
                    self.MM(self.ps[bs][:, q0:TB], kn[:, ksl], qn[:, qsl], True, False, R=[knB, qnB], W=[self.pb[bs]], sig=False)
                    self.MM(self.ps[bs][:, q0:TB], kpeR[:, ksl], qp[:, qsl], False, True, R=[kpeB, qpB], W=[self.pb[bs]], sig=True)
                    P_, PB = Pt.next()
                    self.ACT(P_[:, q0:TB], self.ps[bs][:, q0:TB], AF.Exp, R=[self.pb[bs]], W=[PB], scale=scale)
                    if j >= 0:
                        self.I(mk.dve, "tensor_tensor", P_[:, q0:q0 + 128], P_[:, q0:q0 + 128], self.causB, op=ALU.mult, R=[PB, self.cbB], W=[PB])
                    if pend is not None:
                        emit_pv(pend)
                    pend = (kt, q0, P_, PB)
                emit_pv(pend)
                rc, rcB = rec.next()
                self.I(mk.dve, "reciprocal", rc[:], self.ps[bd][:, 0:TB], R=[self.pb[bd]], W=[rcB])
                self.I(mk.dve, "tensor_tensor", aT[:, h, qb * TB:(qb + 1) * TB], self.ps[bo][:, 0:TB], rc[:], op=ALU.mult, R=[self.pb[bo], rcB], W=[aTB])
        self._pset = 0
        slots = Ring([(self.sb([128, H, 512], BF16, "womla"), Buf("womla")) for _ in range(2)])
        gt = Ring([(self.sb([128, TB], BF16, "gat"), Buf("gat")) for _ in range(4)])
        mo = Ring([(self.sb([128, TB], BF16, "mo"), Buf("mo")) for _ in range(4)])

        def evac(g, M, tb, psap, psB):
            sl = slice(tb * TB, (tb + 1) * TB)
            g_, gB = gt.next()
            mk.dma(mk.sp, g_[:], self.s_gate[g, :, sl], writes=[gB])
            m_, mB = mo.next()
            self.I(mk.dve, "tensor_tensor", m_[:], psap, g_[:], op=ALU.mult, R=[psB, gB], W=[mB])
            mk.dma(mk.sp, self.s_ma[g, :, sl], m_[:], reads=[mB])
        self.stream_mm(self.w_o_mla[l], self.col_blocks(0, cfg.D), aT, aTB, H, evac, slots)
        self.pop()

    def phase_gdn(self, l):
        cfg, mk = self.cfg, self.mk
        T, TB, NTB, GH, HG, NCH, KD = cfg.T, cfg.TB, cfg.NTB, cfg.GH, cfg.HG, cfg.NCH, cfg.KD
        V = mk.dve
        self.push()
        ogT = self.sb([128, GH, T], BF16, "ogT"); ogB = Buf("ogT")
        self.push()
        gsh = [64, NCH, GH]
        beta = self.sb(gsh, F32, "beta"); g_ = self.sb(gsh, F32, "g"); expG = self.sb(gsh, F32, "expG")
        expGLmG = self.sb(gsh, F32, "expGLmG"); c1 = self.sb(gsh, F32, "c1"); nbeta = self.sb(gsh, F32, "nbeta")
        expGL = self.sb([128, NCH, GH], F32, "expGL")
        gB = Buf("gates")
        self.push()
        ba = self.sb([2 * GH, T], F32, "ba"); baB = Buf("ba")
        batok = self.sb([64, NCH, 2 * GH], F32, "batok"); btB = Buf("batok")
        mk.dma(mk.sp, ba[:], self.s_ba, writes=[baB])
        b = 0
        for n in range(NCH):
            self.TR(self.ps[b][0:64, n * 2 * GH:(n + 1) * 2 * GH], ba[:, n * 64:(n + 1) * 64], self.identF[0:2 * GH, 0:2 * GH],
                    R=[baB, self.cstB], W=[self.pb[b]], sig=(n == NCH - 1))
        self.I(V, "tensor_copy", batok[:], self.ps[b][0:64, 0:NCH * 2 * GH].rearrange("p (n g) -> p n g", n=NCH), R=[self.pb[b]], W=[btB])
        self.ACT(beta[:], batok[:, :, 0:GH], AF.Sigmoid, R=[btB], W=[gB])
        self.I(V, "tensor_tensor", g_[:], batok[:, :, GH:2 * GH], self.grow[:, l, 1, :].unsqueeze(1).to_broadcast(gsh), op=ALU.add, R=[btB, self.growB], W=[gB])
        self.ACT(g_[:], g_[:], AF.Exp, R=[gB], W=[gB])
        self.ACT(g_[:], g_[:], AF.Ln, R=[gB, self.ccB], W=[gB], bias=self.onec[0:64, :])
        self.I(V, "tensor_tensor", g_[:], g_[:], self.grow[:, l, 0, :].unsqueeze(1).to_broadcast(gsh), op=ALU.mult, R=[gB, self.growB], W=[gB])
        for (lhs, M, dst, bb) in [(self.U64, 64, expG, 1), (self.SL64, 64, expGLmG, 2), (self.onesF[0:64, :], 128, expGL, 3)]:
            for n in range(NCH):
                self.MM(self.ps[bb][0:M, n * GH:(n + 1) * GH], lhs, g_[:, n, :], True, True, R=[self.cstB, gB], W=[self.pb[bb]], sig=(n == NCH - 1))
            self.ACT(dst[:], self.ps[bb][0:M, 0:NCH * GH].rearrange("p (n g) -> p n g", n=NCH), AF.Exp, R=[self.pb[bb]], W=[gB])
        self.I(V, "tensor_tensor", c1[:], beta[:], expG[:], op=ALU.mult, R=[gB], W=[gB])
        self.I(V, "tensor_scalar", nbeta[:], beta[:], -1.0, None, op0=ALU.mult, R=[gB], W=[gB])
        self.pop()
        gstop = getattr(self, "gstop", 9)
        if gstop == 0:
            self.pop(); self.pop()
            return
        for gp in range(GH // HG):
            hs = slice(gp * HG, (gp + 1) * HG)
            self.push()
            qT = self.sb([128, HG, T], BF16, "gqT"); kT = self.sb([128, HG, T], BF16, "gkT"); vT = self.sb([128, HG, T], BF16, "gvT")
            qkvB = Buf("gqkv")
            self.push()
            ub = Ring([(self.sb([128, T], F32, "u"), Buf("u")) for _ in range(2)])
            ab = Ring([(self.sb([128, T], F32, "acc"), Buf("acc")) for _ in range(2)])
            sqb = Ring([(self.sb([128, T], BF16, "csq"), Buf("csq")) for _ in range(1)])
            rsb = Ring([(self.sb([128, TB], F32, "crs"), Buf("crs")) for _ in range(2)])
            banks = Ring([4, 5, 6, 7])
            for kind, dstT in ((0, qT), (1, kT), (2, vT)):
                for hh in range(HG):
                    ci = kind * GH + gp * HG + hh
                    u, uB = ub.next()
                    mk.dma(mk.sp, u[:], self.s_qkv[ci], writes=[uB])
                    a, aB = ab.next()
                    w = lambda j: self.cw[:, l, ci, j:j + 1]
                    self.I(V, "tensor_scalar", a[:], u[:], w(3), None, op0=ALU.mult, R=[uB, self.cwB], W=[aB])
                    for d in (1, 2, 3):
                        self.I(V, "scalar_tensor_tensor", a[:, d:T], u[:, 0:T - d], w(3 - d), a[:, d:T], op0=ALU.mult, op1=ALU.add,
                               R=[uB, self.cwB, aB], W=[aB])
                    if kind == 2:
                        self.ACT(dstT[:, hh, :], a[:], AF.Silu, R=[aB], W=[qkvB])
                        continue
                    self.ACT(a[:], a[:], AF.Silu, R=[aB], W=[aB])
                    s_, sB = sqb.next()
                    self.ACT(s_[:], a[:], AF.Square, R=[aB], W=[sB])
                    for tb in range(NTB):
                        sl = slice(tb * TB, (tb + 1) * TB)
                        b = banks.next()
                        self.MM(self.ps[b][:, 0:TB], self.onesB, s_[:, sl], True, True, R=[self.cbB, sB], W=[self.pb[b]], sig=True)
                        r_, rB = rsb.next()
                        self.ACT(r_[:], self.ps[b][:, 0:TB], AF.Sqrt, R=[self.pb[b], self.ccB], W=[rB], bias=self.epsc)
                        self.I(V, "reciprocal", r_[:], r_[:], R=[rB], W=[rB])
                        self.I(V, "scalar_tensor_tensor", dstT[:, hh, sl], a[:, sl], float(128 ** -0.5) if kind == 0 else 1.0, r_[:],
                               op0=ALU.mult, op1=ALU.mult, R=[aB, rB], W=[qkvB])
            self.pop()
            if gstop == 1:
                self.pop()
                continue
            self.push()
            W4 = HG * 64
            W8 = HG * 128
            S = self.sb([128, HG, 128], F32, "S"); SB_ = Buf("S")
            self.I(V, "memset", S[:], 0.0, W=[SB_])
            f4 = lambda nm, n: Ring([(self.sb([64, HG, 64], F32, nm), Buf(nm)) for _ in range(n)])
            isets = []
            for par_ in range(2):
                isets.append(dict(gU=f4("gU", 1), E=f4("E", 1), Es=f4("Es", 1), at=f4("att", 1), A=f4("A", 3), B=f4("Bm", 3), P=f4("P", 2),
                                  ainv=f4("ainv", 1), attT=f4("attT", 1)))
            ibanks = Ring([6, 7])
            f8 = lambda nm, n, dt=F32: Ring([(self.sb([64, HG, 128], dt, nm), Buf(nm)) for _ in range(n)])
            kd_r, vb_r, t_r, r_r, vn_r, o_r, on_r = f8("kdec", 2), f8("vb", 2), f8("t8", 2), f8("r8", 2), f8("vnew", 2), f8("o8", 2), f8("on", 2)
            ss_r = Ring([(self.sb([64, HG], F32, "ss"), Buf("ss")) for _ in range(2)])
            tS = self.sb([128, HG, 128], F32, "tS"); tSB = Buf("tS")
            fb = Ring([0, 1, 2, 3, 4, 5])
            I64 = self.identF[0:64, 0:64]
            bc4 = lambda ap2: ap2.unsqueeze(1).to_broadcast([64, HG, 64])
            results = {}

            cf_r = Ring([(self.sb([128, 3, HG, 64], F32, "cf"), Buf("cf")) for _ in range(3)])
            cf_cache = {}

            def chunk_f32(n):
                if n not in cf_cache:
                    c_, cB_ = cf_r.next()
                    csl_ = slice(n * 64, (n + 1) * 64)
                    self.ACT(c_[:, 0], qT[:, :, csl_], AF.Copy, R=[qkvB], W=[cB_])
                    self.I(V, "tensor_copy", c_[:, 1], kT[:, :, csl_], R=[qkvB], W=[cB_])
                    self.ACT(c_[:, 2], vT[:, :, csl_], AF.Copy, R=[qkvB], W=[cB_]) if True else self.I(V, "tensor_copy", c_[:, 2], vT[:, :, csl_], R=[qkvB], W=[cB_])
                    cf_cache[n] = (c_[:, 1], c_[:, 0], c_[:, 2], cB_)
                    if getattr(self, "debug", False) and n == 0 and l == 0 and gp == 0:
                        mk.dma(mk.sp, self.dram("dbg_cf", [128, 3, HG, 64], F32, "Internal"), c_[:], reads=[cB_])
                    cf_cache.pop(n - 3, None)
                return cf_cache[n]

            def half_(st_):
                b_ = ibanks.next()
                return self.ps[b_][0:64, 0:W4].rearrange("p (h c) -> p h c", h=HG), self.pb[b_]

            def inverse(n):
                csl = slice(n * 64, (n + 1) * 64)
                st_ = isets[n % 2]
                gU_r, E_r, Es_r, at_r, A_r, B_r, P_r, ainv, attT = (st_[k_] for k_ in ("gU", "E", "Es", "at", "A", "B", "P", "ainv", "attT"))
                half = lambda: half_(st_)
                gU, gUB = gU_r.next()
                self.I(V, "tensor_tensor", gU[:], bc4(self.U64), g_[:, n, hs].unsqueeze(2).to_broadcast([64, HG, 64]), op=ALU.mult, R=[self.cstB, gB], W=[gUB])
                pD, pDB = half()
                for hh in range(HG):
                    self.MM(pD[:, hh, :], gU[:, hh, :], self.SL64, True, True, R=[gUB, self.cstB], W=[pDB], sig=(hh == HG - 1))
                E, EB = E_r.next()
                self.ACT(E[:], pD, AF.Exp, R=[pDB], W=[EB])
                yield
                self.I(V, "tensor_tensor", E[:], E[:], bc4(self.LINC), op=ALU.mult, R=[EB, self.cstB], W=[EB])
                Es, EsB = Es_r.next()
                self.I(V, "tensor_tensor", Es[:], E[:], bc4(self.SL64), op=ALU.mult, R=[EB, self.cstB], W=[EsB])
                pK, pKB = half()
                pQ, pQB = half()
                kf, qf, vf, cfB = chunk_f32(n)
                for hh in range(HG):
                    self.MM(pK[:, hh, :], kf[:, hh, :], kf[:, hh, :], True, True, R=[cfB], W=[pKB], sig=(hh == HG - 1))
                for hh in range(HG):
                    self.MM(pQ[:, hh, :], qf[:, hh, :], kf[:, hh, :], True, True, R=[cfB], W=[pQB], sig=(hh == HG - 1))
                yield
                A, AB = A_r.next()
                self.I(V, "tensor_tensor", A[:], pK, Es[:], op=ALU.mult, R=[pKB, EsB], W=[AB])
                self.I(V, "tensor_tensor", A[:], A[:], nbeta[:, n, hs].unsqueeze(2).to_broadcast([64, HG, 64]), op=ALU.mult, R=[AB, gB], W=[AB])
                at, atB = at_r.next()
                self.I(V, "tensor_tensor", at[:], pQ, E[:], op=ALU.mult, R=[pQB, EB], W=[atB])
                if getattr(self, "debug", False) and n == 0 and l == 0 and gp == 0:
                    dk_ = self.sb([64, 2, HG, 64], F32, "dbgk"); dkB = Buf("dbgk")
                    self.I(V, "tensor_copy", dk_[:, 0], pK, R=[pKB], W=[dkB])
                    self.I(V, "tensor_copy", dk_[:, 1], pQ, R=[pQB], W=[dkB])
                    mk.dma(mk.sp, self.dram("dbg_KQ", [64, 2, HG, 64], F32, "Internal"), dk_[:], reads=[dkB])
                    mk.dma(mk.sp, self.dram("dbg_Es", [64, HG, 64], F32, "Internal"), Es[:], reads=[EsB])
                    mk.dma(mk.sp, self.dram("dbg_E", [64, HG, 64], F32, "Internal"), E[:], reads=[EB])
                    mk.dma(mk.sp, self.dram("dbg_N", [64, HG, 64], F32, "Internal"), A[:], reads=[AB])
                    mk.dma(mk.sp, self.dram("dbg_at", [64, HG, 64], F32, "Internal"), at[:], reads=[atB])
                pT1, pT1B = half()
                pT2, pT2B = half()
                for hh in range(HG):
                    self.TR(pT1[:, hh, :], A[:, hh, :], I64, R=[AB, self.cstB], W=[pT1B], sig=(hh == HG - 1))
                for hh in range(HG):
                    self.TR(pT2[:, hh, :], at[:, hh, :], I64, R=[atB, self.cstB], W=[pT2B], sig=(hh == HG - 1))
                yield
                Bm, BB = B_r.next()
                self.ACT(Bm[:], pT1, AF.Copy, R=[pT1B], W=[BB])
                aT_, aTB_ = attT.next()
                self.ACT(aT_[:], pT2, AF.Copy, R=[pT2B], W=[aTB_])
                P, PB = P_r.next()
                self.I(V, "tensor_tensor", P[:], Bm[:], bc4(I64), op=ALU.add, R=[BB, self.cstB], W=[PB])
                for lev in range(1, 6):
                    pB, pBB = half()
                    pA, pAB = half()
                    for hh in range(HG):
                        self.MM(pB[:, hh, :], A[:, hh, :], Bm[:, hh, :], True, True, R=[AB, BB], W=[pBB], sig=(hh == HG - 1))
                    for hh in range(HG):
                        self.MM(pA[:, hh, :], Bm[:, hh, :], A[:, hh, :], True, True, R=[AB, BB], W=[pAB], sig=(hh == HG - 1))
                    yield
                    A2, A2B = A_r.next()
                    B2, B2B = B_r.next()
                    self.ACT(A2[:], pA, AF.Copy, R=[pAB], W=[A2B])
                    if lev < 5:
                        self.I(V, "tensor_copy", B2[:], pB, R=[pBB], W=[B2B])
                    A, AB, Bm, BB = A2, A2B, B2, B2B
                    pP, pPB = half()
                    for hh in range(HG):
                        self.MM(pP[:, hh, :], A[:, hh, :], P[:, hh, :], True, True, R=[AB, PB], W=[pPB], sig=(hh == HG - 1))
                    yield
                    if lev < 5:
                        P2, P2B = P_r.next()
                    else:
                        P2, P2B = ainv.next()
                    self.I(V, "tensor_tensor", P2[:], pP, P[:], op=ALU.add, R=[pPB, PB], W=[P2B])
                    P, PB = P2, P2B
                results[n] = (P, PB, aT_, aTB_)
                if getattr(self, "debug", False) and n == 0 and l == 0 and gp == 0:
                    mk.dma(mk.sp, self.dram("dbg_ainv", [64, HG, 64], F32, "Internal"), P[:], reads=[PB])
                    mk.dma(mk.sp, self.dram("dbg_attT", [64, HG, 64], F32, "Internal"), aT_[:], reads=[aTB_])
                yield

            def full():
                b = fb.next()
                return self.ps[b], self.pb[b]

            def scan(n):
                csl = slice(n * 64, (n + 1) * 64)
                bc8 = lambda t3: t3[:, n, hs].unsqueeze(2).to_broadcast([64, HG, 128])
                kf, qf, vf, cfB = chunk_f32(n)
                v3 = lambda ap: ap[0:64, 0:W8].rearrange("p (h d) -> p h d", h=HG)
                pXk, pXkB = full()
                pXv, pXvB = full()
                for hh in range(HG):
                    self.TR(pXk[0:64, hh * 128:(hh + 1) * 128], kf[:, hh, :], self.identF, R=[cfB, self.cstB], W=[pXkB], sig=(hh == HG - 1))
                for hh in range(HG):
                    self.TR(pXv[0:64, hh * 128:(hh + 1) * 128], vf[:, hh, :], self.identF, R=[cfB, self.cstB], W=[pXvB], sig=(hh == HG - 1))
                kd, kdB = kd_r.next()
                vb, vbB = vb_r.next()
                self.I(V, "tensor_tensor", kd[:], v3(pXk), bc8(expGLmG), op=ALU.mult, R=[pXkB, gB], W=[kdB])
                self.I(V, "tensor_tensor", vb[:], v3(pXv), bc8(beta), op=ALU.mult, R=[pXvB, gB], W=[vbB])
                pKS, pKSB = full()
                pQS, pQSB = full()
                for hh in range(HG):
                    self.MM(pKS[0:64, hh * 128:(hh + 1) * 128], kf[:, hh, :], S[:, hh, :], True, True, R=[cfB, SB_], W=[pKSB], sig=(hh == HG - 1))
                for hh in range(HG):
                    self.MM(pQS[0:64, hh * 128:(hh + 1) * 128], qf[:, hh, :], S[:, hh, :], True, True, R=[cfB, SB_], W=[pQSB], sig=(hh == HG - 1))
                yield
                t8, t8B = t_r.next()
                r8, r8B = r_r.next()
                self.I(V, "tensor_tensor", t8[:], v3(pKS), bc8(c1), op=ALU.mult, R=[pKSB, gB], W=[t8B])
                self.I(V, "tensor_tensor", r8[:], vb[:], t8[:], op=ALU.subtract, R=[vbB, t8B], W=[r8B])
                while n not in results:
                    yield
                AinvT, AiB, aT_, aTB_ = results.pop(n)
                pVN, pVNB = full()
                for hh in range(HG):
                    self.MM(pVN[0:64, hh * 128:(hh + 1) * 128], AinvT[:, hh, :], r8[:, hh, :], True, True, R=[AiB, r8B], W=[pVNB], sig=(hh == HG - 1))
                yield
                vn, vnB = vn_r.next()
                self.ACT(vn[:], v3(pVN), AF.Copy, R=[pVNB], W=[vnB])
                pS, pSB = full()
                for hh in range(HG):
                    self.MM(pS[:, hh * 128:(hh + 1) * 128], kd[:, hh, :], vn[:, hh, :], True, True, R=[kdB, vnB], W=[pSB], sig=(hh == HG - 1))
                pAV, pAVB = full()
                for hh in range(HG):
                    self.MM(pAV[0:64, hh * 128:(hh + 1) * 128], aT_[:, hh, :], vn[:, hh, :], True, True, R=[aTB_, vnB], W=[pAVB], sig=(hh == HG - 1))
                yield
                self.I(V, "tensor_tensor", tS[:], S[:], expGL[:, n, hs].unsqueeze(2).to_broadcast([128, HG, 128]), op=ALU.mult, R=[SB_, gB], W=[tSB])
                self.I(V, "tensor_tensor", S[:], tS[:], pS[:, 0:W8].rearrange("p (h d) -> p h d", h=HG), op=ALU.add, R=[tSB, pSB], W=[SB_])
                o8, o8B = o_r.next()
                t8b, t8bB = t_r.next()
                self.I(V, "tensor_tensor", t8b[:], v3(pQS), bc8(expG), op=ALU.mult, R=[pQSB, gB], W=[t8bB])
                self.I(V, "tensor_tensor", o8[:], t8b[:], v3(pAV), op=ALU.add, R=[t8bB, pAVB], W=[o8B])
                yield
                self.I(V, "tensor_tensor", t8b[:], o8[:], o8[:], op=ALU.mult, R=[o8B], W=[t8bB])
                ss, ssB = ss_r.next()
                self.I(V, "tensor_reduce", ss[:], t8b[:], axis=AX.X, op=ALU.add, R=[t8bB], W=[ssB])
                self.ACT(ss[:], ss[:], AF.Sqrt, R=[ssB, self.ccB], W=[ssB], bias=self.epsc[0:64, :], scale=1.0 / 128)
                self.I(V, "reciprocal", ss[:], ss[:], R=[ssB], W=[ssB])
                on, onB = on_r.next()
                self.I(V, "tensor_tensor", on[:], o8[:], ss[:].unsqueeze(2).to_broadcast([64, HG, 128]), op=ALU.mult, R=[o8B, ssB], W=[onB])
                pO, pOB = full()
                for hh in range(HG):
                    self.TR(pO[:, hh * 64:(hh + 1) * 64], on[:, hh, :], I64, R=[onB, self.cstB], W=[pOB], sig=(hh == HG - 1))
                yield
                self.I(V, "tensor_scalar", ogT[:, hs, csl], pO[:, 0:W4].rearrange("p (h c) -> p h c", h=HG), self.par[:, l, self.o_gn:self.o_gn + 1], None,
                       op0=ALU.mult, R=[pOB, self.parB], W=[ogB])
                yield

            inv_gen = None
            inv_next = 0
            g2mode = getattr(self, "g2mode", "")
            if g2mode.startswith("inv"):
                lim_ = int(g2mode[3:] or 99)
                for n in range(NCH):
                    for i_, _ in enumerate(inverse(n)):
                        if i_ + 1 >= lim_:
                            break
            for n in range(NCH if not g2mode.startswith("inv") else 0):
                sg_ = scan(n)
                done = False
                while not done:
                    if inv_gen is None and inv_next < NCH and inv_next <= n + 1:
                        inv_gen = inverse(inv_next)
                        inv_next += 1
                    if inv_gen is not None:
                        try:
                            next(inv_gen)
                        except StopIteration:
                            inv_gen = None
                    try:
                        next(sg_)
                    except StopIteration:
                        done = True
            assert inv_gen is None or all(False for _ in inv_gen)
            self.pop()
            self.pop()
        self.pop()
        if gstop <= 2:
            self.pop()
            return
        zb = Ring([(self.sb([128, TB], F32, "zb"), Buf("zb")) for _ in range(3)])
        for h in range(GH):
            for tb in range(NTB):
                sl = slice(tb * TB, (tb + 1) * TB)
                z_, zB = zb.next()
                mk.dma(mk.sp, z_[:], self.s_z[h, :, sl], writes=[zB])
                self.ACT(z_[:], z_[:], AF.Silu, R=[zB], W=[zB])
                self.I(V, "tensor_tensor", ogT[:, h, sl], ogT[:, h, sl], z_[:], op=ALU.mult, R=[ogB, zB], W=[ogB])
        if getattr(self, "debug", False):
            dbg = self.dram(f"dbg_og{l}", [GH, 128, T], BF16, "Internal")
            mk.dma(mk.sp, dbg.rearrange("h p t -> p h t"), ogT[:], reads=[ogB])
        mT = self.sb([128, KD, T], BF16, "mT"); mTB = Buf("mT")
        self._pset = 0
        slots = Ring([(self.sb([128, max(GH, KD), 512], BF16, "wog"), Buf("wog")) for _ in range(2)])
        gt = Ring([(self.sb([128, TB], BF16, "gbt"), Buf("gbt")) for _ in range(4)])
        mat = Ring([(self.sb([128, TB], BF16, "mat"), Buf("mat")) for _ in range(4)])
        tm = Ring([(self.sb([128, TB], F32, "tm"), Buf("tm")) for _ in range(4)])

        def evac_b(g, M, tb, psap, psB):
            sl = slice(tb * TB, (tb + 1) * TB)
            g2, g2B = gt.next()
            mk.dma(mk.sp, g2[:], self.s_gate[KD + g, :, sl], writes=[g2B])
            ma, maB = mat.next()
            mk.dma(mk.sp, ma[:], self.s_ma[g, :, sl], writes=[maB])
            t_, tB = tm.next()
            self.I(V, "tensor_tensor", t_[:], psap, g2[:], op=ALU.mult, R=[psB, g2B], W=[tB])
            self.I(V, "tensor_tensor", mT[:, g, sl], t_[:], ma[:], op=ALU.add, R=[tB, maB], W=[mTB])
        self.stream_mm(self.w_o_gdn[l], self.col_blocks(0, cfg.D), ogT, ogB, GH, evac_b, slots)
        xt = Ring([(self.sb([128, TB], F32, "xt"), Buf("xt")) for _ in range(4)])
        gtc = 2 * KD

        def evac_o(g, M, tb, psap, psB):
            sl = slice(tb * TB, (tb + 1) * TB)
            x_, xB = xt.next()
            key = (g, tb)
            xb_ = self.xTB.setdefault(key, Buf(f"xT{key}"))
            mk.dma(mk.sp, x_[:], self.xT[g, :, sl], reads=[xb_], writes=[xB])
            self.I(V, "scalar_tensor_tensor", x_[:], psap, self.mod[:, l, gtc + g:gtc + g + 1], x_[:], op0=ALU.mult, op1=ALU.add,
                   R=[psB, self.modB, xB], W=[xB])
            mk.dma(mk.sp, self.xT[g, :, sl], x_[:], reads=[xB], writes=[xb_])
        self.stream_mm(self.w_o[l], self.col_blocks(0, cfg.D), mT, mTB, KD, evac_o, slots)
        self.pop()

    def phase_ffn(self, l, hT, hB):
        cfg, mk = self.cfg, self.mk
        T, TB, NTB, KD, FQ, DFF = cfg.T, cfg.TB, cfg.NTB, cfg.KD, cfg.FQ, cfg.DFF
        V = mk.dve
        act = self.sb([128, FQ, T], BF16, "ffact"); actB = Buf("ffact")
        slots = Ring([(self.sb([128, max(KD, FQ), 512], BF16, "wff"), Buf("wff")) for _ in range(3)])
        sg = {}
        sgr = Ring([(self.sb([128, TB], F32, "sg"), Buf("sg")) for _ in range(2 * NTB + 2)])
        xt = Ring([(self.sb([128, TB], F32, "fxt"), Buf("fxt")) for _ in range(4)])
        gtc = 5 * KD
        self._pset = 0
        for q in range(cfg.NFP):
            blocks = []
            for j0 in range(0, FQ, 2):
                segs, groups = [], []
                off = 0
                for j in range(j0, min(FQ, j0 + 2)):
                    c = (q * FQ + j) * 128
                    segs += [(c, 128), (DFF + c, 128)]
                    groups += [(off, 128, ("g", j)), (off + 128, 128, ("u", j))]
                    off += 256
                blocks.append((segs, groups))

            def evac_gu(gid, M, tb, psap, psB):
                kind, j = gid
                sl = slice(tb * TB, (tb + 1) * TB)
                if kind == "g":
                    s_, sB = sgr.next()
                    self.ACT(s_[:], psap, AF.Silu, R=[psB], W=[sB])
                    sg[(j, tb)] = (s_, sB)
                else:
                    s_, sB = sg.pop((j, tb))
                    self.I(V, "tensor_tensor", act[:, j, sl], psap, s_[:], op=ALU.mult, R=[psB, sB], W=[actB])
            self.stream_mm(self.w_gate_up[l], blocks, hT, hB, KD, evac_gu, slots)

            def evac_d(g, M, tb, psap, psB):
                sl = slice(tb * TB, (tb + 1) * TB)
                x_, xB = xt.next()
                key = (g, tb)
                xb_ = self.xTB.setdefault(key, Buf(f"xT{key}"))
                mk.dma(mk.sp, x_[:], self.xT[g, :, sl], reads=[xb_], writes=[xB])
                self.I(V, "scalar_tensor_tensor", x_[:], psap, self.mod[:, l, gtc + g:gtc + g + 1], x_[:], op0=ALU.mult, op1=ALU.add,
                       R=[psB, self.modB, xB], W=[xB])
                mk.dma(mk.sp, self.xT[g, :, sl], x_[:], reads=[xB], writes=[xb_])
            self.stream_mm(self.w_down[l][q * FQ * 128:(q + 1) * FQ * 128, :], self.col_blocks(0, cfg.D), act, actB, FQ, evac_d, slots)

    def phase_final(self):
        cfg, mk = self.cfg, self.mk
        KD, T = cfg.KD, cfg.T
        V = mk.dve
        NBF = 128
        self.push()
        xb = Ring([(self.sb([128, KD, NBF], F32, "fx"), Buf("fx")) for _ in range(2)])
        sq = Ring([(self.sb([128, KD, NBF], BF16, "fsq"), Buf("fsq")) for _ in range(2)])
        rs = Ring([(self.sb([128, NBF], F32, "frs"), Buf("frs")) for _ in range(2)])
        ob = Ring([(self.sb([128, cfg.D], F32, "fo"), Buf("fo")) for _ in range(2)])
        banks = Ring([1, 2, 3, 4, 5, 6, 7])
        xTv = self.xT.rearrange("k p t -> p k t")
        gsz = min(4, KD)
        for blk in range(T // NBF):
            sl = slice(blk * NBF, (blk + 1) * NBF)
            x_, xB = xb.next()
            mk.dma(mk.sp, x_[:], xTv[:, :, sl], writes=[xB])
            s_, sB = sq.next()
            self.ACT(s_[:], x_[:], AF.Square, R=[xB], W=[sB])
            for k in range(KD):
                self.MM(self.ps[0][:, 0:NBF], self.onesB, s_[:, k, :], k == 0, k == KD - 1, R=[self.cbB, sB], W=[self.pb[0]], sig=(k == KD - 1))
            r_, rB = rs.next()
            self.ACT(r_[:], self.ps[0][:, 0:NBF], AF.Sqrt, R=[self.pb[0], self.ccB], W=[rB], bias=self.epsc, scale=1.0 / cfg.D)
            self.I(V, "reciprocal", r_[:], r_[:], R=[rB], W=[rB])
            for k in range(KD):
                self.I(V, "scalar_tensor_tensor", x_[:, k, :], x_[:, k, :], self.fn[:, k:k + 1], r_[:], op0=ALU.mult, op1=ALU.mult,
                       R=[xB, self.fnB, rB], W=[xB])
            o_, oB = ob.next()
            for k0 in range(0, KD, gsz):
                b = banks.next()
                for j in range(gsz):
                    self.TR(self.ps[b][:, j * 128:(j + 1) * 128], x_[:, k0 + j, :], self.identF, R=[xB, self.cstB], W=[self.pb[b]], sig=(j == gsz - 1))
                self.ACT(o_[:, k0 * 128:(k0 + gsz) * 128], self.ps[b][:, 0:gsz * 128], AF.Copy, R=[self.pb[b]], W=[oB])
            mk.dma(mk.sp, self.out_d[sl, :], o_[:], reads=[oB], writes=[self.outB])
        self.pop()


_PROGRAM_CACHE = {}


def make_in_map(cfg, inp, b):
    L, KD = cfg.L, cfg.KD
    f = lambda a: np.ascontiguousarray(np.asarray(a))
    m = {
        "x": f(inp["x"][b]),
        "c": f(inp["c"][b]).reshape(KD, 128),
        "pos": f(inp["positions"][b]).reshape(1, cfg.T).astype(np.int32),
        "consts": make_consts(),
        "w_ada": f(inp["w_ada"]),
        "b_ada": f(inp["b_ada"]).reshape(L, 6 * KD, 128),
        "norm_mix": f(inp["norm_mix"]).reshape(L, KD, 128),
        "norm_ffn": f(inp["norm_ffn"]).reshape(L, KD, 128),
        "w_in": f(inp["w_in"]),
        "q_a_norm": f(inp["q_a_norm"]).reshape(L, cfg.NLq, 128),
        "kv_a_norm": f(inp["kv_a_norm"]).reshape(L, cfg.NLk, 128),
        "w_uq": f(inp["w_uq"]),
        "w_ukv": f(inp["w_ukv"]),
        "w_o_mla": f(inp["w_o_mla"]),
        "conv_w": f(inp["conv_w"]),
        "A_log": f(inp["A_log"]).reshape(L, 1, cfg.GH),
        "dt_bias": f(inp["dt_bias"]).reshape(L, 1, cfg.GH),
        "gdn_norm": f(inp["gdn_norm"]).reshape(L, 1, 128),
        "w_o_gdn": f(inp["w_o_gdn"]),
        "w_o": f(inp["w_o"]),
        "w_gate_up": f(inp["w_gate_up"]),
        "w_down": f(inp["w_down"]),
        "final_norm": f(inp["final_norm"]).reshape(KD, 128),
    }
    return m


def kernel(**inputs):
    cfg = Cfg()
    B = inputs["x"].shape[0]
    nc = Builder(cfg).build()
    in_maps = [make_in_map(cfg, inputs, b) for b in range(B)]
    res = run_bass_kernel_spmd(nc, in_maps, core_ids=list(range(B)))
    out = np.stack([np.asarray(r["out"]) for r in res.results], axis=0)
    return out.astype(np.float32)
```

```python
import numpy as np
from contextlib import ExitStack
import concourse.bass as bass
import concourse.mybir as mybir
from concourse.bass_utils import run_bass_kernel_spmd

F32 = mybir.dt.float32
BF16 = mybir.dt.bfloat16
I32 = mybir.dt.int32
AF = mybir.ActivationFunctionType
ALU = mybir.AluOpType
AX = mybir.AxisListType


class Cfg:
    def __init__(self, D=2048, T=2048, L=2, H=8, QL=512, KVL=512, GH=8, DFF=5632):
        self.D, self.T, self.L, self.H, self.QL, self.KVL, self.GH, self.DFF = D, T, L, H, QL, KVL, GH, DFF
        self.KD = D // 128
        self.TB = min(512, T)
        self.NTB = T // self.TB
        self.NT128 = T // 128
        self.NCH = T // 64
        self.NB = min(256, T)
        self.NLq, self.NLk = QL // 128, KVL // 128
        self.o_cq = 0
        self.o_ckv = QL
        self.o_kpe = QL + KVL
        self.o_q = self.o_kpe + 64
        self.o_k = self.o_q + GH * 128
        self.o_v = self.o_k + GH * 128
        self.o_z = self.o_v + GH * 128
        self.o_b = self.o_z + GH * 128
        self.o_a = self.o_b + GH
        self.o_g = self.o_a + GH
        self.INW = self.o_g + 2 * D
        self.EPS = 1e-6
        self.HG = min(4, GH)
        nff = DFF // 128
        self.FQ = max(d for d in range(1, 12) if nff % d == 0)
        self.NFP = nff // self.FQ


class Ev:
    __slots__ = ("sem", "val")

    def __init__(self, sem, val=None):
        self.sem = sem
        self.val = val


class Buf:
    __slots__ = ("name", "w", "r")

    def __init__(self, name="b"):
        self.name = name
        self.w = None
        self.r = {}


class Eng:
    def __init__(self, name, eng, sem):
        self.name, self.eng, self.sem = name, eng, sem
        self.cnt = 0
        self.seen = {}
        self.pending = []
        self.nwait = 0
        self.nins = 0


class MK:
    def __init__(self, nc, stack, n_dma_sems=24):
        self.nc = nc
        ec = stack.enter_context
        self.pe = Eng("pe", nc.tensor, ec(nc.semaphore("s_pe")))
        self.act = Eng("act", nc.scalar, ec(nc.semaphore("s_act")))
        self.dve = Eng("dve", nc.vector, ec(nc.semaphore("s_dve")))
        self.pool = Eng("pool", nc.gpsimd, ec(nc.semaphore("s_pool")))
        self.sp = Eng("sp", nc.sync, ec(nc.semaphore("s_sp")))
        self.engs = [self.pe, self.act, self.dve, self.pool, self.sp]
        self.dsems = {}
        for q in (self.sp, self.pool):
            self.dsems[q.name] = [[ec(nc.semaphore(f"d_{q.name}{i}")), 0, None] for i in range(n_dma_sems)]
        self.drr = {"sp": 0, "pool": 0}

    def _wait(self, E, evs):
        best = {}
        for ev in evs:
            if ev is None:
                continue
            if ev.val is None:
                assert ev.sem is E.sem and E is self.pe, f"unresolved event waited by {E.name}"
                continue
            k = id(ev.sem)
            if k not in best or best[k].val < ev.val:
                best[k] = ev
        for k, ev in best.items():
            if E.seen.get(k, 0) < ev.val:
                E.eng.wait_ge(ev.sem, ev.val)
                E.seen[k] = ev.val
                E.nwait += 1

    @staticmethod
    def _deps(reads, writes):
        need = []
        for b in reads:
            need.append(b.w)
        for b in writes:
            need.append(b.w)
            need.extend(b.r.values())
        return need

    @staticmethod
    def _commit(ev, reads, writes):
        for b in reads:
            b.r[id(ev.sem)] = ev
        for b in writes:
            b.w = ev
            b.r = {}

    def op(self, E, fn, reads=(), writes=(), signal=True):
        self._wait(E, self._deps(reads, writes))
        ins = fn(E.eng)
        E.nins += 1
        ev = Ev(E.sem)
        E.pending.append(ev)
        if signal:
            E.cnt += 1
            ins.then_inc(E.sem, 1)
            for p in E.pending:
                p.val = E.cnt
            E.pending = []
        self._commit(ev, reads, writes)
        return ev

    def dma(self, Q, out_ap, in_ap, reads=(), writes=()):
        pool = self.dsems[Q.name]
        i = self.drr[Q.name]
        self.drr[Q.name] = (i + 1) % len(pool)
        slot = pool[i]
        need = self._deps(reads, writes)
        need.append(slot[2])
        self._wait(Q, need)
        slot[1] += 16
        ev = Ev(slot[0], slot[1])
        Q.eng.dma_start(out=out_ap, in_=in_ap).then_inc(slot[0], 16)
        Q.nins += 1
        slot[2] = ev
        self._commit(ev, reads, writes)
        return ev

    def barrier(self):
        for Q in (self.sp, self.pool):
            self._wait(Q, [s[2] for s in self.dsems[Q.name]])
            assert not Q.pending
            Q.cnt += 1
            Q.eng.sem_inc(Q.sem, 1)
        assert not self.pe.pending
        evs = [Ev(E.sem, E.cnt) for E in self.engs if E.cnt > 0]
        for E in self.engs:
            self._wait(E, evs)


class Ring:
    def __init__(self, items):
        self.items = items
        self.i = 0

    def next(self):
        it = self.items[self.i]
        self.i = (self.i + 1) % len(self.items)
        return it


def make_consts():
    c = np.zeros((128, 640), np.float32)
    c[:, 0:128] = np.eye(128)
    k = np.arange(64)
    c[0:64, 128:192] = (k[:, None] <= k[None, :])
    c[0:64, 192:256] = (k[:, None] > k[None, :])
    c[0:64, 256:320] = (k[:, None] >= k[None, :])
    c[:, 320:448] = 1.0
    p = np.arange(128)
    c[:, 448:576] = (p[:, None] <= p[None, :])
    inv = (1.0 / (10000.0 ** (np.arange(0, 64, 2, dtype=np.float32) / 64))).astype(np.float32)
    c[0:64, 576] = np.concatenate([inv, inv])
    c[0:32, 577] = -1.0
    c[32:64, 577] = 1.0
    return c


class Builder:
    def __init__(self, cfg, nlayers=None, debug_dump=False):
        self.cfg = cfg
        self.nl = cfg.L if nlayers is None else nlayers
        self.nc = bass.Bass("TRN2", target_bir_lowering=False)
        self._uid = 0

    def I(self, E, name, *a, R=(), W=(), sig=True, **kw):
        return self.mk.op(E, lambda e: getattr(e, name)(*a, **kw), reads=R, writes=W, signal=sig)

    def MM(self, out, lhsT, rhs, start, stop, R, W, sig):
        return self.mk.op(self.mk.pe, lambda e: e.matmul(out, lhsT=lhsT, rhs=rhs, start=start, stop=stop),
                          reads=R, writes=W, signal=sig)

    def TR(self, out, in_, ident, R, W, sig=True):
        return self.mk.op(self.mk.pe, lambda e: e.transpose(out, in_, ident), reads=R, writes=W, signal=sig)

    def ACT(self, out, in_, func, R, W, bias=None, scale=1.0):
        if bias is None:
            return self.mk.op(self.mk.act, lambda e: e.activation(out, in_, func, scale=scale), reads=R, writes=W)
        return self.mk.op(self.mk.act, lambda e: e.activation(out, in_, func, bias=bias, scale=scale), reads=R, writes=W)

    def sb(self, shape, dtype, name=None):
        self._uid += 1
        t = self.scopes[-1].enter_context(self.nc.sbuf_tensor(f"{name or 't'}_{self._uid}", list(shape), dtype))
        return t

    def push(self):
        st = ExitStack()
        st.__enter__()
        self.scopes.append(st)

    def pop(self):
        self.mk.barrier()
        st = self.scopes.pop()
        st.__exit__(None, None, None)

    def dram(self, name, shape, dtype, kind):
        return self.nc.dram_tensor(name, list(shape), dtype, kind=kind).ap()

    def mark(self, name):
        self.marks.append((name, self.mk.pe.nins + self.mk.pe.nwait))

    def build(self):
        cfg, nc = self.cfg, self.nc
        self.marks = []
        D, T, L, KD = cfg.D, cfg.T, cfg.L, cfg.KD
        dr = self.dram
        self.x_d = dr("x", [KD, 128, T], F32, "ExternalInput")
        self.c_d = dr("c", [128, KD], F32, "ExternalInput")
        self.pos_d = dr("pos", [1, T], I32, "ExternalInput")
        self.cst_d = dr("consts", [128, 640], F32, "ExternalInput")
        self.w_ada = dr("w_ada", [L, D, 6 * D], F32, "ExternalInput")
        self.par_d = dr("par", [128, L, 8 * KD + 16], F32, "ExternalInput")
        self.cw_d = dr("cw", [128, L, 3 * cfg.GH, 4], F32, "ExternalInput")
        self.w_in = dr("w_in", [L, D, cfg.INW], F32, "ExternalInput")
        self.w_uq = dr("w_uq", [L, cfg.QL, cfg.H * 192], F32, "ExternalInput")
        self.w_ukv = dr("w_ukv", [L, cfg.KVL, cfg.H * 256], F32, "ExternalInput")
        self.w_o_mla = dr("w_o_mla", [L, cfg.H * 128, D], F32, "ExternalInput")
        self.A_log = dr("A_log", [L, 1, cfg.GH], F32, "ExternalInput")
        self.dt_bias = dr("dt_bias", [L, 1, cfg.GH], F32, "ExternalInput")
        self.w_o_gdn = dr("w_o_gdn", [L, cfg.GH * 128, D], F32, "ExternalInput")
        self.w_o = dr("w_o", [L, D, D], F32, "ExternalInput")
        self.w_gate_up = dr("w_gate_up", [L, D, 2 * cfg.DFF], F32, "ExternalInput")
        self.w_down = dr("w_down", [L, cfg.DFF, D], F32, "ExternalInput")
        self.final_norm = dr("final_norm", [128, KD], F32, "ExternalInput")
        self.out_d = dr("out", [KD, 128, T], F32, "ExternalOutput")
        self.xT = dr("s_xT", [KD, 128, T], F32, "Internal")
        self.s_lat = dr("s_lat", [cfg.NLq + cfg.NLk, 128, T], F32, "Internal")
        self.s_kpe = dr("s_kpe", [2, 64, T], F32, "Internal")
        self.s_qkv = dr("s_qkv", [3 * cfg.GH, 128, T], F32, "Internal")
        self.s_z = dr("s_z", [cfg.GH, 128, T], F32, "Internal")
        self.s_ba = dr("s_ba", [2 * cfg.GH, T], F32, "Internal")
        self.s_gate = dr("s_gate", [2 * KD, 128, T], BF16, "Internal")
        self.s_ma = dr("s_ma", [KD, 128, T], BF16, "Internal")
        self.xTB = {}
        self.outB = Buf("out")

        with ExitStack() as top:
            self.mk = MK(nc, top)
            self.scopes = [top]
            mk = self.mk
            self.ps = [top.enter_context(nc.psum_tensor(f"ps{i}", [128, 512], F32)) for i in range(8)]
            self.pb = [Buf(f"ps{i}") for i in range(8)]
            stop = getattr(self, "stop_after", None)
            order = ["globals", "x0", "ada", "rope", "norm0", "proj", "mla", "gdn", "norm1", "ffn", "final"]
            lim = order.index(stop) if stop else len(order)
            on = lambda nm: order.index(nm) <= lim
            self.setup_globals()
            if on("x0"):
                self.phase_x0()
            if on("ada"):
                self.phase_ada()
            if on("rope"):
                self.phase_rope()
            for l in range(self.nl):
                if on("norm0"):
                    self.push()
                    hT = self.sb([128, KD, T], BF16, "hT")
                    hB = Buf("hT")
                    self.phase_norm(l, 0, hT, hB)
                    if on("proj"):
                        self.phase_proj(l, hT, hB)
                    self.pop()
                if on("mla"):
                    self.phase_mla(l)
                if on("gdn"):
                    self.phase_gdn(l)
                if on("norm1"):
                    self.push()
                    hT = self.sb([128, KD, T], BF16, "hT2")
                    hB = Buf("hT2")
                    self.phase_norm(l, 1, hT, hB)
                    if on("ffn"):
                        self.phase_ffn(l, hT, hB)
                    self.pop()
            if on("final"):
                self.phase_final()
            mk._wait(mk.sp, [self.outB.w] + list(self.outB.r.values()))
            mk.barrier()
            self.mark('end')
            self.stats = {e.name: (e.nins, e.nwait) for e in mk.engs}
        return nc

    def setup_globals(self):
        self.mark('setup_globals')
        cfg, mk = self.cfg, self.mk
        KD, L = cfg.KD, cfg.L
        self.cst = self.sb([128, 640], F32, "cst")
        self.cstB = Buf("cst")
        mk.dma(mk.sp, self.cst[:], self.cst_d, writes=[self.cstB])
        c = self.cst
        self.identF = c[:, 0:128]
        self.U64 = c[0:64, 128:192]
        self.SL64 = c[0:64, 192:256]
        self.LINC = c[0:64, 256:320]
        self.onesF = c[:, 320:448]
        self.invf = c[0:64, 576:577]
        self.sgn = c[0:64, 577:578]
        self.cb = self.sb([128, 384], BF16, "cb")
        self.cbB = Buf("cb")
        self.identB = self.cb[:, 0:128]
        self.onesB = self.cb[:, 128:256]
        self.causB = self.cb[:, 256:384]
        self.I(mk.dve, "tensor_copy", self.cb[:, 0:128], c[:, 0:128], R=[self.cstB], W=[self.cbB])
        self.I(mk.dve, "tensor_copy", self.cb[:, 128:256], c[:, 320:448], R=[self.cstB], W=[self.cbB])
        self.I(mk.dve, "tensor_copy", self.cb[:, 256:384], c[:, 448:576], R=[self.cstB], W=[self.cbB])
        self.cc = self.sb([128, 4], F32, "cc")
        self.ccB = Buf("cc")
        self.I(mk.dve, "memset", self.cc[:, 0:1], cfg.EPS, W=[self.ccB])
        self.I(mk.dve, "memset", self.cc[:, 1:2], 1.0, W=[self.ccB])
        self.I(mk.dve, "memset", self.cc[:, 2:3], 0.0, W=[self.ccB])
        self.epsc = self.cc[:, 0:1]
        self.onec = self.cc[:, 1:2]
        self.par = self.sb([128, L, 8 * KD + 16], F32, "par")
        self.parB = Buf("par")
        self.mod = self.sb([128, L, 8 * KD], F32, "mod")
        self.modB = Buf("mod")
        self.cw = self.sb([128, L, 3 * cfg.GH, 4], F32, "cw")
        self.cwB = Buf("cw")
        self.fn = self.sb([128, KD], F32, "fn")
        self.fnB = Buf("fn")
        self.cact = self.sb([128, KD], BF16, "cact")
        self.cactB = Buf("cact")
        self.grow = self.sb([64, L, 2, cfg.GH], F32, "grow")
        self.growB = Buf("grow")
        mk.dma(mk.sp, self.par[:], self.par_d, writes=[self.parB])
        mk.dma(mk.sp, self.cw[:], self.cw_d, writes=[self.cwB])
        mk.dma(mk.sp, self.fn[:], self.final_norm, writes=[self.fnB])
        for l in range(L):
            mk.dma(mk.sp, self.grow[:, l, 0, :], self.A_log[l].partition_broadcast(64), writes=[self.growB])
            mk.dma(mk.sp, self.grow[:, l, 1, :], self.dt_bias[l].partition_broadcast(64), writes=[self.growB])
            self.ACT(self.grow[:, l, 0, :], self.grow[:, l, 0, :], AF.Exp, R=[self.growB], W=[self.growB])
            self.I(mk.dve, "tensor_scalar", self.grow[:, l, 0, :], self.grow[:, l, 0, :], -1.0, None, op0=ALU.mult,
                   R=[self.growB], W=[self.growB])
        self.push()
        ctmp = self.sb([128, KD], F32, "ctmp")
        ctB = Buf("ctmp")
        mk.dma(mk.sp, ctmp[:], self.c_d, writes=[ctB])
        self.ACT(self.cact[:, :], ctmp[:, :], AF.Silu, R=[ctB], W=[self.cactB])
        self.pop()
        self.o_nmix, self.o_nffn, self.o_bada, self.o_qan, self.o_kvan, self.o_gn = 0, KD, 2 * KD, 8 * KD, 8 * KD + 4, 8 * KD + 8

    def phase_x0(self):
        self.mark('phase_x0')
        cfg, mk = self.cfg, self.mk
        for k in range(cfg.KD):
            mk.dma(mk.sp, self.xT[k], self.x_d[k])
        mk.barrier()

    def phase_ada(self):
        self.mark('phase_ada')
        cfg, mk = self.cfg, self.mk
        KD = cfg.KD
        self.push()
        slots = Ring([(self.sb([128, KD, 512], BF16, "wada"), Buf("wada")) for _ in range(3)])
        psA, psAB = self.ps[0], self.pb[0]
        for l in range(self.nl):
            nblk = 6 * cfg.D // 512
            for jb in range(nblk):
                sl, slB = slots.next()
                mk.dma(mk.pool, sl[:], self.w_ada[l][:, jb * 512:(jb + 1) * 512].rearrange("(k p) n -> p k n", p=128), writes=[slB])
                for jj in range(4):
                    j = jb * 4 + jj
                    for k in range(KD):
                        self.MM(psA[:, j:j + 1], sl[:, k, jj * 128:(jj + 1) * 128], self.cact[:, k:k + 1], k == 0, k == KD - 1,
                                R=[slB, self.cactB], W=[psAB], sig=(k == KD - 1))
            m = self.mod
            self.I(mk.dve, "tensor_tensor", m[:, l, 0:6 * KD], psA[:, 0:6 * KD], self.par[:, l, self.o_bada:self.o_bada + 6 * KD],
                   op=ALU.add, R=[psAB, self.parB], W=[self.modB])
            self.I(mk.dve, "scalar_tensor_tensor", m[:, l, 6 * KD:7 * KD], m[:, l, KD:2 * KD], 1.0, self.par[:, l, 0:KD],
                   op0=ALU.add, op1=ALU.mult, R=[self.modB, self.parB], W=[self.modB])
            self.I(mk.dve, "scalar_tensor_tensor", m[:, l, 7 * KD:8 * KD], m[:, l, 4 * KD:5 * KD], 1.0, self.par[:, l, KD:2 * KD],
                   op0=ALU.add, op1=ALU.mult, R=[self.modB, self.parB], W=[self.modB])
        self.pop()

    def phase_rope(self):
        self.mark('phase_rope')
        cfg, mk = self.cfg, self.mk
        T = cfg.T
        self.COS = self.sb([64, T], F32, "cos")
        self.SINS = self.sb([64, T], F32, "sins")
        self.ropeB = Buf("rope")
        self.push()
        pi_ = self.sb([64, T], I32, "posi")
        t0 = self.sb([64, T], F32, "t0")
        t1 = self.sb([64, T], F32, "t1")
        t2 = self.sb([64, T], F32, "t2")
        B = Buf("ropetmp")
        mk.dma(mk.sp, pi_[:], self.pos_d.partition_broadcast(64), writes=[B])
        V = lambda name, *a, **kw: self.I(mk.dve, name, *a, R=[B, self.cstB], W=[B], **kw)
        V("tensor_copy", t0[:], pi_[:])
        V("tensor_scalar", t0[:], t0[:], self.invf, None, op0=ALU.mult)
        V("tensor_scalar", t0[:], t0[:], float(1.0 / (2 * np.pi)), None, op0=ALU.mult)
        for which in range(2):
            dst = self.SINS if which == 0 else self.COS
            if which == 1:
                V("tensor_scalar", t0[:], t0[:], 0.25, None, op0=ALU.add)
            V("tensor_copy", pi_[:], t0[:])
            V("tensor_copy", t1[:], pi_[:])
            V("tensor_tensor", t1[:], t0[:], t1[:], op=ALU.subtract)
            V("tensor_scalar", t2[:], t1[:], 0.5, None, op0=ALU.is_ge)
            V("tensor_tensor", t1[:], t1[:], t2[:], op=ALU.subtract)
            V("tensor_scalar", t2[:], t1[:], -0.5, None, op0=ALU.is_lt)
            V("tensor_tensor", t1[:], t1[:], t2[:], op=ALU.add)
            self.ACT(dst[:], t1[:], AF.Sin, R=[B], W=[self.ropeB], scale=float(2 * np.pi))
        self.I(mk.dve, "tensor_scalar", self.SINS[:], self.SINS[:], self.sgn, None, op0=ALU.mult, R=[self.ropeB, self.cstB], W=[self.ropeB])
        self.pop()

    def phase_norm(self, l, which, hT, hB):
        self.mark(f'norm{which}_{l}')
        cfg, mk = self.cfg, self.mk
        KD, NB = cfg.KD, cfg.NB
        a_off = (6 * KD) if which == 0 else (7 * KD)
        sh_off = 0 if which == 0 else 3 * KD
        self.push()
        xb = Ring([(self.sb([128, KD, NB], F32, "xb"), Buf("xb")) for _ in range(2)])
        sq = Ring([(self.sb([128, KD, NB], BF16, "sq"), Buf("sq")) for _ in range(2)])
        rs = Ring([(self.sb([128, NB], F32, "rs"), Buf("rs")) for _ in range(2)])
        tmp = Ring([(self.sb([128, NB], F32, "tmp"), Buf("tmp")) for _ in range(4)])
        banks = Ring([0, 1])
        xTv = self.xT.rearrange("k p t -> p k t")
        for blk in range(cfg.T // NB):
            sl = slice(blk * NB, (blk + 1) * NB)
            x_, xB = xb.next()
            rd = [self.xTB[key] for key in self.xTB]
            mk.dma(mk.sp, x_[:], xTv[:, :, sl], reads=rd, writes=[xB])
            s_, sB = sq.next()
            self.ACT(s_[:], x_[:], AF.Square, R=[xB], W=[sB])
            b = banks.next()
            for k in range(KD):
                self.MM(self.ps[b][:, 0:NB], self.onesB, s_[:, k, :], k == 0, k == KD - 1, R=[self.cbB, sB], W=[self.pb[b]], sig=(k == KD - 1))
            r_, rB = rs.next()
            self.ACT(r_[:], self.ps[b][:, 0:NB], AF.Sqrt, R=[self.pb[b], self.ccB], W=[rB], bias=self.epsc, scale=1.0 / cfg.D)
            self.I(mk.dve, "reciprocal", r_[:], r_[:], R=[rB], W=[rB])
            for k in range(KD):
                t_, tB = tmp.next()
                self.I(mk.dve, "scalar_tensor_tensor", t_[:], x_[:, k, :], self.mod[:, l, a_off + k:a_off + k + 1], r_[:],
                       op0=ALU.mult, op1=ALU.mult, R=[xB, self.modB, rB], W=[tB])
                self.ACT(hT[:, k, sl], t_[:], AF.Identity, R=[tB, self.modB], W=[hB], bias=self.mod[:, l, sh_off + k:sh_off + k + 1])
        self.pop()

    def stream_mm(self, W2d, blocks, rhs, rhsB, KC, evac, slots, extraR=()):
        cfg, mk = self.cfg, self.mk
        TB, NTB = cfg.TB, cfg.NTB
        nsets = 8 // NTB
        Wv = W2d.rearrange("(k p) n -> p k n", p=128)
        for segs, groups in blocks:
            sl, slB = slots.next()
            off = 0
            for (c0, n) in segs:
                mk.dma(mk.pool, sl[:, 0:KC, off:off + n], Wv[:, :, c0:c0 + n], writes=[slB])
                off += n
            for (goff, M, gid) in groups:
                s = self._pset
                self._pset = (self._pset + 1) % nsets
                for k in range(KC):
                    for tb in range(NTB):
                        b = s * NTB + tb
                        self.MM(self.ps[b][0:M, 0:TB], sl[:, k, goff:goff + M], rhs[:, k, tb * TB:(tb + 1) * TB], k == 0, k == KC - 1,
                                R=[slB, rhsB] + list(extraR), W=[self.pb[b]], sig=(k == KC - 1))
                for tb in range(NTB):
                    b = s * NTB + tb
                    evac(gid, M, tb, self.ps[b][0:M, 0:TB], self.pb[b])

    def col_blocks(self, c_lo, ncols, gid0=0):
        blocks = []
        g = gid0
        for c0 in range(c_lo, c_lo + ncols, 512):
            n = min(512, c_lo + ncols - c0)
            groups = []
            for o in range(0, n, 128):
                groups.append((o, min(128, n - o), g))
                g += 1
            blocks.append(([(c0, n)], groups))
        return blocks

    def phase_proj(self, l, hT, hB):
        self.mark(f'proj_{l}')
        cfg, mk = self.cfg, self.mk
        KD, TB = cfg.KD, cfg.TB
        self._pset = 0
        slots = Ring([(self.sb([128, KD, 512], BF16, "win"), Buf("win")) for _ in range(3)])
        stg = Ring([(self.sb([128, TB], F32, "pstg"), Buf("pstg")) for _ in range(6)])
        stgb = Ring([(self.sb([128, TB], BF16, "pstgb"), Buf("pstgb")) for _ in range(4)])
        flip = [0]

        def mk_evac(dst_fn, sigm=False):
            def evac(gid, M, tb, psap, psB):
                if sigm:
                    st, stB = stgb.next()
                    self.ACT(st[0:M, :], psap, AF.Sigmoid, R=[psB], W=[stB])
                else:
                    st, stB = stg.next()
                    flip[0] ^= 1
                    if flip[0]:
                        self.ACT(st[0:M, :], psap, AF.Copy, R=[psB], W=[stB])
                    else:
                        self.I(mk.dve, "tensor_copy", st[0:M, :], psap, R=[psB], W=[stB])
                mk.dma(mk.sp, dst_fn(gid, M, tb), st[0:M, :], reads=[stB])
            return evac
        W = self.w_in[l]
        tsl = lambda tb: slice(tb * TB, (tb + 1) * TB)
        self.stream_mm(W, self.col_blocks(0, cfg.QL + cfg.KVL), hT, hB, KD,
                       mk_evac(lambda g, M, tb: self.s_lat[g, :, tsl(tb)]), slots)
        o = cfg.o_kpe
        blocks = [([(o, 64), (o + 32, 32), (o, 32)], [(0, 64, 0), (64, 64, 1)])]
        self.stream_mm(W, blocks, hT, hB, KD, mk_evac(lambda g, M, tb: self.s_kpe[g, :, tsl(tb)]), slots)
        self.stream_mm(W, self.col_blocks(cfg.o_q, 3 * cfg.GH * 128), hT, hB, KD,
                       mk_evac(lambda g, M, tb: self.s_qkv[g, :, tsl(tb)]), slots)
        self.stream_mm(W, self.col_blocks(cfg.o_z, cfg.GH * 128), hT, hB, KD,
                       mk_evac(lambda g, M, tb: self.s_z[g, :, tsl(tb)]), slots)
        blocks = [([(cfg.o_b, 2 * cfg.GH)], [(0, 2 * cfg.GH, 0)])]
        self.stream_mm(W, blocks, hT, hB, KD, mk_evac(lambda g, M, tb: self.s_ba[:, tsl(tb)]), slots)
        self.stream_mm(W, self.col_blocks(cfg.o_g, 2 * cfg.D), hT, hB, KD,
                       mk_evac(lambda g, M, tb: self.s_gate[g, :, tsl(tb)], sigm=True), slots)

    def phase_mla(self, l):
        self.mark(f'mla_{l}')
        cfg, mk = self.cfg, self.mk
        T, TB, NTB, H = cfg.T, cfg.TB, cfg.NTB, cfg.H
        NLq, NLk = cfg.NLq, cfg.NLk
        NL = NLq + NLk
        self.push()
        cqn = self.sb([128, NLq, T], BF16, "cqn"); cqnB = Buf("cqn")
        ckvn = self.sb([128, NLk, T], BF16, "ckvn"); ckvnB = Buf("ckvn")
        kpeR = self.sb([64, T], BF16, "kpeR"); kpeB = Buf("kpeR")
        aT = self.sb([128, H, T], BF16, "aT"); aTB = Buf("aT")
        self.push()
        lat = Ring([(self.sb([128, NL, TB], F32, "lat"), Buf("lat")) for _ in range(2)])
        sq = Ring([(self.sb([128, NL, TB], BF16, "lsq"), Buf("lsq")) for _ in range(1)])
        rq = Ring([(self.sb([128, 2, TB], F32, "lrs"), Buf("lrs")) for _ in range(2)])
        latv = self.s_lat.rearrange("k p t -> p k t")
        for tb in range(NTB):
            sl = slice(tb * TB, (tb + 1) * TB)
            la, laB = lat.next()
            mk.dma(mk.sp, la[:], latv[:, :, sl], writes=[laB])
            s_, sB = sq.next()
            self.ACT(s_[:], la[:], AF.Square, R=[laB], W=[sB])
            r_, rB = rq.next()
            for which, (k0, nk, dim) in enumerate([(0, NLq, cfg.QL), (NLq, NLk, cfg.KVL)]):
                b = which
                for k in range(nk):
                    self.MM(self.ps[b][:, 0:TB], self.onesB, s_[:, k0 + k, :], k == 0, k == nk - 1, R=[self.cbB, sB], W=[self.pb[b]], sig=(k == nk - 1))
                self.ACT(r_[:, which, :], self.ps[b][:, 0:TB], AF.Sqrt, R=[self.pb[b], self.ccB], W=[rB], bias=self.epsc, scale=1.0 / dim)
            self.I(mk.dve, "reciprocal", r_[:], r_[:], R=[rB], W=[rB])
            for k in range(NLq):
                self.I(mk.dve, "scalar_tensor_tensor", cqn[:, k, sl], la[:, k, :], self.par[:, l, self.o_qan + k:self.o_qan + k + 1], r_[:, 0, :],
                       op0=ALU.mult, op1=ALU.mult, R=[laB, self.parB, rB], W=[cqnB])
            for k in range(NLk):
                self.I(mk.dve, "scalar_tensor_tensor", ckvn[:, k, sl], la[:, NLq + k, :], self.par[:, l, self.o_kvan + k:self.o_kvan + k + 1], r_[:, 1, :],
                       op0=ALU.mult, op1=ALU.mult, R=[laB, self.parB, rB], W=[ckvnB])
        kp = self.sb([64, 2, T], F32, "kp"); kpB = Buf("kp")
        mk.dma(mk.sp, kp[:], self.s_kpe.rearrange("g p t -> p g t"), writes=[kpB])
        self.I(mk.dve, "tensor_tensor", kp[:, 0, :], kp[:, 0, :], self.COS[:], op=ALU.mult, R=[kpB, self.ropeB], W=[kpB])
        self.I(mk.dve, "tensor_tensor", kp[:, 1, :], kp[:, 1, :], self.SINS[:], op=ALU.mult, R=[kpB, self.ropeB], W=[kpB])
        self.I(mk.dve, "tensor_tensor", kpeR[:], kp[:, 0, :], kp[:, 1, :], op=ALU.add, R=[kpB], W=[kpeB])
        self.pop()
        wq = self.sb([128, NLq, H * 192], BF16, "wq"); wqB = Buf("wq")
        wqs = self.sb([128, NLq, H, 64], BF16, "wqs"); wqsB = Buf("wqs")
        wkv = self.sb([128, NLk, H * 256], BF16, "wkv"); wkvB = Buf("wkv")
        mk.dma(mk.pool, wq[:], self.w_uq[l].rearrange("(k p) n -> p k n", p=128), writes=[wqB])
        wq4 = self.w_uq[l].rearrange("(k p) (h e) -> p k h e", p=128, e=192)
        for k in range(NLq):
            mk.dma(mk.pool, wqs[:, k, :, 0:32], wq4[:, k, :, 160:192], writes=[wqsB])
            mk.dma(mk.pool, wqs[:, k, :, 32:64], wq4[:, k, :, 128:160], writes=[wqsB])
        mk.dma(mk.pool, wkv[:], self.w_ukv[l].rearrange("(k p) n -> p k n", p=128), writes=[wkvB])
        self.mark(f'mla_attn_{l}')
        knT = Ring([(self.sb([128, T], BF16, "knT"), Buf("knT")) for _ in range(2)])
        qnT = Ring([(self.sb([128, T], BF16, "qnT"), Buf("qnT")) for _ in range(2)])
        qpR = Ring([(self.sb([64, T], BF16, "qpR"), Buf("qpR")) for _ in range(2)])
        vtk = Ring([(self.sb([128, cfg.NT128, 128], BF16, "vtk"), Buf("vtk")) for _ in range(2)])
        rt = Ring([(self.sb([64, 2, TB], F32, "rt"), Buf("rt")) for _ in range(2)])
        Pt = Ring([(self.sb([128, TB], BF16, "Pt"), Buf("Pt")) for _ in range(4)])
        rec = Ring([(self.sb([128, TB], F32, "rec"), Buf("rec")) for _ in range(2)])
        pj = Ring([7, 2, 3, 4, 5])
        sc = Ring([0, 1, 6])
        od = Ring([(2, 3), (4, 5)])
        scale = float(192 ** -0.5)
        KT = TB // 128
        for h in range(H):
            kn, knB = knT.next(); qn, qnB = qnT.next(); qp, qpB = qpR.next(); vt, vtB = vtk.next()
            for tb in range(NTB):
                sl = slice(tb * TB, (tb + 1) * TB)
                b = pj.next()
                for k in range(NLk):
                    self.MM(self.ps[b][:, 0:TB], wkv[:, k, h * 256:h * 256 + 128], ckvn[:, k, sl], k == 0, k == NLk - 1, R=[wkvB, ckvnB], W=[self.pb[b]], sig=(k == NLk - 1))
                self.ACT(kn[:, sl], self.ps[b][:, 0:TB], AF.Copy, R=[self.pb[b]], W=[knB])
                b = pj.next()
                for k in range(NLq):
                    self.MM(self.ps[b][:, 0:TB], wq[:, k, h * 192:h * 192 + 128], cqn[:, k, sl], k == 0, k == NLq - 1, R=[wqB, cqnB], W=[self.pb[b]], sig=(k == NLq - 1))
                self.I(mk.dve, "tensor_copy", qn[:, sl], self.ps[b][:, 0:TB], R=[self.pb[b]], W=[qnB])
                b1 = pj.next()
                for k in range(NLq):
                    self.MM(self.ps[b1][0:64, 0:TB], wq[:, k, h * 192 + 128:h * 192 + 192], cqn[:, k, sl], k == 0, k == NLq - 1, R=[wqB, cqnB], W=[self.pb[b1]], sig=(k == NLq - 1))
                b2 = pj.next()
                for k in range(NLq):
                    self.MM(self.ps[b2][0:64, 0:TB], wqs[:, k, h, :], cqn[:, k, sl], k == 0, k == NLq - 1, R=[wqsB, cqnB], W=[self.pb[b2]], sig=(k == NLq - 1))
                r_, rB = rt.next()
                self.I(mk.dve, "tensor_tensor", r_[:, 0, :], self.ps[b1][0:64, 0:TB], self.COS[:, sl], op=ALU.mult, R=[self.pb[b1], self.ropeB], W=[rB])
                self.I(mk.dve, "tensor_tensor", r_[:, 1, :], self.ps[b2][0:64, 0:TB], self.SINS[:, sl], op=ALU.mult, R=[self.pb[b2], self.ropeB], W=[rB])
                self.I(mk.dve, "tensor_tensor", qp[:, sl], r_[:, 0, :], r_[:, 1, :], op=ALU.add, R=[rB], W=[qpB])
            for i0 in range(0, cfg.NT128, 4):
                n4 = min(4, cfg.NT128 - i0)
                b = pj.next()
                for ii in range(n4):
                    i = i0 + ii
                    for k in range(NLk):
                        self.MM(self.ps[b][:, ii * 128:(ii + 1) * 128], ckvn[:, k, i * 128:(i + 1) * 128], wkv[:, k, h * 256 + 128:h * 256 + 256],
                                k == 0, k == NLk - 1, R=[wkvB, ckvnB], W=[self.pb[b]], sig=(ii == n4 - 1 and k == NLk - 1))
                self.ACT(vt[:, i0:i0 + n4, :], self.ps[b][:, 0:n4 * 128].rearrange("p (i d) -> p i d", i=n4), AF.Copy, R=[self.pb[b]], W=[vtB])
            for qb in range(NTB):
                bo, bd = od.next()
                nkt = (qb + 1) * KT
                pend = []

                def emit_pv(item):
                    kt, q0, P_, PB = item
                    self.MM(self.ps[bo][:, q0:TB], vt[:, kt, :], P_[:, q0:TB], kt == 0, kt == nkt - 1, R=[vtB, PB], W=[self.pb[bo]], sig=False)
                    self.MM(self.ps[bd][:, q0:TB], self.onesB, P_[:, q0:TB], kt == 0, kt == nkt - 1, R=[self.cbB, PB], W=[self.pb[bd]], sig=True)
                for kt in range(nkt):
                    j = kt - qb * KT
                    q0 = max(0, j) * 128
                    bs = sc.next()
                    qsl = slice(qb * TB + q0, (qb + 1) * TB)
                    ksl = slice(kt * 128, (kt + 1) * 128)
                    self.MM(self.ps[bs][:, q0:TB], kn[:, ksl], qn[:, qsl], True, False, R=[knB, qnB], W=[self.pb[bs]], sig=False)
                    self.MM(self.ps[bs][:, q0:TB], kpeR[:, ksl], qp[:, qsl], False, True, R=[kpeB, qpB], W=[self.pb[bs]], sig=True)
                    P_, PB = Pt.next()
                    self.ACT(P_[:, q0:TB], self.ps[bs][:, q0:TB], AF.Exp, R=[self.pb[bs]], W=[PB], scale=scale)
                    if j >= 0:
                        self.I(mk.dve, "tensor_tensor", P_[:, q0:q0 + 128], P_[:, q0:q0 + 128], self.causB, op=ALU.mult, R=[PB, self.cbB], W=[PB])
                    pend.append((kt, q0, P_, PB))
                    if len(pend) > 2:
                        emit_pv(pend.pop(0))
                while pend:
                    emit_pv(pend.pop(0))
                rc, rcB = rec.next()
                self.I(mk.dve, "reciprocal", rc[:], self.ps[bd][:, 0:TB], R=[self.pb[bd]], W=[rcB])
                self.I(mk.dve, "tensor_tensor", aT[:, h, qb * TB:(qb + 1) * TB], self.ps[bo][:, 0:TB], rc[:], op=ALU.mult, R=[self.pb[bo], rcB], W=[aTB])
        self.mark(f'mla_out_{l}')
        self._pset = 0
        slots = Ring([(self.sb([128, H, 512], BF16, "womla"), Buf("womla")) for _ in range(2)])
        gt = Ring([(self.sb([128, TB], BF16, "gat"), Buf("gat")) for _ in range(4)])
        mo = Ring([(self.sb([128, TB], BF16, "mo"), Buf("mo")) for _ in range(4)])

        def evac(g, M, tb, psap, psB):
            sl = slice(tb * TB, (tb + 1) * TB)
            g_, gB = gt.next()
            mk.dma(mk.sp, g_[:], self.s_gate[g, :, sl], writes=[gB])
            m_, mB = mo.next()
            self.I(mk.dve, "tensor_tensor", m_[:], psap, g_[:], op=ALU.mult, R=[psB, gB], W=[mB])
            mk.dma(mk.sp, self.s_ma[g, :, sl], m_[:], reads=[mB])
        self.stream_mm(self.w_o_mla[l], self.col_blocks(0, cfg.D), aT, aTB, H, evac, slots)
        self.pop()

    def phase_gdn(self, l):
        self.mark(f'gdn_{l}')
        cfg, mk = self.cfg, self.mk
        T, TB, NTB, GH, HG, NCH, KD = cfg.T, cfg.TB, cfg.NTB, cfg.GH, cfg.HG, cfg.NCH, cfg.KD
        V = mk.dve
        self.push()
        ogT = self.sb([128, GH, T], BF16, "ogT"); ogB = Buf("ogT")
        self.push()
        gsh = [64, NCH, GH]
        beta = self.sb(gsh, F32, "beta"); g_ = self.sb(gsh, F32, "g"); expG = self.sb(gsh, F32, "expG")
        expGLmG = self.sb(gsh, F32, "expGLmG"); c1 = self.sb(gsh, F32, "c1"); nbeta = self.sb(gsh, F32, "nbeta")
        expGL = self.sb([128, NCH, GH], F32, "expGL")
        gB = Buf("gates")
        self.push()
        ba = self.sb([2 * GH, T], F32, "ba"); baB = Buf("ba")
        batok = self.sb([64, NCH, 2 * GH], F32, "batok"); btB = Buf("batok")
        mk.dma(mk.sp, ba[:], self.s_ba, writes=[baB])
        b = 0
        for n in range(NCH):
            self.TR(self.ps[b][0:64, n * 2 * GH:(n + 1) * 2 * GH], ba[:, n * 64:(n + 1) * 64], self.identF[0:2 * GH, 0:2 * GH],
                    R=[baB, self.cstB], W=[self.pb[b]], sig=(n == NCH - 1))
        self.I(V, "tensor_copy", batok[:], self.ps[b][0:64, 0:NCH * 2 * GH].rearrange("p (n g) -> p n g", n=NCH), R=[self.pb[b]], W=[btB])
        self.ACT(beta[:], batok[:, :, 0:GH], AF.Sigmoid, R=[btB], W=[gB])
        self.I(V, "tensor_tensor", g_[:], batok[:, :, GH:2 * GH], self.grow[:, l, 1, :].unsqueeze(1).to_broadcast(gsh), op=ALU.add, R=[btB, self.growB], W=[gB])
        self.ACT(g_[:], g_[:], AF.Exp, R=[gB], W=[gB])
        self.ACT(g_[:], g_[:], AF.Ln, R=[gB, self.ccB], W=[gB], bias=self.onec[0:64, :])
        self.I(V, "tensor_tensor", g_[:], g_[:], self.grow[:, l, 0, :].unsqueeze(1).to_broadcast(gsh), op=ALU.mult, R=[gB, self.growB], W=[gB])
        for (lhs, M, dst, bb) in [(self.U64, 64, expG, 1), (self.SL64, 64, expGLmG, 2), (self.onesF[0:64, :], 128, expGL, 3)]:
            for n in range(NCH):
                self.MM(self.ps[bb][0:M, n * GH:(n + 1) * GH], lhs, g_[:, n, :], True, True, R=[self.cstB, gB], W=[self.pb[bb]], sig=(n == NCH - 1))
            self.ACT(dst[:], self.ps[bb][0:M, 0:NCH * GH].rearrange("p (n g) -> p n g", n=NCH), AF.Exp, R=[self.pb[bb]], W=[gB])
        self.I(V, "tensor_tensor", c1[:], beta[:], expG[:], op=ALU.mult, R=[gB], W=[gB])
        self.I(V, "tensor_scalar", nbeta[:], beta[:], -1.0, None, op0=ALU.mult, R=[gB], W=[gB])
        self.pop()
        gstop = getattr(self, "gstop", 9)
        if gstop == 0:
            self.pop(); self.pop()
            return
        for gp in range(GH // HG):
            hs = slice(gp * HG, (gp + 1) * HG)
            self.push()
            qT = self.sb([128, HG, T], BF16, "gqT"); kT = self.sb([128, HG, T], BF16, "gkT"); vT = self.sb([128, HG, T], BF16, "gvT")
            qkvB = Buf("gqkv")
            self.mark(f'gdn_g1_{l}_{gp}')
            self.push()
            ub = Ring([(self.sb([128, T], F32, "u"), Buf("u")) for _ in range(2)])
            ab = Ring([(self.sb([128, T], F32, "acc"), Buf("acc")) for _ in range(2)])
            sqb = Ring([(self.sb([128, T], BF16, "csq"), Buf("csq")) for _ in range(1)])
            rsb = Ring([(self.sb([128, TB], F32, "crs"), Buf("crs")) for _ in range(2)])
            banks = Ring([4, 5, 6, 7])
            for kind, dstT in ((0, qT), (1, kT), (2, vT)):
                for hh in range(HG):
                    ci = kind * GH + gp * HG + hh
                    u, uB = ub.next()
                    mk.dma(mk.sp, u[:], self.s_qkv[ci], writes=[uB])
                    a, aB = ab.next()
                    w = lambda j: self.cw[:, l, ci, j:j + 1]
                    self.I(V, "tensor_scalar", a[:], u[:], w(3), None, op0=ALU.mult, R=[uB, self.cwB], W=[aB])
                    for d in (1, 2, 3):
                        self.I(V, "scalar_tensor_tensor", a[:, d:T], u[:, 0:T - d], w(3 - d), a[:, d:T], op0=ALU.mult, op1=ALU.add,
                               R=[uB, self.cwB, aB], W=[aB])
                    if kind == 2:
                        self.ACT(dstT[:, hh, :], a[:], AF.Silu, R=[aB], W=[qkvB])
                        continue
                    self.ACT(a[:], a[:], AF.Silu, R=[aB], W=[aB])
                    s_, sB = sqb.next()
                    self.ACT(s_[:], a[:], AF.Square, R=[aB], W=[sB])
                    for tb in range(NTB):
                        sl = slice(tb * TB, (tb + 1) * TB)
                        b = banks.next()
                        self.MM(self.ps[b][:, 0:TB], self.onesB, s_[:, sl], True, True, R=[self.cbB, sB], W=[self.pb[b]], sig=True)
                        r_, rB = rsb.next()
                        self.ACT(r_[:], self.ps[b][:, 0:TB], AF.Sqrt, R=[self.pb[b], self.ccB], W=[rB], bias=self.epsc)
                        self.I(V, "reciprocal", r_[:], r_[:], R=[rB], W=[rB])
                        self.I(V, "scalar_tensor_tensor", dstT[:, hh, sl], a[:, sl], float(128 ** -0.5) if kind == 0 else 1.0, r_[:],
                               op0=ALU.mult, op1=ALU.mult, R=[aB, rB], W=[qkvB])
            self.pop()
            if gstop == 1:
                self.pop()
                continue
            self.mark(f'gdn_g2_{l}_{gp}')
            self.push()
            W4 = HG * 64
            W8 = HG * 128
            S = self.sb([128, HG, 128], F32, "S"); SB_ = Buf("S")
            self.I(V, "memset", S[:], 0.0, W=[SB_])
            f4 = lambda nm, n: Ring([(self.sb([64, HG, 64], F32, nm), Buf(nm)) for _ in range(n)])
            isets = []
            for par_ in range(2):
                isets.append(dict(gU=f4("gU", 1), E=f4("E", 1), Es=f4("Es", 1), at=f4("att", 1), A=f4("A", 3), B=f4("Bm", 3), P=f4("P", 2),
                                  ainv=f4("ainv", 1), attT=f4("attT", 1)))
            ibanks = Ring([6, 7])
            f8 = lambda nm, n, dt=F32: Ring([(self.sb([64, HG, 128], dt, nm), Buf(nm)) for _ in range(n)])
            kd_r, vb_r, t_r, r_r, vn_r, o_r, on_r = f8("kdec", 2), f8("vb", 2), f8("t8", 2), f8("r8", 2), f8("vnew", 2), f8("o8", 2), f8("on", 2)
            ss_r = Ring([(self.sb([64, HG], F32, "ss"), Buf("ss")) for _ in range(2)])
            tS = self.sb([128, HG, 128], F32, "tS"); tSB = Buf("tS")
            fb = Ring([0, 1, 2, 3, 4, 5])
            I64 = self.identF[0:64, 0:64]
            bc4 = lambda ap2: ap2.unsqueeze(1).to_broadcast([64, HG, 64])
            results = {}

            cf_r = Ring([(self.sb([128, 3, HG, 64], F32, "cf"), Buf("cf")) for _ in range(3)])
            cf_cache = {}

            def chunk_f32(n):
                if n not in cf_cache:
                    c_, cB_ = cf_r.next()
                    csl_ = slice(n * 64, (n + 1) * 64)
                    self.ACT(c_[:, 0], qT[:, :, csl_], AF.Copy, R=[qkvB], W=[cB_])
                    self.I(V, "tensor_copy", c_[:, 1], kT[:, :, csl_], R=[qkvB], W=[cB_])
                    self.ACT(c_[:, 2], vT[:, :, csl_], AF.Copy, R=[qkvB], W=[cB_]) if True else self.I(V, "tensor_copy", c_[:, 2], vT[:, :, csl_], R=[qkvB], W=[cB_])
                    cf_cache[n] = (c_[:, 1], c_[:, 0], c_[:, 2], cB_)
                    if getattr(self, "debug", False) and n == 0 and l == 0 and gp == 0:
                        mk.dma(mk.sp, self.dram("dbg_cf", [128, 3, HG, 64], F32, "Internal"), c_[:], reads=[cB_])
                    cf_cache.pop(n - 3, None)
                return cf_cache[n]

            def half_(st_):
                b_ = ibanks.next()
                return self.ps[b_][0:64, 0:W4].rearrange("p (h c) -> p h c", h=HG), self.pb[b_]

            def inverse(n):
                csl = slice(n * 64, (n + 1) * 64)
                st_ = isets[n % 2]
                gU_r, E_r, Es_r, at_r, A_r, B_r, P_r, ainv, attT = (st_[k_] for k_ in ("gU", "E", "Es", "at", "A", "B", "P", "ainv", "attT"))
                half = lambda: half_(st_)
                gU, gUB = gU_r.next()
                self.I(V, "tensor_tensor", gU[:], bc4(self.U64), g_[:, n, hs].unsqueeze(2).to_broadcast([64, HG, 64]), op=ALU.mult, R=[self.cstB, gB], W=[gUB])
                pD, pDB = half()
                for hh in range(HG):
                    self.MM(pD[:, hh, :], gU[:, hh, :], self.SL64, True, True, R=[gUB, self.cstB], W=[pDB], sig=(hh == HG - 1))
                E, EB = E_r.next()
                self.ACT(E[:], pD, AF.Exp, R=[pDB], W=[EB])
                yield
                self.I(V, "tensor_tensor", E[:], E[:], bc4(self.LINC), op=ALU.mult, R=[EB, self.cstB], W=[EB])
                Es, EsB = Es_r.next()
                self.I(V, "tensor_tensor", Es[:], E[:], bc4(self.SL64), op=ALU.mult, R=[EB, self.cstB], W=[EsB])
                pK, pKB = half()
                pQ, pQB = half()
                kf, qf, vf, cfB = chunk_f32(n)
                for hh in range(HG):
                    self.MM(pK[:, hh, :], kf[:, hh, :], kf[:, hh, :], True, True, R=[cfB], W=[pKB], sig=(hh == HG - 1))
                for hh in range(HG):
                    self.MM(pQ[:, hh, :], qf[:, hh, :], kf[:, hh, :], True, True, R=[cfB], W=[pQB], sig=(hh == HG - 1))
                yield
                A, AB = A_r.next()
                self.I(V, "tensor_tensor", A[:], pK, Es[:], op=ALU.mult, R=[pKB, EsB], W=[AB])
                self.I(V, "tensor_tensor", A[:], A[:], nbeta[:, n, hs].unsqueeze(2).to_broadcast([64, HG, 64]), op=ALU.mult, R=[AB, gB], W=[AB])
                at, atB = at_r.next()
                self.I(V, "tensor_tensor", at[:], pQ, E[:], op=ALU.mult, R=[pQB, EB], W=[atB])
                if getattr(self, "debug", False) and n == 0 and l == 0 and gp == 0:
                    dk_ = self.sb([64, 2, HG, 64], F32, "dbgk"); dkB = Buf("dbgk")
                    self.I(V, "tensor_copy", dk_[:, 0], pK, R=[pKB], W=[dkB])
                    self.I(V, "tensor_copy", dk_[:, 1], pQ, R=[pQB], W=[dkB])
                    mk.dma(mk.sp, self.dram("dbg_KQ", [64, 2, HG, 64], F32, "Internal"), dk_[:], reads=[dkB])
                    mk.dma(mk.sp, self.dram("dbg_Es", [64, HG, 64], F32, "Internal"), Es[:], reads=[EsB])
                    mk.dma(mk.sp, self.dram("dbg_E", [64, HG, 64], F32, "Internal"), E[:], reads=[EB])
                    mk.dma(mk.sp, self.dram("dbg_N", [64, HG, 64], F32, "Internal"), A[:], reads=[AB])
                    mk.dma(mk.sp, self.dram("dbg_at", [64, HG, 64], F32, "Internal"), at[:], reads=[atB])
                pT1, pT1B = half()
                pT2, pT2B = half()
                for hh in range(HG):
                    self.TR(pT1[:, hh, :], A[:, hh, :], I64, R=[AB, self.cstB], W=[pT1B], sig=(hh == HG - 1))
                for hh in range(HG):
                    self.TR(pT2[:, hh, :], at[:, hh, :], I64, R=[atB, self.cstB], W=[pT2B], sig=(hh == HG - 1))
                yield
                Bm, BB = B_r.next()
                self.ACT(Bm[:], pT1, AF.Copy, R=[pT1B], W=[BB])
                aT_, aTB_ = attT.next()
                self.ACT(aT_[:], pT2, AF.Copy, R=[pT2B], W=[aTB_])
                P, PB = P_r.next()
                self.I(V, "tensor_tensor", P[:], Bm[:], bc4(I64), op=ALU.add, R=[BB, self.cstB], W=[PB])
                for lev in range(1, 6):
                    pB, pBB = half()
                    pA, pAB = half()
                    for hh in range(HG):
                        self.MM(pB[:, hh, :], A[:, hh, :], Bm[:, hh, :], True, True, R=[AB, BB], W=[pBB], sig=(hh == HG - 1))
                    for hh in range(HG):
                        self.MM(pA[:, hh, :], Bm[:, hh, :], A[:, hh, :], True, True, R=[AB, BB], W=[pAB], sig=(hh == HG - 1))
                    yield
                    A2, A2B = A_r.next()
                    B2, B2B = B_r.next()
                    self.ACT(A2[:], pA, AF.Copy, R=[pAB], W=[A2B])
                    if lev < 5:
                        self.I(V, "tensor_copy", B2[:], pB, R=[pBB], W=[B2B])
                    A, AB, Bm, BB = A2, A2B, B2, B2B
                    pP, pPB = half()
                    for hh in range(HG):
                        self.MM(pP[:, hh, :], A[:, hh, :], P[:, hh, :], True, True, R=[AB, PB], W=[pPB], sig=(hh == HG - 1))
                    yield
                    if lev < 5:
                        P2, P2B = P_r.next()
                    else:
                        P2, P2B = ainv.next()
                    self.I(V, "tensor_tensor", P2[:], pP, P[:], op=ALU.add, R=[pPB, PB], W=[P2B])
                    P, PB = P2, P2B
                results[n] = (P, PB, aT_, aTB_)
                if getattr(self, "debug", False) and n == 0 and l == 0 and gp == 0:
                    mk.dma(mk.sp, self.dram("dbg_ainv", [64, HG, 64], F32, "Internal"), P[:], reads=[PB])
                    mk.dma(mk.sp, self.dram("dbg_attT", [64, HG, 64], F32, "Internal"), aT_[:], reads=[aTB_])
                yield

            def full():
                b = fb.next()
                return self.ps[b], self.pb[b]

            def scan(n):
                csl = slice(n * 64, (n + 1) * 64)
                bc8 = lambda t3: t3[:, n, hs].unsqueeze(2).to_broadcast([64, HG, 128])
                kf, qf, vf, cfB = chunk_f32(n)
                v3 = lambda ap: ap[0:64, 0:W8].rearrange("p (h d) -> p h d", h=HG)
                pXk, pXkB = full()
                pXv, pXvB = full()
                for hh in range(HG):
                    self.TR(pXk[0:64, hh * 128:(hh + 1) * 128], kf[:, hh, :], self.identF, R=[cfB, self.cstB], W=[pXkB], sig=(hh == HG - 1))
                for hh in range(HG):
                    self.TR(pXv[0:64, hh * 128:(hh + 1) * 128], vf[:, hh, :], self.identF, R=[cfB, self.cstB], W=[pXvB], sig=(hh == HG - 1))
                kd, kdB = kd_r.next()
                vb, vbB = vb_r.next()
                self.I(V, "tensor_tensor", kd[:], v3(pXk), bc8(expGLmG), op=ALU.mult, R=[pXkB, gB], W=[kdB])
                self.I(V, "tensor_tensor", vb[:], v3(pXv), bc8(beta), op=ALU.mult, R=[pXvB, gB], W=[vbB])
                pKS, pKSB = full()
                pQS, pQSB = full()
                for hh in range(HG):
                    self.MM(pKS[0:64, hh * 128:(hh + 1) * 128], kf[:, hh, :], S[:, hh, :], True, True, R=[cfB, SB_], W=[pKSB], sig=(hh == HG - 1))
                for hh in range(HG):
                    self.MM(pQS[0:64, hh * 128:(hh + 1) * 128], qf[:, hh, :], S[:, hh, :], True, True, R=[cfB, SB_], W=[pQSB], sig=(hh == HG - 1))
                yield
                t8, t8B = t_r.next()
                r8, r8B = r_r.next()
                self.I(V, "tensor_tensor", t8[:], v3(pKS), bc8(c1), op=ALU.mult, R=[pKSB, gB], W=[t8B])
                self.I(V, "tensor_tensor", r8[:], vb[:], t8[:], op=ALU.subtract, R=[vbB, t8B], W=[r8B])
                while n not in results:
                    yield
                AinvT, AiB, aT_, aTB_ = results.pop(n)
                pVN, pVNB = full()
                for hh in range(HG):
                    self.MM(pVN[0:64, hh * 128:(hh + 1) * 128], AinvT[:, hh, :], r8[:, hh, :], True, True, R=[AiB, r8B], W=[pVNB], sig=(hh == HG - 1))
                yield
                vn, vnB = vn_r.next()
                self.ACT(vn[:], v3(pVN), AF.Copy, R=[pVNB], W=[vnB])
                pS, pSB = full()
                for hh in range(HG):
                    self.MM(pS[:, hh * 128:(hh + 1) * 128], kd[:, hh, :], vn[:, hh, :], True, True, R=[kdB, vnB], W=[pSB], sig=(hh == HG - 1))
                pAV, pAVB = full()
                for hh in range(HG):
                    self.MM(pAV[0:64, hh * 128:(hh + 1) * 128], aT_[:, hh, :], vn[:, hh, :], True, True, R=[aTB_, vnB], W=[pAVB], sig=(hh == HG - 1))
                yield
                self.I(V, "tensor_tensor", tS[:], S[:], expGL[:, n, hs].unsqueeze(2).to_broadcast([128, HG, 128]), op=ALU.mult, R=[SB_, gB], W=[tSB])
                self.I(V, "tensor_tensor", S[:], tS[:], pS[:, 0:W8].rearrange("p (h d) -> p h d", h=HG), op=ALU.add, R=[tSB, pSB], W=[SB_])
                o8, o8B = o_r.next()
                t8b, t8bB = t_r.next()
                self.I(V, "tensor_tensor", t8b[:], v3(pQS), bc8(expG), op=ALU.mult, R=[pQSB, gB], W=[t8bB])
                self.I(V, "tensor_tensor", o8[:], t8b[:], v3(pAV), op=ALU.add, R=[t8bB, pAVB], W=[o8B])
                yield
                self.I(V, "tensor_tensor", t8b[:], o8[:], o8[:], op=ALU.mult, R=[o8B], W=[t8bB])
                ss, ssB = ss_r.next()
                self.I(V, "tensor_reduce", ss[:], t8b[:], axis=AX.X, op=ALU.add, R=[t8bB], W=[ssB])
                self.ACT(ss[:], ss[:], AF.Sqrt, R=[ssB, self.ccB], W=[ssB], bias=self.epsc[0:64, :], scale=1.0 / 128)
                self.I(V, "reciprocal", ss[:], ss[:], R=[ssB], W=[ssB])
                on, onB = on_r.next()
                self.I(V, "tensor_tensor", on[:], o8[:], ss[:].unsqueeze(2).to_broadcast([64, HG, 128]), op=ALU.mult, R=[o8B, ssB], W=[onB])
                pO, pOB = full()
                for hh in range(HG):
                    self.TR(pO[:, hh * 64:(hh + 1) * 64], on[:, hh, :], I64, R=[onB, self.cstB], W=[pOB], sig=(hh == HG - 1))
                yield
                self.I(V, "tensor_scalar", ogT[:, hs, csl], pO[:, 0:W4].rearrange("p (h c) -> p h c", h=HG), self.par[:, l, self.o_gn:self.o_gn + 1], None,
                       op0=ALU.mult, R=[pOB, self.parB], W=[ogB])
                yield

            inv_gen = None
            inv_next = 0
            g2mode = getattr(self, "g2mode", "")
            if g2mode.startswith("inv"):
                lim_ = int(g2mode[3:] or 99)
                for n in range(NCH):
                    for i_, _ in enumerate(inverse(n)):
                        if i_ + 1 >= lim_:
                            break
            for n in range(NCH if not g2mode.startswith("inv") else 0):
                sg_ = scan(n)
                done = False
                while not done:
                    if inv_gen is None and inv_next < NCH and inv_next <= n + 1:
                        inv_gen = inverse(inv_next)
                        inv_next += 1
                    if inv_gen is not None:
                        try:
                            next(inv_gen)
                        except StopIteration:
                            inv_gen = None
                    try:
                        next(sg_)
                    except StopIteration:
                        done = True
            assert inv_gen is None or all(False for _ in inv_gen)
            self.pop()
            self.pop()
        self.pop()
        if gstop <= 2:
            self.pop()
            return
        self.mark(f'gdn_g3_{l}')
        zb = Ring([(self.sb([128, TB], F32, "zb"), Buf("zb")) for _ in range(3)])
        for h in range(GH):
            for tb in range(NTB):
                sl = slice(tb * TB, (tb + 1) * TB)
                z_, zB = zb.next()
                mk.dma(mk.sp, z_[:], self.s_z[h, :, sl], writes=[zB])
                self.ACT(z_[:], z_[:], AF.Silu, R=[zB], W=[zB])
                self.I(V, "tensor_tensor", ogT[:, h, sl], ogT[:, h, sl], z_[:], op=ALU.mult, R=[ogB, zB], W=[ogB])
        if getattr(self, "debug", False):
            dbg = self.dram(f"dbg_og{l}", [GH, 128, T], BF16, "Internal")
            mk.dma(mk.sp, dbg.rearrange("h p t -> p h t"), ogT[:], reads=[ogB])
        mT = self.sb([128, KD, T], BF16, "mT"); mTB = Buf("mT")
        self._pset = 0
        slots = Ring([(self.sb([128, max(GH, KD), 512], BF16, "wog"), Buf("wog")) for _ in range(2)])
        gt = Ring([(self.sb([128, TB], BF16, "gbt"), Buf("gbt")) for _ in range(4)])
        mat = Ring([(self.sb([128, TB], BF16, "mat"), Buf("mat")) for _ in range(4)])
        tm = Ring([(self.sb([128, TB], F32, "tm"), Buf("tm")) for _ in range(4)])

        def evac_b(g, M, tb, psap, psB):
            sl = slice(tb * TB, (tb + 1) * TB)
            g2, g2B = gt.next()
            mk.dma(mk.sp, g2[:], self.s_gate[KD + g, :, sl], writes=[g2B])
            ma, maB = mat.next()
            mk.dma(mk.sp, ma[:], self.s_ma[g, :, sl], writes=[maB])
            t_, tB = tm.next()
            self.I(V, "tensor_tensor", t_[:], psap, g2[:], op=ALU.mult, R=[psB, g2B], W=[tB])
            self.I(V, "tensor_tensor", mT[:, g, sl], t_[:], ma[:], op=ALU.add, R=[tB, maB], W=[mTB])
        self.stream_mm(self.w_o_gdn[l], self.col_blocks(0, cfg.D), ogT, ogB, GH, evac_b, slots)
        xt = Ring([(self.sb([128, TB], F32, "xt"), Buf("xt")) for _ in range(4)])
        gtc = 2 * KD

        def evac_o(g, M, tb, psap, psB):
            sl = slice(tb * TB, (tb + 1) * TB)
            x_, xB = xt.next()
            key = (g, tb)
            xb_ = self.xTB.setdefault(key, Buf(f"xT{key}"))
            mk.dma(mk.sp, x_[:], self.xT[g, :, sl], reads=[xb_], writes=[xB])
            self.I(V, "scalar_tensor_tensor", x_[:], psap, self.mod[:, l, gtc + g:gtc + g + 1], x_[:], op0=ALU.mult, op1=ALU.add,
                   R=[psB, self.modB, xB], W=[xB])
            mk.dma(mk.sp, self.xT[g, :, sl], x_[:], reads=[xB], writes=[xb_])
        self.stream_mm(self.w_o[l], self.col_blocks(0, cfg.D), mT, mTB, KD, evac_o, slots)
        self.pop()

    def phase_ffn(self, l, hT, hB):
        self.mark(f'ffn_{l}')
        cfg, mk = self.cfg, self.mk
        T, TB, NTB, KD, FQ, DFF = cfg.T, cfg.TB, cfg.NTB, cfg.KD, cfg.FQ, cfg.DFF
        V = mk.dve
        act = self.sb([128, FQ, T], BF16, "ffact"); actB = Buf("ffact")
        slots = Ring([(self.sb([128, max(KD, FQ), 512], BF16, "wff"), Buf("wff")) for _ in range(3)])
        sg = {}
        sgr = Ring([(self.sb([128, TB], F32, "sg"), Buf("sg")) for _ in range(2 * NTB + 2)])
        xt = Ring([(self.sb([128, TB], F32, "fxt"), Buf("fxt")) for _ in range(4)])
        gtc = 5 * KD
        self._pset = 0
        for q in range(cfg.NFP):
            blocks = []
            for j0 in range(0, FQ, 2):
                segs, groups = [], []
                off = 0
                for j in range(j0, min(FQ, j0 + 2)):
                    groups += [(off, 128, ("g", j)), (off + 128, 128, ("u", j))]
                    off += 256
                segs = [((q * FQ + j0) * 256, off)]
                blocks.append((segs, groups))

            def evac_gu(gid, M, tb, psap, psB):
                kind, j = gid
                sl = slice(tb * TB, (tb + 1) * TB)
                if kind == "g":
                    s_, sB = sgr.next()
                    self.ACT(s_[:], psap, AF.Silu, R=[psB], W=[sB])
                    sg[(j, tb)] = (s_, sB)
                else:
                    s_, sB = sg.pop((j, tb))
                    self.I(V, "tensor_tensor", act[:, j, sl], psap, s_[:], op=ALU.mult, R=[psB, sB], W=[actB])
            self.stream_mm(self.w_gate_up[l], blocks, hT, hB, KD, evac_gu, slots)

            def evac_d(g, M, tb, psap, psB):
                sl = slice(tb * TB, (tb + 1) * TB)
                x_, xB = xt.next()
                key = (g, tb)
                xb_ = self.xTB.setdefault(key, Buf(f"xT{key}"))
                mk.dma(mk.sp, x_[:], self.xT[g, :, sl], reads=[xb_], writes=[xB])
                self.I(V, "scalar_tensor_tensor", x_[:], psap, self.mod[:, l, gtc + g:gtc + g + 1], x_[:], op0=ALU.mult, op1=ALU.add,
                       R=[psB, self.modB, xB], W=[xB])
                mk.dma(mk.sp, self.xT[g, :, sl], x_[:], reads=[xB], writes=[xb_])
            self.stream_mm(self.w_down[l][q * FQ * 128:(q + 1) * FQ * 128, :], self.col_blocks(0, cfg.D), act, actB, FQ, evac_d, slots)

    def phase_final(self):
        self.mark('phase_final')
        cfg, mk = self.cfg, self.mk
        KD, T = cfg.KD, cfg.T
        V = mk.dve
        NBF = cfg.NB
        self.push()
        xb = Ring([(self.sb([128, KD, NBF], F32, "fx"), Buf("fx")) for _ in range(3)])
        sq = Ring([(self.sb([128, KD, NBF], BF16, "fsq"), Buf("fsq")) for _ in range(2)])
        rs = Ring([(self.sb([128, NBF], F32, "frs"), Buf("frs")) for _ in range(2)])
        banks = Ring([1, 2, 3, 4, 5, 6, 7])
        xTv = self.xT.rearrange("k p t -> p k t")
        gsz = min(4, KD)
        for blk in range(T // NBF):
            sl = slice(blk * NBF, (blk + 1) * NBF)
            x_, xB = xb.next()
            mk.dma(mk.sp, x_[:], xTv[:, :, sl], writes=[xB])
            s_, sB = sq.next()
            self.ACT(s_[:], x_[:], AF.Square, R=[xB], W=[sB])
            for k in range(KD):
                self.MM(self.ps[0][:, 0:NBF], self.onesB, s_[:, k, :], k == 0, k == KD - 1, R=[self.cbB, sB], W=[self.pb[0]], sig=(k == KD - 1))
            r_, rB = rs.next()
            self.ACT(r_[:], self.ps[0][:, 0:NBF], AF.Sqrt, R=[self.pb[0], self.ccB], W=[rB], bias=self.epsc, scale=1.0 / cfg.D)
            self.I(V, "reciprocal", r_[:], r_[:], R=[rB], W=[rB])
            for k in range(KD):
                self.I(V, "scalar_tensor_tensor", x_[:, k, :], x_[:, k, :], self.fn[:, k:k + 1], r_[:], op0=ALU.mult, op1=ALU.mult,
                       R=[xB, self.fnB, rB], W=[xB])
            mk.dma(mk.sp, self.out_d.rearrange("k p t -> p k t")[:, :, sl], x_[:], reads=[xB], writes=[self.outB])
        self.pop()


_PROGRAM_CACHE = {}


def make_in_map(cfg, inp, b):
    L, KD, GH = cfg.L, cfg.KD, cfg.GH
    f = lambda a: np.ascontiguousarray(np.asarray(a))
    colT = lambda a, n: np.asarray(a).reshape(L, n, 128).transpose(2, 0, 1)
    par = np.zeros((128, L, 8 * KD + 16), np.float32)
    par[:, :, 0:KD] = colT(inp["norm_mix"], KD)
    par[:, :, KD:2 * KD] = colT(inp["norm_ffn"], KD)
    par[:, :, 2 * KD:8 * KD] = colT(inp["b_ada"], 6 * KD)
    par[:, :, 8 * KD:8 * KD + cfg.NLq] = colT(inp["q_a_norm"], cfg.NLq)
    par[:, :, 8 * KD + 4:8 * KD + 4 + cfg.NLk] = colT(inp["kv_a_norm"], cfg.NLk)
    par[:, :, 8 * KD + 8:8 * KD + 9] = colT(inp["gdn_norm"], 1)
    cw = np.asarray(inp["conv_w"]).reshape(L, 4, 3 * GH, 128).transpose(3, 0, 2, 1)
    nff = cfg.DFF // 128
    wgu = np.asarray(inp["w_gate_up"]).reshape(L, cfg.D, 2, nff, 128).transpose(0, 1, 3, 2, 4).reshape(L, cfg.D, 2 * cfg.DFF)
    m = {
        "x": f(np.asarray(inp["x"][b]).T.reshape(KD, 128, cfg.T)),
        "c": f(np.asarray(inp["c"][b]).reshape(KD, 128).T),
        "pos": f(inp["positions"][b]).reshape(1, cfg.T).astype(np.int32),
        "consts": make_consts(),
        "w_ada": f(inp["w_ada"]),
        "par": f(par),
        "cw": f(cw),
        "w_in": f(inp["w_in"]),
        "w_uq": f(inp["w_uq"]),
        "w_ukv": f(inp["w_ukv"]),
        "w_o_mla": f(inp["w_o_mla"]),
        "A_log": f(inp["A_log"]).reshape(L, 1, GH),
        "dt_bias": f(inp["dt_bias"]).reshape(L, 1, GH),
        "w_o_gdn": f(inp["w_o_gdn"]),
        "w_o": f(inp["w_o"]),
        "w_gate_up": f(wgu),
        "w_down": f(inp["w_down"]),
        "final_norm": f(np.asarray(inp["final_norm"]).reshape(KD, 128).T),
    }
    return m


def kernel(**inputs):
    cfg = Cfg()
    B = inputs["x"].shape[0]
    nc = Builder(cfg).build()
    shared = None
    in_maps = []
    for b in range(B):
        m = make_in_map(cfg, inputs, b) if shared is None else dict(shared)
        if shared is None:
            shared = m
        else:
            m["x"] = np.ascontiguousarray(np.asarray(inputs["x"][b]).T.reshape(cfg.KD, 128, cfg.T))
            m["c"] = np.ascontiguousarray(np.asarray(inputs["c"][b]).reshape(cfg.KD, 128).T)
            m["pos"] = np.ascontiguousarray(inputs["positions"][b]).reshape(1, cfg.T).astype(np.int32)
        in_maps.append(m)
    res = run_bass_kernel_spmd(nc, in_maps, core_ids=list(range(B)))
    out = np.stack([np.asarray(r["out"]).reshape(cfg.D, cfg.T).T for r in res.results], axis=0)
    return np.ascontiguousarray(out.astype(np.float32))
```

```python
import numpy as np
from contextlib import ExitStack
import concourse.bass as bass
import concourse.mybir as mybir
from concourse.bass_utils import run_bass_kernel_spmd

F32 = mybir.dt.float32
BF16 = mybir.dt.bfloat16
I32 = mybir.dt.int32
AF = mybir.ActivationFunctionType
ALU = mybir.AluOpType
AX = mybir.AxisListType


class Cfg:
    def __init__(self, D=2048, T=2048, L=2, H=8, QL=512, KVL=512, GH=8, DFF=5632):
        self.D, self.T, self.L, self.H, self.QL, self.KVL, self.GH, self.DFF = D, T, L, H, QL, KVL, GH, DFF
        self.KD = D // 128
        self.TB = min(512, T)
        self.NTB = T // self.TB
        self.NT128 = T // 128
        self.NCH = T // 64
        self.NB = min(256, T)
        self.NLq, self.NLk = QL // 128, KVL // 128
        self.o_cq = 0
        self.o_ckv = QL
        self.o_kpe = QL + KVL
        self.o_q = self.o_kpe + 64
        self.o_k = self.o_q + GH * 128
        self.o_v = self.o_k + GH * 128
        self.o_z = self.o_v + GH * 128
        self.o_b = self.o_z + GH * 128
        self.o_a = self.o_b + GH
        self.o_g = self.o_a + GH
        self.INW = self.o_g + 2 * D
        self.EPS = 1e-6
        self.HG = min(4, GH)
        nff = DFF // 128
        self.FQ = max(d for d in range(1, 12) if nff % d == 0)
        self.NFP = nff // self.FQ


class Ev:
    __slots__ = ("sem", "val")

    def __init__(self, sem, val=None):
        self.sem = sem
        self.val = val


class Buf:
    __slots__ = ("name", "w", "r")

    def __init__(self, name="b"):
        self.name = name
        self.w = None
        self.r = {}


class Eng:
    def __init__(self, name, eng, sem):
        self.name, self.eng, self.sem = name, eng, sem
        self.cnt = 0
        self.seen = {}
        self.pending = []
        self.nwait = 0
        self.nins = 0


class MK:
    def __init__(self, nc, stack, n_dma_sems=24):
        self.nc = nc
        ec = stack.enter_context
        self.pe = Eng("pe", nc.tensor, ec(nc.semaphore("s_pe")))
        self.act = Eng("act", nc.scalar, ec(nc.semaphore("s_act")))
        self.dve = Eng("dve", nc.vector, ec(nc.semaphore("s_dve")))
        self.pool = Eng("pool", nc.gpsimd, ec(nc.semaphore("s_pool")))
        self.sp = Eng("sp", nc.sync, ec(nc.semaphore("s_sp")))
        self.engs = [self.pe, self.act, self.dve, self.pool, self.sp]
        self.dsems = {}
        for q in (self.sp, self.pool):
            self.dsems[q.name] = [[ec(nc.semaphore(f"d_{q.name}{i}")), 0, None] for i in range(n_dma_sems)]
        self.drr = {"sp": 0, "pool": 0}
        self.bar_log = []
        self.bar_names = None

    def _wait(self, E, evs):
        best = {}
        for ev in evs:
            if ev is None:
                continue
            if ev.val is None:
                assert ev.sem is E.sem and E is self.pe, f"unresolved event waited by {E.name}"
                continue
            k = id(ev.sem)
            if k not in best or best[k].val < ev.val:
                best[k] = ev
        for k, ev in best.items():
            if E.seen.get(k, 0) < ev.val:
                E.eng.wait_ge(ev.sem, ev.val)
                E.seen[k] = ev.val
                E.nwait += 1

    @staticmethod
    def _deps(reads, writes):
        need = []
        for b in reads:
            need.append(b.w)
        for b in writes:
            need.append(b.w)
            need.extend(b.r.values())
        return need

    @staticmethod
    def _commit(ev, reads, writes):
        for b in reads:
            b.r[id(ev.sem)] = ev
        for b in writes:
            b.w = ev
            b.r = {}

    def op(self, E, fn, reads=(), writes=(), signal=True):
        self._wait(E, self._deps(reads, writes))
        ins = fn(E.eng)
        E.nins += 1
        ev = Ev(E.sem)
        E.pending.append(ev)
        if signal:
            E.cnt += 1
            ins.then_inc(E.sem, 1)
            for p in E.pending:
                p.val = E.cnt
            E.pending = []
        self._commit(ev, reads, writes)
        return ev

    def dma(self, Q, out_ap, in_ap, reads=(), writes=()):
        pool = self.dsems[Q.name]
        i = self.drr[Q.name]
        self.drr[Q.name] = (i + 1) % len(pool)
        slot = pool[i]
        need = self._deps(reads, writes)
        need.append(slot[2])
        self._wait(Q, need)
        slot[1] += 16
        ev = Ev(slot[0], slot[1])
        Q.eng.dma_start(out=out_ap, in_=in_ap).then_inc(slot[0], 16)
        Q.nins += 1
        slot[2] = ev
        self._commit(ev, reads, writes)
        return ev

    def barrier(self):
        self.bar_log.append(getattr(self, 'cur_phase', '?'))
        for Q in (self.sp, self.pool):
            self._wait(Q, [s[2] for s in self.dsems[Q.name]])
            assert not Q.pending
            Q.cnt += 1
            Q.eng.sem_inc(Q.sem, 1)
        assert not self.pe.pending
        evs = [Ev(E.sem, E.cnt) for E in self.engs if E.cnt > 0]
        for E in self.engs:
            self._wait(E, evs)


class Ring:
    def __init__(self, items):
        self.items = items
        self.i = 0

    def next(self):
        it = self.items[self.i]
        self.i = (self.i + 1) % len(self.items)
        return it


def make_consts():
    c = np.zeros((128, 640), np.float32)
    c[:, 0:128] = np.eye(128)
    k = np.arange(64)
    c[0:64, 128:192] = (k[:, None] <= k[None, :])
    c[0:64, 192:256] = (k[:, None] > k[None, :])
    c[0:64, 256:320] = (k[:, None] >= k[None, :])
    c[:, 320:448] = 1.0
    p = np.arange(128)
    c[:, 448:576] = (p[:, None] <= p[None, :])
    inv = (1.0 / (10000.0 ** (np.arange(0, 64, 2, dtype=np.float32) / 64))).astype(np.float32)
    c[0:64, 576] = np.concatenate([inv, inv])
    c[0:32, 577] = -1.0
    c[32:64, 577] = 1.0
    return c


class Builder:
    def __init__(self, cfg, nlayers=None, debug_dump=False):
        self.cfg = cfg
        self.nl = cfg.L if nlayers is None else nlayers
        self.nc = bass.Bass("TRN2", target_bir_lowering=False)
        self._uid = 0

    def I(self, E, name, *a, R=(), W=(), sig=True, **kw):
        return self.mk.op(E, lambda e: getattr(e, name)(*a, **kw), reads=R, writes=W, signal=sig)

    def MM(self, out, lhsT, rhs, start, stop, R, W, sig):
        return self.mk.op(self.mk.pe, lambda e: e.matmul(out, lhsT=lhsT, rhs=rhs, start=start, stop=stop),
                          reads=R, writes=W, signal=sig)

    def TR(self, out, in_, ident, R, W, sig=True):
        return self.mk.op(self.mk.pe, lambda e: e.transpose(out, in_, ident), reads=R, writes=W, signal=sig)

    def ACT(self, out, in_, func, R, W, bias=None, scale=1.0):
        if bias is None:
            return self.mk.op(self.mk.act, lambda e: e.activation(out, in_, func, scale=scale), reads=R, writes=W)
        return self.mk.op(self.mk.act, lambda e: e.activation(out, in_, func, bias=bias, scale=scale), reads=R, writes=W)

    def sb(self, shape, dtype, name=None):
        self._uid += 1
        t = self.scopes[-1].enter_context(self.nc.sbuf_tensor(f"{name or 't'}_{self._uid}", list(shape), dtype))
        return t

    def push(self):
        st = ExitStack()
        st.__enter__()
        self.scopes.append(st)

    def pop(self):
        self.mk.barrier()
        st = self.scopes.pop()
        st.__exit__(None, None, None)

    def dram(self, name, shape, dtype, kind):
        return self.nc.dram_tensor(name, list(shape), dtype, kind=kind).ap()

    def mark(self, name):
        self.mk.cur_phase = name
        self.marks.append((name, self.mk.pe.nins + self.mk.pe.nwait))

    def build(self):
        cfg, nc = self.cfg, self.nc
        self.marks = []
        D, T, L, KD = cfg.D, cfg.T, cfg.L, cfg.KD
        dr = self.dram
        self.x_d = dr("x", [KD, 128, T], F32, "ExternalInput")
        self.c_d = dr("c", [128, KD], F32, "ExternalInput")
        self.pos_d = dr("pos", [1, T], I32, "ExternalInput")
        self.cst_d = dr("consts", [128, 640], F32, "ExternalInput")
        self.w_ada = dr("w_ada", [L, D, 6 * D], F32, "ExternalInput")
        self.par_d = dr("par", [128, L, 8 * KD + 16], F32, "ExternalInput")
        self.cw_d = dr("cw", [128, L, 3 * cfg.GH, 4], F32, "ExternalInput")
        self.w_in = dr("w_in", [L, D, cfg.INW], F32, "ExternalInput")
        self.w_uq = dr("w_uq", [L, cfg.QL, cfg.H * 192], F32, "ExternalInput")
        self.w_ukv = dr("w_ukv", [L, cfg.KVL, cfg.H * 256], F32, "ExternalInput")
        self.w_o_mla = dr("w_o_mla", [L, cfg.H * 128, D], F32, "ExternalInput")
        self.A_log = dr("A_log", [L, 1, cfg.GH], F32, "ExternalInput")
        self.dt_bias = dr("dt_bias", [L, 1, cfg.GH], F32, "ExternalInput")
        self.w_o_gdn = dr("w_o_gdn", [L, cfg.GH * 128, D], F32, "ExternalInput")
        self.w_o = dr("w_o", [L, D, D], F32, "ExternalInput")
        self.w_gate_up = dr("w_gate_up", [L, D, 2 * cfg.DFF], F32, "ExternalInput")
        self.w_down = dr("w_down", [L, cfg.DFF, D], F32, "ExternalInput")
        self.final_norm = dr("final_norm", [128, KD], F32, "ExternalInput")
        self.out_d = dr("out", [KD, 128, T], F32, "ExternalOutput")
        self.xT = dr("s_xT", [KD, 128, T], F32, "Internal")
        self.s_lat = dr("s_lat", [cfg.NLq + cfg.NLk, 128, T], F32, "Internal")
        self.s_kpe = dr("s_kpe", [2, 64, T], F32, "Internal")
        self.s_qkv = dr("s_qkv", [3 * cfg.GH, 128, T], F32, "Internal")
        self.s_z = dr("s_z", [cfg.GH, 128, T], F32, "Internal")
        self.s_ba = dr("s_ba", [2 * cfg.GH, T], F32, "Internal")
        self.s_gate = dr("s_gate", [2 * KD, 128, T], BF16, "Internal")
        self.s_ma = dr("s_ma", [KD, 128, T], BF16, "Internal")
        self.xTB = {}
        self.outB = Buf("out")

        with ExitStack() as top:
            self.mk = MK(nc, top)
            self.scopes = [top]
            mk = self.mk
            self.ps = [top.enter_context(nc.psum_tensor(f"ps{i}", [128, 512], F32)) for i in range(8)]
            self.pb = [Buf(f"ps{i}") for i in range(8)]
            stop = getattr(self, "stop_after", None)
            order = ["globals", "x0", "ada", "rope", "norm0", "proj", "mla", "gdn", "norm1", "ffn", "final"]
            lim = order.index(stop) if stop else len(order)
            on = lambda nm: order.index(nm) <= lim
            self.setup_globals()
            if on("x0"):
                self.phase_x0()
            if on("ada"):
                self.phase_ada()
            if on("rope"):
                self.phase_rope()
            for l in range(self.nl):
                if on("norm0"):
                    self.push()
                    hT = self.sb([128, KD, T], BF16, "hT")
                    hB = Buf("hT")
                    self.phase_norm(l, 0, hT, hB)
                    if on("proj"):
                        self.phase_proj(l, hT, hB)
                    self.pop()
                if on("mla"):
                    self.phase_mla(l)
                if on("gdn"):
                    self.phase_gdn(l)
                if on("norm1"):
                    self.push()
                    hT = self.sb([128, KD, T], BF16, "hT2")
                    hB = Buf("hT2")
                    self.phase_norm(l, 1, hT, hB)
                    if on("ffn"):
                        self.phase_ffn(l, hT, hB)
                    self.pop()
            if on("final"):
                self.phase_final()
            mk._wait(mk.sp, [self.outB.w] + list(self.outB.r.values()))
            mk.barrier()
            self.mark('end')
            self.stats = {e.name: (e.nins, e.nwait) for e in mk.engs}
        return nc

    def setup_globals(self):
        self.mark('setup_globals')
        cfg, mk = self.cfg, self.mk
        KD, L = cfg.KD, cfg.L
        self.cst = self.sb([128, 640], F32, "cst")
        self.cstB = Buf("cst")
        mk.dma(mk.sp, self.cst[:], self.cst_d, writes=[self.cstB])
        c = self.cst
        self.identF = c[:, 0:128]
        self.U64 = c[0:64, 128:192]
        self.SL64 = c[0:64, 192:256]
        self.LINC = c[0:64, 256:320]
        self.onesF = c[:, 320:448]
        self.invf = c[0:64, 576:577]
        self.sgn = c[0:64, 577:578]
        self.cb = self.sb([128, 384], BF16, "cb")
        self.cbB = Buf("cb")
        self.identB = self.cb[:, 0:128]
        self.onesB = self.cb[:, 128:256]
        self.causB = self.cb[:, 256:384]
        self.I(mk.dve, "tensor_copy", self.cb[:, 0:128], c[:, 0:128], R=[self.cstB], W=[self.cbB])
        self.I(mk.dve, "tensor_copy", self.cb[:, 128:256], c[:, 320:448], R=[self.cstB], W=[self.cbB])
        self.I(mk.dve, "tensor_copy", self.cb[:, 256:384], c[:, 448:576], R=[self.cstB], W=[self.cbB])
        self.cc = self.sb([128, 4], F32, "cc")
        self.ccB = Buf("cc")
        self.I(mk.dve, "memset", self.cc[:, 0:1], cfg.EPS, W=[self.ccB])
        self.I(mk.dve, "memset", self.cc[:, 1:2], 1.0, W=[self.ccB])
        self.I(mk.dve, "memset", self.cc[:, 2:3], 0.0, W=[self.ccB])
        self.epsc = self.cc[:, 0:1]
        self.onec = self.cc[:, 1:2]
        self.par = self.sb([128, L, 8 * KD + 16], F32, "par")
        self.parB = Buf("par")
        self.mod = self.sb([128, L, 8 * KD], F32, "mod")
        self.modB = Buf("mod")
        self.cw = self.sb([128, L, 3 * cfg.GH, 4], F32, "cw")
        self.cwB = Buf("cw")
        self.fn = self.sb([128, KD], F32, "fn")
        self.fnB = Buf("fn")
        self.cact = self.sb([128, KD], BF16, "cact")
        self.cactB = Buf("cact")
        self.grow = self.sb([64, L, 2, cfg.GH], F32, "grow")
        self.growB = Buf("grow")
        mk.dma(mk.sp, self.par[:], self.par_d, writes=[self.parB])
        mk.dma(mk.sp, self.cw[:], self.cw_d, writes=[self.cwB])
        mk.dma(mk.sp, self.fn[:], self.final_norm, writes=[self.fnB])
        for l in range(L):
            mk.dma(mk.sp, self.grow[:, l, 0, :], self.A_log[l].partition_broadcast(64), writes=[self.growB])
            mk.dma(mk.sp, self.grow[:, l, 1, :], self.dt_bias[l].partition_broadcast(64), writes=[self.growB])
            self.ACT(self.grow[:, l, 0, :], self.grow[:, l, 0, :], AF.Exp, R=[self.growB], W=[self.growB])
            self.I(mk.dve, "tensor_scalar", self.grow[:, l, 0, :], self.grow[:, l, 0, :], -1.0, None, op0=ALU.mult,
                   R=[self.growB], W=[self.growB])
        self.push()
        ctmp = self.sb([128, KD], F32, "ctmp")
        ctB = Buf("ctmp")
        mk.dma(mk.sp, ctmp[:], self.c_d, writes=[ctB])
        self.ACT(self.cact[:, :], ctmp[:, :], AF.Silu, R=[ctB], W=[self.cactB])
        self.pop()
        self.o_nmix, self.o_nffn, self.o_bada, self.o_qan, self.o_kvan, self.o_gn = 0, KD, 2 * KD, 8 * KD, 8 * KD + 4, 8 * KD + 8

    def phase_x0(self):
        self.mark('phase_x0')
        cfg, mk = self.cfg, self.mk
        for k in range(cfg.KD):
            mk.dma(mk.sp, self.xT[k], self.x_d[k])
        mk.barrier()

    def phase_ada(self):
        self.mark('phase_ada')
        cfg, mk = self.cfg, self.mk
        KD = cfg.KD
        self.push()
        slots = Ring([(self.sb([128, KD, 512], BF16, "wada"), Buf("wada")) for _ in range(3)])
        psA, psAB = self.ps[0], self.pb[0]
        for l in range(self.nl):
            nblk = 6 * cfg.D // 512
            for jb in range(nblk):
                sl, slB = slots.next()
                mk.dma(mk.pool, sl[:], self.w_ada[l][:, jb * 512:(jb + 1) * 512].rearrange("(k p) n -> p k n", p=128), writes=[slB])
                for jj in range(4):
                    j = jb * 4 + jj
                    for k in range(KD):
                        self.MM(psA[:, j:j + 1], sl[:, k, jj * 128:(jj + 1) * 128], self.cact[:, k:k + 1], k == 0, k == KD - 1,
                                R=[slB, self.cactB], W=[psAB], sig=(k == KD - 1))
            m = self.mod
            self.I(mk.dve, "tensor_tensor", m[:, l, 0:6 * KD], psA[:, 0:6 * KD], self.par[:, l, self.o_bada:self.o_bada + 6 * KD],
                   op=ALU.add, R=[psAB, self.parB], W=[self.modB])
            self.I(mk.dve, "scalar_tensor_tensor", m[:, l, 6 * KD:7 * KD], m[:, l, KD:2 * KD], 1.0, self.par[:, l, 0:KD],
                   op0=ALU.add, op1=ALU.mult, R=[self.modB, self.parB], W=[self.modB])
            self.I(mk.dve, "scalar_tensor_tensor", m[:, l, 7 * KD:8 * KD], m[:, l, 4 * KD:5 * KD], 1.0, self.par[:, l, KD:2 * KD],
                   op0=ALU.add, op1=ALU.mult, R=[self.modB, self.parB], W=[self.modB])
        self.pop()

    def phase_rope(self):
        self.mark('phase_rope')
        cfg, mk = self.cfg, self.mk
        T = cfg.T
        self.COS = self.sb([64, T], F32, "cos")
        self.SINS = self.sb([64, T], F32, "sins")
        self.ropeB = Buf("rope")
        self.push()
        pi_ = self.sb([64, T], I32, "posi")
        t0 = self.sb([64, T], F32, "t0")
        t1 = self.sb([64, T], F32, "t1")
        t2 = self.sb([64, T], F32, "t2")
        B = Buf("ropetmp")
        mk.dma(mk.sp, pi_[:], self.pos_d.partition_broadcast(64), writes=[B])
        V = lambda name, *a, **kw: self.I(mk.dve, name, *a, R=[B, self.cstB], W=[B], **kw)
        V("tensor_copy", t0[:], pi_[:])
        V("tensor_scalar", t0[:], t0[:], self.invf, None, op0=ALU.mult)
        V("tensor_scalar", t0[:], t0[:], float(1.0 / (2 * np.pi)), None, op0=ALU.mult)
        for which in range(2):
            dst = self.SINS if which == 0 else self.COS
            if which == 1:
                V("tensor_scalar", t0[:], t0[:], 0.25, None, op0=ALU.add)
            V("tensor_copy", pi_[:], t0[:])
            V("tensor_copy", t1[:], pi_[:])
            V("tensor_tensor", t1[:], t0[:], t1[:], op=ALU.subtract)
            V("tensor_scalar", t2[:], t1[:], 0.5, None, op0=ALU.is_ge)
            V("tensor_tensor", t1[:], t1[:], t2[:], op=ALU.subtract)
            V("tensor_scalar", t2[:], t1[:], -0.5, None, op0=ALU.is_lt)
            V("tensor_tensor", t1[:], t1[:], t2[:], op=ALU.add)
            self.ACT(dst[:], t1[:], AF.Sin, R=[B], W=[self.ropeB], scale=float(2 * np.pi))
        self.I(mk.dve, "tensor_scalar", self.SINS[:], self.SINS[:], self.sgn, None, op0=ALU.mult, R=[self.ropeB, self.cstB], W=[self.ropeB])
        self.pop()

    def phase_norm(self, l, which, hT, hB):
        self.mark(f'norm{which}_{l}')
        cfg, mk = self.cfg, self.mk
        KD, NB = cfg.KD, cfg.NB
        a_off = (6 * KD) if which == 0 else (7 * KD)
        sh_off = 0 if which == 0 else 3 * KD
        self.push()
        xb = Ring([(self.sb([128, KD, NB], F32, "xb"), Buf("xb")) for _ in range(2)])
        sq = Ring([(self.sb([128, KD, NB], BF16, "sq"), Buf("sq")) for _ in range(2)])
        rs = Ring([(self.sb([128, NB], F32, "rs"), Buf("rs")) for _ in range(2)])
        tmp = Ring([(self.sb([128, NB], F32, "tmp"), Buf("tmp")) for _ in range(4)])
        banks = Ring([0, 1])
        xTv = self.xT.rearrange("k p t -> p k t")
        for blk in range(cfg.T // NB):
            sl = slice(blk * NB, (blk + 1) * NB)
            x_, xB = xb.next()
            rd = [self.xTB[key] for key in self.xTB]
            mk.dma(mk.sp, x_[:], xTv[:, :, sl], reads=rd, writes=[xB])
            s_, sB = sq.next()
            self.ACT(s_[:], x_[:], AF.Square, R=[xB], W=[sB])
            b = banks.next()
            for k in range(KD):
                self.MM(self.ps[b][:, 0:NB], self.onesB, s_[:, k, :], k == 0, k == KD - 1, R=[self.cbB, sB], W=[self.pb[b]], sig=(k == KD - 1))
            r_, rB = rs.next()
            self.ACT(r_[:], self.ps[b][:, 0:NB], AF.Sqrt, R=[self.pb[b], self.ccB], W=[rB], bias=self.epsc, scale=1.0 / cfg.D)
            self.I(mk.dve, "reciprocal", r_[:], r_[:], R=[rB], W=[rB])
            for k in range(KD):
                t_, tB = tmp.next()
                self.I(mk.dve, "scalar_tensor_tensor", t_[:], x_[:, k, :], self.mod[:, l, a_off + k:a_off + k + 1], r_[:],
                       op0=ALU.mult, op1=ALU.mult, R=[xB, self.modB, rB], W=[tB])
                self.ACT(hT[:, k, sl], t_[:], AF.Identity, R=[tB, self.modB], W=[hB], bias=self.mod[:, l, sh_off + k:sh_off + k + 1])
        self.pop()

    def stream_mm(self, W2d, blocks, rhs, rhsB, KC, evac, slots, extraR=(), pre=None):
        cfg, mk = self.cfg, self.mk
        TB, NTB = cfg.TB, cfg.NTB
        nsets = 8 // NTB
        Wv = W2d.rearrange("(k p) n -> p k n", p=128)
        flat = [g[2] for _, groups in blocks for g in groups]
        gi = 0
        if pre is not None and flat:
            pre(flat[0])
        for segs, groups in blocks:
            sl, slB = slots.next()
            off = 0
            for (c0, n) in segs:
                mk.dma(mk.pool, sl[:, 0:KC, off:off + n], Wv[:, :, c0:c0 + n], writes=[slB])
                off += n
            for (goff, M, gid) in groups:
                gi += 1
                if pre is not None and gi < len(flat):
                    pre(flat[gi])
                s = self._pset
                self._pset = (self._pset + 1) % nsets
                for k in range(KC):
                    for tb in range(NTB):
                        b = s * NTB + tb
                        self.MM(self.ps[b][0:M, 0:TB], sl[:, k, goff:goff + M], rhs[:, k, tb * TB:(tb + 1) * TB], k == 0, k == KC - 1,
                                R=[slB, rhsB] + list(extraR), W=[self.pb[b]], sig=(k == KC - 1))
                for tb in range(NTB):
                    b = s * NTB + tb
                    evac(gid, M, tb, self.ps[b][0:M, 0:TB], self.pb[b])

    def col_blocks(self, c_lo, ncols, gid0=0):
        blocks = []
        g = gid0
        for c0 in range(c_lo, c_lo + ncols, 512):
            n = min(512, c_lo + ncols - c0)
            groups = []
            for o in range(0, n, 128):
                groups.append((o, min(128, n - o), g))
                g += 1
            blocks.append(([(c0, n)], groups))
        return blocks

    def phase_proj(self, l, hT, hB):
        self.mark(f'proj_{l}')
        cfg, mk = self.cfg, self.mk
        KD, TB = cfg.KD, cfg.TB
        self._pset = 0
        slots = Ring([(self.sb([128, KD, 512], BF16, "win"), Buf("win")) for _ in range(3)])
        stg = Ring([(self.sb([128, TB], F32, "pstg"), Buf("pstg")) for _ in range(6)])
        stgb = Ring([(self.sb([128, TB], BF16, "pstgb"), Buf("pstgb")) for _ in range(4)])
        flip = [0]

        def mk_evac(dst_fn, sigm=False):
            def evac(gid, M, tb, psap, psB):
                if sigm:
                    st, stB = stgb.next()
                    self.ACT(st[0:M, :], psap, AF.Sigmoid, R=[psB], W=[stB])
                else:
                    st, stB = stg.next()
                    flip[0] ^= 1
                    if flip[0]:
                        self.ACT(st[0:M, :], psap, AF.Copy, R=[psB], W=[stB])
                    else:
                        self.I(mk.dve, "tensor_copy", st[0:M, :], psap, R=[psB], W=[stB])
                mk.dma(mk.sp, dst_fn(gid, M, tb), st[0:M, :], reads=[stB])
            return evac
        W = self.w_in[l]
        tsl = lambda tb: slice(tb * TB, (tb + 1) * TB)
        self.stream_mm(W, self.col_blocks(0, cfg.QL + cfg.KVL), hT, hB, KD,
                       mk_evac(lambda g, M, tb: self.s_lat[g, :, tsl(tb)]), slots)
        o = cfg.o_kpe
        blocks = [([(o, 64), (o + 32, 32), (o, 32)], [(0, 64, 0), (64, 64, 1)])]
        self.stream_mm(W, blocks, hT, hB, KD, mk_evac(lambda g, M, tb: self.s_kpe[g, :, tsl(tb)]), slots)
        self.stream_mm(W, self.col_blocks(cfg.o_q, 3 * cfg.GH * 128), hT, hB, KD,
                       mk_evac(lambda g, M, tb: self.s_qkv[g, :, tsl(tb)]), slots)
        self.stream_mm(W, self.col_blocks(cfg.o_z, cfg.GH * 128), hT, hB, KD,
                       mk_evac(lambda g, M, tb: self.s_z[g, :, tsl(tb)]), slots)
        blocks = [([(cfg.o_b, 2 * cfg.GH)], [(0, 2 * cfg.GH, 0)])]
        self.stream_mm(W, blocks, hT, hB, KD, mk_evac(lambda g, M, tb: self.s_ba[:, tsl(tb)]), slots)
        self.stream_mm(W, self.col_blocks(cfg.o_g, 2 * cfg.D), hT, hB, KD,
                       mk_evac(lambda g, M, tb: self.s_gate[g, :, tsl(tb)], sigm=True), slots)

    def phase_mla(self, l):
        self.mark(f'mla_{l}')
        cfg, mk = self.cfg, self.mk
        T, TB, NTB, H = cfg.T, cfg.TB, cfg.NTB, cfg.H
        NLq, NLk = cfg.NLq, cfg.NLk
        NL = NLq + NLk
        self.push()
        cqn = self.sb([128, NLq, T], BF16, "cqn"); cqnB = Buf("cqn")
        ckvn = self.sb([128, NLk, T], BF16, "ckvn"); ckvnB = Buf("ckvn")
        kpeR = self.sb([64, T], BF16, "kpeR"); kpeB = Buf("kpeR")
        aT = self.sb([128, H, T], BF16, "aT"); aTB = Buf("aT")
        self.push()
        lat = Ring([(self.sb([128, NL, TB], F32, "lat"), Buf("lat")) for _ in range(2)])
        sq = Ring([(self.sb([128, NL, TB], BF16, "lsq"), Buf("lsq")) for _ in range(1)])
        rq = Ring([(self.sb([128, 2, TB], F32, "lrs"), Buf("lrs")) for _ in range(2)])
        latv = self.s_lat.rearrange("k p t -> p k t")
        for tb in range(NTB):
            sl = slice(tb * TB, (tb + 1) * TB)
            la, laB = lat.next()
            mk.dma(mk.sp, la[:], latv[:, :, sl], writes=[laB])
            s_, sB = sq.next()
            self.ACT(s_[:], la[:], AF.Square, R=[laB], W=[sB])
            r_, rB = rq.next()
            for which, (k0, nk, dim) in enumerate([(0, NLq, cfg.QL), (NLq, NLk, cfg.KVL)]):
                b = which
                for k in range(nk):
                    self.MM(self.ps[b][:, 0:TB], self.onesB, s_[:, k0 + k, :], k == 0, k == nk - 1, R=[self.cbB, sB], W=[self.pb[b]], sig=(k == nk - 1))
                self.ACT(r_[:, which, :], self.ps[b][:, 0:TB], AF.Sqrt, R=[self.pb[b], self.ccB], W=[rB], bias=self.epsc, scale=1.0 / dim)
            self.I(mk.dve, "reciprocal", r_[:], r_[:], R=[rB], W=[rB])
            for k in range(NLq):
                self.I(mk.dve, "scalar_tensor_tensor", cqn[:, k, sl], la[:, k, :], self.par[:, l, self.o_qan + k:self.o_qan + k + 1], r_[:, 0, :],
                       op0=ALU.mult, op1=ALU.mult, R=[laB, self.parB, rB], W=[cqnB])
            for k in range(NLk):
                self.I(mk.dve, "scalar_tensor_tensor", ckvn[:, k, sl], la[:, NLq + k, :], self.par[:, l, self.o_kvan + k:self.o_kvan + k + 1], r_[:, 1, :],
                       op0=ALU.mult, op1=ALU.mult, R=[laB, self.parB, rB], W=[ckvnB])
        kp = self.sb([64, 2, T], F32, "kp"); kpB = Buf("kp")
        mk.dma(mk.sp, kp[:], self.s_kpe.rearrange("g p t -> p g t"), writes=[kpB])
        self.I(mk.dve, "tensor_tensor", kp[:, 0, :], kp[:, 0, :], self.COS[:], op=ALU.mult, R=[kpB, self.ropeB], W=[kpB])
        self.I(mk.dve, "tensor_tensor", kp[:, 1, :], kp[:, 1, :], self.SINS[:], op=ALU.mult, R=[kpB, self.ropeB], W=[kpB])
        self.I(mk.dve, "tensor_tensor", kpeR[:], kp[:, 0, :], kp[:, 1, :], op=ALU.add, R=[kpB], W=[kpeB])
        self.pop()
        wq = self.sb([128, NLq, H * 192], BF16, "wq"); wqB = Buf("wq")
        wqs = self.sb([128, NLq, H, 64], BF16, "wqs"); wqsB = Buf("wqs")
        wkv = self.sb([128, NLk, H * 256], BF16, "wkv"); wkvB = Buf("wkv")
        mk.dma(mk.pool, wq[:], self.w_uq[l].rearrange("(k p) n -> p k n", p=128), writes=[wqB])
        wq4 = self.w_uq[l].rearrange("(k p) (h e) -> p k h e", p=128, e=192)
        for k in range(NLq):
            mk.dma(mk.pool, wqs[:, k, :, 0:32], wq4[:, k, :, 160:192], writes=[wqsB])
            mk.dma(mk.pool, wqs[:, k, :, 32:64], wq4[:, k, :, 128:160], writes=[wqsB])
        mk.dma(mk.pool, wkv[:], self.w_ukv[l].rearrange("(k p) n -> p k n", p=128), writes=[wkvB])
        self.mark(f'mla_attn_{l}')
        knT = Ring([(self.sb([128, T], BF16, "knT"), Buf("knT")) for _ in range(2)])
        qnT = Ring([(self.sb([128, T], BF16, "qnT"), Buf("qnT")) for _ in range(2)])
        qpR = Ring([(self.sb([64, T], BF16, "qpR"), Buf("qpR")) for _ in range(2)])
        vtk = Ring([(self.sb([128, cfg.NT128, 128], BF16, "vtk"), Buf("vtk")) for _ in range(2)])
        rt = Ring([(self.sb([64, 2, TB], F32, "rt"), Buf("rt")) for _ in range(2)])
        Pt = Ring([(self.sb([128, TB], BF16, "Pt"), Buf("Pt")) for _ in range(4)])
        rec = Ring([(self.sb([128, TB], F32, "rec"), Buf("rec")) for _ in range(2)])
        pj = Ring([7, 2, 3, 4, 5])
        sc = Ring([0, 1, 6])
        od = Ring([(2, 3), (4, 5)])
        scale = float(192 ** -0.5)
        KT = TB // 128
        for h in range(H):
            kn, knB = knT.next(); qn, qnB = qnT.next(); qp, qpB = qpR.next(); vt, vtB = vtk.next()
            for tb in range(NTB):
                sl = slice(tb * TB, (tb + 1) * TB)
                b = pj.next()
                for k in range(NLk):
                    self.MM(self.ps[b][:, 0:TB], wkv[:, k, h * 256:h * 256 + 128], ckvn[:, k, sl], k == 0, k == NLk - 1, R=[wkvB, ckvnB], W=[self.pb[b]], sig=(k == NLk - 1))
                self.ACT(kn[:, sl], self.ps[b][:, 0:TB], AF.Copy, R=[self.pb[b]], W=[knB])
                b = pj.next()
                for k in range(NLq):
                    self.MM(self.ps[b][:, 0:TB], wq[:, k, h * 192:h * 192 + 128], cqn[:, k, sl], k == 0, k == NLq - 1, R=[wqB, cqnB], W=[self.pb[b]], sig=(k == NLq - 1))
                self.I(mk.dve, "tensor_copy", qn[:, sl], self.ps[b][:, 0:TB], R=[self.pb[b]], W=[qnB])
                b1 = pj.next()
                for k in range(NLq):
                    self.MM(self.ps[b1][0:64, 0:TB], wq[:, k, h * 192 + 128:h * 192 + 192], cqn[:, k, sl], k == 0, k == NLq - 1, R=[wqB, cqnB], W=[self.pb[b1]], sig=(k == NLq - 1))
                b2 = pj.next()
                for k in range(NLq):
                    self.MM(self.ps[b2][0:64, 0:TB], wqs[:, k, h, :], cqn[:, k, sl], k == 0, k == NLq - 1, R=[wqsB, cqnB], W=[self.pb[b2]], sig=(k == NLq - 1))
                r_, rB = rt.next()
                self.I(mk.dve, "tensor_tensor", r_[:, 0, :], self.ps[b1][0:64, 0:TB], self.COS[:, sl], op=ALU.mult, R=[self.pb[b1], self.ropeB], W=[rB])
                self.I(mk.dve, "tensor_tensor", r_[:, 1, :], self.ps[b2][0:64, 0:TB], self.SINS[:, sl], op=ALU.mult, R=[self.pb[b2], self.ropeB], W=[rB])
                self.I(mk.dve, "tensor_tensor", qp[:, sl], r_[:, 0, :], r_[:, 1, :], op=ALU.add, R=[rB], W=[qpB])
            for i0 in range(0, cfg.NT128, 4):
                n4 = min(4, cfg.NT128 - i0)
                b = pj.next()
                for ii in range(n4):
                    i = i0 + ii
                    for k in range(NLk):
                        self.MM(self.ps[b][:, ii * 128:(ii + 1) * 128], ckvn[:, k, i * 128:(i + 1) * 128], wkv[:, k, h * 256 + 128:h * 256 + 256],
                                k == 0, k == NLk - 1, R=[wkvB, ckvnB], W=[self.pb[b]], sig=(ii == n4 - 1 and k == NLk - 1))
                self.ACT(vt[:, i0:i0 + n4, :], self.ps[b][:, 0:n4 * 128].rearrange("p (i d) -> p i d", i=n4), AF.Copy, R=[self.pb[b]], W=[vtB])
            for qb in range(NTB):
                bo, bd = od.next()
                nkt = (qb + 1) * KT
                pend = []

                def emit_pv(item):
                    kt, q0, P_, PB = item
                    self.MM(self.ps[bo][:, q0:TB], vt[:, kt, :], P_[:, q0:TB], kt == 0, kt == nkt - 1, R=[vtB, PB], W=[self.pb[bo]], sig=False)
                    self.MM(self.ps[bd][:, q0:TB], self.onesB, P_[:, q0:TB], kt == 0, kt == nkt - 1, R=[self.cbB, PB], W=[self.pb[bd]], sig=True)
                for kt in range(nkt):
                    j = kt - qb * KT
                    q0 = max(0, j) * 128
                    bs = sc.next()
                    qsl = slice(qb * TB + q0, (qb + 1) * TB)
                    ksl = slice(kt * 128, (kt + 1) * 128)
                    self.MM(self.ps[bs][:, q0:TB], kn[:, ksl], qn[:, qsl], True, False, R=[knB, qnB], W=[self.pb[bs]], sig=False)
                    self.MM(self.ps[bs][:, q0:TB], kpeR[:, ksl], qp[:, qsl], False, True, R=[kpeB, qpB], W=[self.pb[bs]], sig=True)
                    P_, PB = Pt.next()
                    self.ACT(P_[:, q0:TB], self.ps[bs][:, q0:TB], AF.Exp, R=[self.pb[bs]], W=[PB], scale=scale)
                    if j >= 0:
                        self.I(mk.dve, "tensor_tensor", P_[:, q0:q0 + 128], P_[:, q0:q0 + 128], self.causB, op=ALU.mult, R=[PB, self.cbB], W=[PB])
                    pend.append((kt, q0, P_, PB))
                    if len(pend) > 2:
                        emit_pv(pend.pop(0))
                while pend:
                    emit_pv(pend.pop(0))
                rc, rcB = rec.next()
                self.I(mk.dve, "reciprocal", rc[:], self.ps[bd][:, 0:TB], R=[self.pb[bd]], W=[rcB])
                self.I(mk.dve, "tensor_tensor", aT[:, h, qb * TB:(qb + 1) * TB], self.ps[bo][:, 0:TB], rc[:], op=ALU.mult, R=[self.pb[bo], rcB], W=[aTB])
        self.mark(f'mla_out_{l}')
        self._pset = 0
        slots = Ring([(self.sb([128, H, 512], BF16, "womla"), Buf("womla")) for _ in range(2)])
        gt = Ring([(self.sb([128, TB], BF16, "gat"), Buf("gat")) for _ in range(3 * NTB)])
        mo = Ring([(self.sb([128, TB], BF16, "mo"), Buf("mo")) for _ in range(4)])
        pend5 = {}

        def pre5(g):
            for tb in range(NTB):
                g_, gB = gt.next()
                mk.dma(mk.sp, g_[:], self.s_gate[g, :, tb * TB:(tb + 1) * TB], writes=[gB])
                pend5[(g, tb)] = (g_, gB)

        def evac(g, M, tb, psap, psB):
            sl = slice(tb * TB, (tb + 1) * TB)
            g_, gB = pend5.pop((g, tb))
            m_, mB = mo.next()
            self.I(mk.dve, "tensor_tensor", m_[:], psap, g_[:], op=ALU.mult, R=[psB, gB], W=[mB])
            mk.dma(mk.sp, self.s_ma[g, :, sl], m_[:], reads=[mB])
        self.stream_mm(self.w_o_mla[l], self.col_blocks(0, cfg.D), aT, aTB, H, evac, slots, pre=pre5)
        self.pop()

    def phase_gdn(self, l):
        self.mark(f'gdn_{l}')
        cfg, mk = self.cfg, self.mk
        T, TB, NTB, GH, HG, NCH, KD = cfg.T, cfg.TB, cfg.NTB, cfg.GH, cfg.HG, cfg.NCH, cfg.KD
        V = mk.dve
        self.push()
        ogT = self.sb([128, GH, T], BF16, "ogT"); ogB = Buf("ogT")
        self.push()
        gsh = [64, NCH, GH]
        beta = self.sb(gsh, F32, "beta"); g_ = self.sb(gsh, F32, "g"); expG = self.sb(gsh, F32, "expG")
        expGLmG = self.sb(gsh, F32, "expGLmG"); c1 = self.sb(gsh, F32, "c1"); nbeta = self.sb(gsh, F32, "nbeta")
        expGL = self.sb([128, NCH, GH], F32, "expGL")
        gB = Buf("gates")
        self.push()
        ba = self.sb([2 * GH, T], F32, "ba"); baB = Buf("ba")
        batok = self.sb([64, NCH, 2 * GH], F32, "batok"); btB = Buf("batok")
        mk.dma(mk.sp, ba[:], self.s_ba, writes=[baB])
        b = 0
        for n in range(NCH):
            self.TR(self.ps[b][0:64, n * 2 * GH:(n + 1) * 2 * GH], ba[:, n * 64:(n + 1) * 64], self.identF[0:2 * GH, 0:2 * GH],
                    R=[baB, self.cstB], W=[self.pb[b]], sig=(n == NCH - 1))
        self.I(V, "tensor_copy", batok[:], self.ps[b][0:64, 0:NCH * 2 * GH].rearrange("p (n g) -> p n g", n=NCH), R=[self.pb[b]], W=[btB])
        self.ACT(beta[:], batok[:, :, 0:GH], AF.Sigmoid, R=[btB], W=[gB])
        self.I(V, "tensor_tensor", g_[:], batok[:, :, GH:2 * GH], self.grow[:, l, 1, :].unsqueeze(1).to_broadcast(gsh), op=ALU.add, R=[btB, self.growB], W=[gB])
        self.ACT(g_[:], g_[:], AF.Exp, R=[gB], W=[gB])
        self.ACT(g_[:], g_[:], AF.Ln, R=[gB, self.ccB], W=[gB], bias=self.onec[0:64, :])
        self.I(V, "tensor_tensor", g_[:], g_[:], self.grow[:, l, 0, :].unsqueeze(1).to_broadcast(gsh), op=ALU.mult, R=[gB, self.growB], W=[gB])
        for (lhs, M, dst, bb) in [(self.U64, 64, expG, 1), (self.SL64, 64, expGLmG, 2), (self.onesF[0:64, :], 128, expGL, 3)]:
            for n in range(NCH):
                self.MM(self.ps[bb][0:M, n * GH:(n + 1) * GH], lhs, g_[:, n, :], True, True, R=[self.cstB, gB], W=[self.pb[bb]], sig=(n == NCH - 1))
            self.ACT(dst[:], self.ps[bb][0:M, 0:NCH * GH].rearrange("p (n g) -> p n g", n=NCH), AF.Exp, R=[self.pb[bb]], W=[gB])
        self.I(V, "tensor_tensor", c1[:], beta[:], expG[:], op=ALU.mult, R=[gB], W=[gB])
        self.I(V, "tensor_scalar", nbeta[:], beta[:], -1.0, None, op0=ALU.mult, R=[gB], W=[gB])
        self.pop()
        gstop = getattr(self, "gstop", 9)
        if gstop == 0:
            self.pop(); self.pop()
            return
        for gp in range(GH // HG):
            hs = slice(gp * HG, (gp + 1) * HG)
            self.push()
            qT = self.sb([128, HG, T], BF16, "gqT"); kT = self.sb([128, HG, T], BF16, "gkT"); vT = self.sb([128, HG, T], BF16, "gvT")
            qkvB = Buf("gqkv")
            self.mark(f'gdn_g1_{l}_{gp}')
            self.push()
            ub = Ring([(self.sb([128, T], F32, "u"), Buf("u")) for _ in range(2)])
            ab = Ring([(self.sb([128, T], F32, "acc"), Buf("acc")) for _ in range(2)])
            sqb = Ring([(self.sb([128, T], BF16, "csq"), Buf("csq")) for _ in range(1)])
            rsb = Ring([(self.sb([128, TB], F32, "crs"), Buf("crs")) for _ in range(2)])
            banks = Ring([4, 5, 6, 7])
            for kind, dstT in ((0, qT), (1, kT), (2, vT)):
                for hh in range(HG):
                    ci = kind * GH + gp * HG + hh
                    u, uB = ub.next()
                    mk.dma(mk.sp, u[:], self.s_qkv[ci], writes=[uB])
                    a, aB = ab.next()
                    w = lambda j: self.cw[:, l, ci, j:j + 1]
                    self.I(V, "tensor_scalar", a[:], u[:], w(3), None, op0=ALU.mult, R=[uB, self.cwB], W=[aB])
                    for d in (1, 2, 3):
                        self.I(V, "scalar_tensor_tensor", a[:, d:T], u[:, 0:T - d], w(3 - d), a[:, d:T], op0=ALU.mult, op1=ALU.add,
                               R=[uB, self.cwB, aB], W=[aB])
                    if kind == 2:
                        self.ACT(dstT[:, hh, :], a[:], AF.Silu, R=[aB], W=[qkvB])
                        continue
                    self.ACT(a[:], a[:], AF.Silu, R=[aB], W=[aB])
                    s_, sB = sqb.next()
                    self.ACT(s_[:], a[:], AF.Square, R=[aB], W=[sB])
                    for tb in range(NTB):
                        sl = slice(tb * TB, (tb + 1) * TB)
                        b = banks.next()
                        self.MM(self.ps[b][:, 0:TB], self.onesB, s_[:, sl], True, True, R=[self.cbB, sB], W=[self.pb[b]], sig=True)
                        r_, rB = rsb.next()
                        self.ACT(r_[:], self.ps[b][:, 0:TB], AF.Sqrt, R=[self.pb[b], self.ccB], W=[rB], bias=self.epsc)
                        self.I(V, "reciprocal", r_[:], r_[:], R=[rB], W=[rB])
                        self.I(V, "scalar_tensor_tensor", dstT[:, hh, sl], a[:, sl], float(128 ** -0.5) if kind == 0 else 1.0, r_[:],
                               op0=ALU.mult, op1=ALU.mult, R=[aB, rB], W=[qkvB])
            self.pop()
            if gstop == 1:
                self.pop()
                continue
            self.mark(f'gdn_g2_{l}_{gp}')
            self.push()
            W4 = HG * 64
            W8 = HG * 128
            S = self.sb([128, HG, 128], F32, "S"); SB_ = Buf("S")
            self.I(V, "memset", S[:], 0.0, W=[SB_])
            f4 = lambda nm, n: Ring([(self.sb([64, HG, 64], F32, nm), Buf(nm)) for _ in range(n)])
            isets = []
            for par_ in range(2):
                isets.append(dict(gU=f4("gU", 1), E=f4("E", 1), Es=f4("Es", 1), at=f4("att", 1), A=f4("A", 3), B=f4("Bm", 3), P=f4("P", 2),
                                  ainv=f4("ainv", 1), attT=f4("attT", 1)))
            ibanks = Ring([6, 7])
            f8 = lambda nm, n, dt=F32: Ring([(self.sb([64, HG, 128], dt, nm), Buf(nm)) for _ in range(n)])
            kd_r, vb_r, t_r, r_r, vn_r, o_r, on_r = f8("kdec", 2), f8("vb", 2), f8("t8", 2), f8("r8", 2), f8("vnew", 2), f8("o8", 2), f8("on", 2)
            ss_r = Ring([(self.sb([64, HG], F32, "ss"), Buf("ss")) for _ in range(2)])
            tS = self.sb([128, HG, 128], F32, "tS"); tSB = Buf("tS")
            fb = Ring([0, 1, 2, 3, 4, 5])
            I64 = self.identF[0:64, 0:64]
            bc4 = lambda ap2: ap2.unsqueeze(1).to_broadcast([64, HG, 64])
            results = {}

            cf_r = Ring([(self.sb([128, 3, HG, 64], F32, "cf"), Buf("cf")) for _ in range(3)])
            cf_cache = {}

            def chunk_f32(n):
                if n not in cf_cache:
                    c_, cB_ = cf_r.next()
                    csl_ = slice(n * 64, (n + 1) * 64)
                    self.ACT(c_[:, 0], qT[:, :, csl_], AF.Copy, R=[qkvB], W=[cB_])
                    self.I(V, "tensor_copy", c_[:, 1], kT[:, :, csl_], R=[qkvB], W=[cB_])
                    self.ACT(c_[:, 2], vT[:, :, csl_], AF.Copy, R=[qkvB], W=[cB_]) if True else self.I(V, "tensor_copy", c_[:, 2], vT[:, :, csl_], R=[qkvB], W=[cB_])
                    cf_cache[n] = (c_[:, 1], c_[:, 0], c_[:, 2], cB_)
                    if getattr(self, "debug", False) and n == 0 and l == 0 and gp == 0:
                        mk.dma(mk.sp, self.dram("dbg_cf", [128, 3, HG, 64], F32, "Internal"), c_[:], reads=[cB_])
                    cf_cache.pop(n - 3, None)
                return cf_cache[n]

            def half_(st_):
                b_ = ibanks.next()
                return self.ps[b_][0:64, 0:W4].rearrange("p (h c) -> p h c", h=HG), self.pb[b_]

            def inverse(n):
                csl = slice(n * 64, (n + 1) * 64)
                st_ = isets[n % 2]
                gU_r, E_r, Es_r, at_r, A_r, B_r, P_r, ainv, attT = (st_[k_] for k_ in ("gU", "E", "Es", "at", "A", "B", "P", "ainv", "attT"))
                half = lambda: half_(st_)
                gU, gUB = gU_r.next()
                self.I(V, "tensor_tensor", gU[:], bc4(self.U64), g_[:, n, hs].unsqueeze(2).to_broadcast([64, HG, 64]), op=ALU.mult, R=[self.cstB, gB], W=[gUB])
                pD, pDB = half()
                for hh in range(HG):
                    self.MM(pD[:, hh, :], gU[:, hh, :], self.SL64, True, True, R=[gUB, self.cstB], W=[pDB], sig=(hh == HG - 1))
                E, EB = E_r.next()
                self.ACT(E[:], pD, AF.Exp, R=[pDB], W=[EB])
                yield
                self.I(V, "tensor_tensor", E[:], E[:], bc4(self.LINC), op=ALU.mult, R=[EB, self.cstB], W=[EB])
                Es, EsB = Es_r.next()
                self.I(V, "tensor_tensor", Es[:], E[:], bc4(self.SL64), op=ALU.mult, R=[EB, self.cstB], W=[EsB])
                pK, pKB = half()
                pQ, pQB = half()
                kf, qf, vf, cfB = chunk_f32(n)
                for hh in range(HG):
                    self.MM(pK[:, hh, :], kf[:, hh, :], kf[:, hh, :], True, True, R=[cfB], W=[pKB], sig=(hh == HG - 1))
                for hh in range(HG):
                    self.MM(pQ[:, hh, :], qf[:, hh, :], kf[:, hh, :], True, True, R=[cfB], W=[pQB], sig=(hh == HG - 1))
                yield
                A, AB = A_r.next()
                self.I(V, "tensor_tensor", A[:], pK, Es[:], op=ALU.mult, R=[pKB, EsB], W=[AB])
                self.I(V, "tensor_tensor", A[:], A[:], nbeta[:, n, hs].unsqueeze(2).to_broadcast([64, HG, 64]), op=ALU.mult, R=[AB, gB], W=[AB])
                at, atB = at_r.next()
                self.I(V, "tensor_tensor", at[:], pQ, E[:], op=ALU.mult, R=[pQB, EB], W=[atB])
                if getattr(self, "debug", False) and n == 0 and l == 0 and gp == 0:
                    dk_ = self.sb([64, 2, HG, 64], F32, "dbgk"); dkB = Buf("dbgk")
                    self.I(V, "tensor_copy", dk_[:, 0], pK, R=[pKB], W=[dkB])
                    self.I(V, "tensor_copy", dk_[:, 1], pQ, R=[pQB], W=[dkB])
                    mk.dma(mk.sp, self.dram("dbg_KQ", [64, 2, HG, 64], F32, "Internal"), dk_[:], reads=[dkB])
                    mk.dma(mk.sp, self.dram("dbg_Es", [64, HG, 64], F32, "Internal"), Es[:], reads=[EsB])
                    mk.dma(mk.sp, self.dram("dbg_E", [64, HG, 64], F32, "Internal"), E[:], reads=[EB])
                    mk.dma(mk.sp, self.dram("dbg_N", [64, HG, 64], F32, "Internal"), A[:], reads=[AB])
                    mk.dma(mk.sp, self.dram("dbg_at", [64, HG, 64], F32, "Internal"), at[:], reads=[atB])
                pT1, pT1B = half()
                pT2, pT2B = half()
                for hh in range(HG):
                    self.TR(pT1[:, hh, :], A[:, hh, :], I64, R=[AB, self.cstB], W=[pT1B], sig=(hh == HG - 1))
                for hh in range(HG):
                    self.TR(pT2[:, hh, :], at[:, hh, :], I64, R=[atB, self.cstB], W=[pT2B], sig=(hh == HG - 1))
                yield
                Bm, BB = B_r.next()
                self.ACT(Bm[:], pT1, AF.Copy, R=[pT1B], W=[BB])
                aT_, aTB_ = attT.next()
                self.ACT(aT_[:], pT2, AF.Copy, R=[pT2B], W=[aTB_])
                P, PB = P_r.next()
                self.I(V, "tensor_tensor", P[:], Bm[:], bc4(I64), op=ALU.add, R=[BB, self.cstB], W=[PB])
                for lev in range(1, 6):
                    pB, pBB = half()
                    pA, pAB = half()
                    for hh in range(HG):
                        self.MM(pB[:, hh, :], A[:, hh, :], Bm[:, hh, :], True, True, R=[AB, BB], W=[pBB], sig=(hh == HG - 1))
                    for hh in range(HG):
                        self.MM(pA[:, hh, :], Bm[:, hh, :], A[:, hh, :], True, True, R=[AB, BB], W=[pAB], sig=(hh == HG - 1))
                    yield
                    A2, A2B = A_r.next()
                    B2, B2B = B_r.next()
                    self.ACT(A2[:], pA, AF.Copy, R=[pAB], W=[A2B])
                    if lev < 5:
                        self.I(V, "tensor_copy", B2[:], pB, R=[pBB], W=[B2B])
                    A, AB, Bm, BB = A2, A2B, B2, B2B
                    pP, pPB = half()
                    for hh in range(HG):
                        self.MM(pP[:, hh, :], A[:, hh, :], P[:, hh, :], True, True, R=[AB, PB], W=[pPB], sig=(hh == HG - 1))
                    yield
                    if lev < 5:
                        P2, P2B = P_r.next()
                    else:
                        P2, P2B = ainv.next()
                    self.I(V, "tensor_tensor", P2[:], pP, P[:], op=ALU.add, R=[pPB, PB], W=[P2B])
                    P, PB = P2, P2B
                results[n] = (P, PB, aT_, aTB_)
                if getattr(self, "debug", False) and n == 0 and l == 0 and gp == 0:
                    mk.dma(mk.sp, self.dram("dbg_ainv", [64, HG, 64], F32, "Internal"), P[:], reads=[PB])
                    mk.dma(mk.sp, self.dram("dbg_attT", [64, HG, 64], F32, "Internal"), aT_[:], reads=[aTB_])
                yield

            def full():
                b = fb.next()
                return self.ps[b], self.pb[b]

            def scan(n):
                csl = slice(n * 64, (n + 1) * 64)
                bc8 = lambda t3: t3[:, n, hs].unsqueeze(2).to_broadcast([64, HG, 128])
                kf, qf, vf, cfB = chunk_f32(n)
                v3 = lambda ap: ap[0:64, 0:W8].rearrange("p (h d) -> p h d", h=HG)
                pXk, pXkB = full()
                pXv, pXvB = full()
                for hh in range(HG):
                    self.TR(pXk[0:64, hh * 128:(hh + 1) * 128], kf[:, hh, :], self.identF, R=[cfB, self.cstB], W=[pXkB], sig=(hh == HG - 1))
                for hh in range(HG):
                    self.TR(pXv[0:64, hh * 128:(hh + 1) * 128], vf[:, hh, :], self.identF, R=[cfB, self.cstB], W=[pXvB], sig=(hh == HG - 1))
                kd, kdB = kd_r.next()
                vb, vbB = vb_r.next()
                self.I(V, "tensor_tensor", kd[:], v3(pXk), bc8(expGLmG), op=ALU.mult, R=[pXkB, gB], W=[kdB])
                self.I(V, "tensor_tensor", vb[:], v3(pXv), bc8(beta), op=ALU.mult, R=[pXvB, gB], W=[vbB])
                pKS, pKSB = full()
                pQS, pQSB = full()
                for hh in range(HG):
                    self.MM(pKS[0:64, hh * 128:(hh + 1) * 128], kf[:, hh, :], S[:, hh, :], True, True, R=[cfB, SB_], W=[pKSB], sig=(hh == HG - 1))
                for hh in range(HG):
                    self.MM(pQS[0:64, hh * 128:(hh + 1) * 128], qf[:, hh, :], S[:, hh, :], True, True, R=[cfB, SB_], W=[pQSB], sig=(hh == HG - 1))
                yield
                t8, t8B = t_r.next()
                r8, r8B = r_r.next()
                self.I(V, "tensor_tensor", t8[:], v3(pKS), bc8(c1), op=ALU.mult, R=[pKSB, gB], W=[t8B])
                self.I(V, "tensor_tensor", r8[:], vb[:], t8[:], op=ALU.subtract, R=[vbB, t8B], W=[r8B])
                while n not in results:
                    yield
                AinvT, AiB, aT_, aTB_ = results.pop(n)
                pVN, pVNB = full()
                for hh in range(HG):
                    self.MM(pVN[0:64, hh * 128:(hh + 1) * 128], AinvT[:, hh, :], r8[:, hh, :], True, True, R=[AiB, r8B], W=[pVNB], sig=(hh == HG - 1))
                yield
                vn, vnB = vn_r.next()
                self.ACT(vn[:], v3(pVN), AF.Copy, R=[pVNB], W=[vnB])
                pS, pSB = full()
                for hh in range(HG):
                    self.MM(pS[:, hh * 128:(hh + 1) * 128], kd[:, hh, :], vn[:, hh, :], True, True, R=[kdB, vnB], W=[pSB], sig=(hh == HG - 1))
                pAV, pAVB = full()
                for hh in range(HG):
                    self.MM(pAV[0:64, hh * 128:(hh + 1) * 128], aT_[:, hh, :], vn[:, hh, :], True, True, R=[aTB_, vnB], W=[pAVB], sig=(hh == HG - 1))
                yield
                self.I(V, "tensor_tensor", tS[:], S[:], expGL[:, n, hs].unsqueeze(2).to_broadcast([128, HG, 128]), op=ALU.mult, R=[SB_, gB], W=[tSB])
                self.I(V, "tensor_tensor", S[:], tS[:], pS[:, 0:W8].rearrange("p (h d) -> p h d", h=HG), op=ALU.add, R=[tSB, pSB], W=[SB_])
                o8, o8B = o_r.next()
                t8b, t8bB = t_r.next()
                self.I(V, "tensor_tensor", t8b[:], v3(pQS), bc8(expG), op=ALU.mult, R=[pQSB, gB], W=[t8bB])
                self.I(V, "tensor_tensor", o8[:], t8b[:], v3(pAV), op=ALU.add, R=[t8bB, pAVB], W=[o8B])
                yield
                self.I(V, "tensor_tensor", t8b[:], o8[:], o8[:], op=ALU.mult, R=[o8B], W=[t8bB])
                ss, ssB = ss_r.next()
                self.I(V, "tensor_reduce", ss[:], t8b[:], axis=AX.X, op=ALU.add, R=[t8bB], W=[ssB])
                self.ACT(ss[:], ss[:], AF.Sqrt, R=[ssB, self.ccB], W=[ssB], bias=self.epsc[0:64, :], scale=1.0 / 128)
                self.I(V, "reciprocal", ss[:], ss[:], R=[ssB], W=[ssB])
                on, onB = on_r.next()
                self.I(V, "tensor_tensor", on[:], o8[:], ss[:].unsqueeze(2).to_broadcast([64, HG, 128]), op=ALU.mult, R=[o8B, ssB], W=[onB])
                pO, pOB = full()
                for hh in range(HG):
                    self.TR(pO[:, hh * 64:(hh + 1) * 64], on[:, hh, :], I64, R=[onB, self.cstB], W=[pOB], sig=(hh == HG - 1))
                yield
                self.I(V, "tensor_scalar", ogT[:, hs, csl], pO[:, 0:W4].rearrange("p (h c) -> p h c", h=HG), self.par[:, l, self.o_gn:self.o_gn + 1], None,
                       op0=ALU.mult, R=[pOB, self.parB], W=[ogB])
                yield

            inv_gen = None
            inv_next = 0
            g2mode = getattr(self, "g2mode", "")
            if g2mode.startswith("inv"):
                lim_ = int(g2mode[3:] or 99)
                for n in range(NCH):
                    for i_, _ in enumerate(inverse(n)):
                        if i_ + 1 >= lim_:
                            break
            for n in range(NCH if not g2mode.startswith("inv") else 0):
                sg_ = scan(n)
                done = False
                while not done:
                    if inv_gen is None and inv_next < NCH and inv_next <= n + 1:
                        inv_gen = inverse(inv_next)
                        inv_next += 1
                    if inv_gen is not None:
                        try:
                            next(inv_gen)
                        except StopIteration:
                            inv_gen = None
                    try:
                        next(sg_)
                    except StopIteration:
                        done = True
            assert inv_gen is None or all(False for _ in inv_gen)
            self.pop()
            self.pop()
        self.pop()
        if gstop <= 2:
            self.pop()
            return
        self.mark(f'gdn_g3_{l}')
        self.push()
        zb = Ring([(self.sb([128, TB], F32, "zb"), Buf("zb")) for _ in range(4)])
        for h in range(GH):
            for tb in range(NTB):
                sl = slice(tb * TB, (tb + 1) * TB)
                z_, zB = zb.next()
                mk.dma(mk.sp, z_[:], self.s_z[h, :, sl], writes=[zB])
                self.ACT(z_[:], z_[:], AF.Silu, R=[zB], W=[zB])
                self.I(V, "tensor_tensor", ogT[:, h, sl], ogT[:, h, sl], z_[:], op=ALU.mult, R=[ogB, zB], W=[ogB])
        self.pop()
        if getattr(self, "debug", False):
            dbg = self.dram(f"dbg_og{l}", [GH, 128, T], BF16, "Internal")
            mk.dma(mk.sp, dbg.rearrange("h p t -> p h t"), ogT[:], reads=[ogB])
        mT = self.sb([128, KD, T], BF16, "mT"); mTB = Buf("mT")
        self._pset = 0
        slots = Ring([(self.sb([128, max(GH, KD), 512], BF16, "wog"), Buf("wog")) for _ in range(2)])
        gt = Ring([(self.sb([128, TB], BF16, "gbt"), Buf("gbt")) for _ in range(3 * NTB)])
        mat = Ring([(self.sb([128, TB], BF16, "mat"), Buf("mat")) for _ in range(3 * NTB)])
        tm = Ring([(self.sb([128, TB], F32, "tm"), Buf("tm")) for _ in range(3)])
        pendb = {}

        def pre_b(g):
            for tb in range(NTB):
                sl = slice(tb * TB, (tb + 1) * TB)
                g2, g2B = gt.next()
                mk.dma(mk.sp, g2[:], self.s_gate[KD + g, :, sl], writes=[g2B])
                ma, maB = mat.next()
                mk.dma(mk.sp, ma[:], self.s_ma[g, :, sl], writes=[maB])
                pendb[(g, tb)] = (g2, g2B, ma, maB)

        def evac_b(g, M, tb, psap, psB):
            sl = slice(tb * TB, (tb + 1) * TB)
            g2, g2B, ma, maB = pendb.pop((g, tb))
            t_, tB = tm.next()
            self.I(V, "tensor_tensor", t_[:], psap, g2[:], op=ALU.mult, R=[psB, g2B], W=[tB])
            self.I(V, "tensor_tensor", mT[:, g, sl], t_[:], ma[:], op=ALU.add, R=[tB, maB], W=[mTB])
        self.stream_mm(self.w_o_gdn[l], self.col_blocks(0, cfg.D), ogT, ogB, GH, evac_b, slots, pre=pre_b)
        xt = Ring([(self.sb([128, TB], F32, "xt"), Buf("xt")) for _ in range(3 * NTB)])
        gtc = 2 * KD
        pendo = {}

        def pre_o(g):
            for tb in range(NTB):
                x_, xB = xt.next()
                key = (g, tb)
                xb_ = self.xTB.setdefault(key, Buf(f"xT{key}"))
                mk.dma(mk.sp, x_[:], self.xT[g, :, tb * TB:(tb + 1) * TB], reads=[xb_], writes=[xB])
                pendo[key] = (x_, xB, xb_)

        def evac_o(g, M, tb, psap, psB):
            sl = slice(tb * TB, (tb + 1) * TB)
            x_, xB, xb_ = pendo.pop((g, tb))
            self.I(V, "scalar_tensor_tensor", x_[:], psap, self.mod[:, l, gtc + g:gtc + g + 1], x_[:], op0=ALU.mult, op1=ALU.add,
                   R=[psB, self.modB, xB], W=[xB])
            mk.dma(mk.sp, self.xT[g, :, sl], x_[:], reads=[xB], writes=[xb_])
        self.stream_mm(self.w_o[l], self.col_blocks(0, cfg.D), mT, mTB, KD, evac_o, slots, pre=pre_o)
        self.pop()

    def phase_ffn(self, l, hT, hB):
        self.mark(f'ffn_{l}')
        cfg, mk = self.cfg, self.mk
        T, TB, NTB, KD, FQ, DFF = cfg.T, cfg.TB, cfg.NTB, cfg.KD, cfg.FQ, cfg.DFF
        V = mk.dve
        act = self.sb([128, FQ, T], BF16, "ffact"); actB = Buf("ffact")
        slots = Ring([(self.sb([128, max(KD, FQ), 512], BF16, "wff"), Buf("wff")) for _ in range(2)])
        sg = {}
        sgr = Ring([(self.sb([128, TB], F32, "sg"), Buf("sg")) for _ in range(2 * NTB + 2)])
        xt = Ring([(self.sb([128, TB], F32, "fxt"), Buf("fxt")) for _ in range(2 * NTB + 2)])
        pendd = {}
        gtc = 5 * KD
        self._pset = 0
        for q in range(cfg.NFP):
            blocks = []
            for j0 in range(0, FQ, 2):
                segs, groups = [], []
                off = 0
                for j in range(j0, min(FQ, j0 + 2)):
                    groups += [(off, 128, ("g", j)), (off + 128, 128, ("u", j))]
                    off += 256
                segs = [((q * FQ + j0) * 256, off)]
                blocks.append((segs, groups))

            def evac_gu(gid, M, tb, psap, psB):
                kind, j = gid
                sl = slice(tb * TB, (tb + 1) * TB)
                if kind == "g":
                    s_, sB = sgr.next()
                    self.ACT(s_[:], psap, AF.Silu, R=[psB], W=[sB])
                    sg[(j, tb)] = (s_, sB)
                else:
                    s_, sB = sg.pop((j, tb))
                    self.I(V, "tensor_tensor", act[:, j, sl], psap, s_[:], op=ALU.mult, R=[psB, sB], W=[actB])
            self.stream_mm(self.w_gate_up[l], blocks, hT, hB, KD, evac_gu, slots)

            def pre_d(g):
                for tb in range(NTB):
                    x_, xB = xt.next()
                    key = (g, tb)
                    xb_ = self.xTB.setdefault(key, Buf(f"xT{key}"))
                    mk.dma(mk.sp, x_[:], self.xT[g, :, tb * TB:(tb + 1) * TB], reads=[xb_], writes=[xB])
                    pendd[key] = (x_, xB, xb_)

            def evac_d(g, M, tb, psap, psB):
                sl = slice(tb * TB, (tb + 1) * TB)
                x_, xB, xb_ = pendd.pop((g, tb))
                self.I(V, "scalar_tensor_tensor", x_[:], psap, self.mod[:, l, gtc + g:gtc + g + 1], x_[:], op0=ALU.mult, op1=ALU.add,
                       R=[psB, self.modB, xB], W=[xB])
                mk.dma(mk.sp, self.xT[g, :, sl], x_[:], reads=[xB], writes=[xb_])
            self.stream_mm(self.w_down[l][q * FQ * 128:(q + 1) * FQ * 128, :], self.col_blocks(0, cfg.D), act, actB, FQ, evac_d, slots, pre=pre_d)

    def phase_final(self):
        self.mark('phase_final')
        cfg, mk = self.cfg, self.mk
        KD, T = cfg.KD, cfg.T
        V = mk.dve
        NBF = cfg.NB
        self.push()
        xb = Ring([(self.sb([128, KD, NBF], F32, "fx"), Buf("fx")) for _ in range(3)])
        sq = Ring([(self.sb([128, KD, NBF], BF16, "fsq"), Buf("fsq")) for _ in range(2)])
        rs = Ring([(self.sb([128, NBF], F32, "frs"), Buf("frs")) for _ in range(2)])
        banks = Ring([1, 2, 3, 4, 5, 6, 7])
        xTv = self.xT.rearrange("k p t -> p k t")
        gsz = min(4, KD)
        for blk in range(T // NBF):
            sl = slice(blk * NBF, (blk + 1) * NBF)
            x_, xB = xb.next()
            mk.dma(mk.sp, x_[:], xTv[:, :, sl], writes=[xB])
            s_, sB = sq.next()
            self.ACT(s_[:], x_[:], AF.Square, R=[xB], W=[sB])
            for k in range(KD):
                self.MM(self.ps[0][:, 0:NBF], self.onesB, s_[:, k, :], k == 0, k == KD - 1, R=[self.cbB, sB], W=[self.pb[0]], sig=(k == KD - 1))
            r_, rB = rs.next()
            self.ACT(r_[:], self.ps[0][:, 0:NBF], AF.Sqrt, R=[self.pb[0], self.ccB], W=[rB], bias=self.epsc, scale=1.0 / cfg.D)
            self.I(V, "reciprocal", r_[:], r_[:], R=[rB], W=[rB])
            for k in range(KD):
                self.I(V, "scalar_tensor_tensor", x_[:, k, :], x_[:, k, :], self.fn[:, k:k + 1], r_[:], op0=ALU.mult, op1=ALU.mult,
                       R=[xB, self.fnB, rB], W=[xB])
            mk.dma(mk.sp, self.out_d.rearrange("k p t -> p k t")[:, :, sl], x_[:], reads=[xB], writes=[self.outB])
        self.pop()


_PROGRAM_CACHE = {}


def make_in_map(cfg, inp, b):
    L, KD, GH = cfg.L, cfg.KD, cfg.GH
    f = lambda a: np.ascontiguousarray(np.asarray(a))
    colT = lambda a, n: np.asarray(a).reshape(L, n, 128).transpose(2, 0, 1)
    par = np.zeros((128, L, 8 * KD + 16), np.float32)
    par[:, :, 0:KD] = colT(inp["norm_mix"], KD)
    par[:, :, KD:2 * KD] = colT(inp["norm_ffn"], KD)
    par[:, :, 2 * KD:8 * KD] = colT(inp["b_ada"], 6 * KD)
    par[:, :, 8 * KD:8 * KD + cfg.NLq] = colT(inp["q_a_norm"], cfg.NLq)
    par[:, :, 8 * KD + 4:8 * KD + 4 + cfg.NLk] = colT(inp["kv_a_norm"], cfg.NLk)
    par[:, :, 8 * KD + 8:8 * KD + 9] = colT(inp["gdn_norm"], 1)
    cw = np.asarray(inp["conv_w"]).reshape(L, 4, 3 * GH, 128).transpose(3, 0, 2, 1)
    nff = cfg.DFF // 128
    wgu = np.asarray(inp["w_gate_up"]).reshape(L, cfg.D, 2, nff, 128).transpose(0, 1, 3, 2, 4).reshape(L, cfg.D, 2 * cfg.DFF)
    m = {
        "x": f(np.asarray(inp["x"][b]).T.reshape(KD, 128, cfg.T)),
        "c": f(np.asarray(inp["c"][b]).reshape(KD, 128).T),
        "pos": f(inp["positions"][b]).reshape(1, cfg.T).astype(np.int32),
        "consts": make_consts(),
        "w_ada": f(inp["w_ada"]),
        "par": f(par),
        "cw": f(cw),
        "w_in": f(inp["w_in"]),
        "w_uq": f(inp["w_uq"]),
        "w_ukv": f(inp["w_ukv"]),
        "w_o_mla": f(inp["w_o_mla"]),
        "A_log": f(inp["A_log"]).reshape(L, 1, GH),
        "dt_bias": f(inp["dt_bias"]).reshape(L, 1, GH),
        "w_o_gdn": f(inp["w_o_gdn"]),
        "w_o": f(inp["w_o"]),
        "w_gate_up": f(wgu),
        "w_down": f(inp["w_down"]),
        "final_norm": f(np.asarray(inp["final_norm"]).reshape(KD, 128).T),
    }
    return m


def kernel(**inputs):
    cfg = Cfg()
    B = inputs["x"].shape[0]
    nc = Builder(cfg).build()
    shared = None
    in_maps = []
    for b in range(B):
        m = make_in_map(cfg, inputs, b) if shared is None else dict(shared)
        if shared is None:
            shared = m
        else:
            m["x"] = np.ascontiguousarray(np.asarray(inputs["x"][b]).T.reshape(cfg.KD, 128, cfg.T))
            m["c"] = np.ascontiguousarray(np.asarray(inputs["c"][b]).reshape(cfg.KD, 128).T)
            m["pos"] = np.ascontiguousarray(inputs["positions"][b]).reshape(1, cfg.T).astype(np.int32)
        in_maps.append(m)
    res = run_bass_kernel_spmd(nc, in_maps, core_ids=list(range(B)))
    out = np.stack([np.asarray(r["out"]).reshape(cfg.D, cfg.T).T for r in res.results], axis=0)
    return np.ascontiguousarray(out.astype(np.float32))
```

```python
import numpy as np
from contextlib import ExitStack
import concourse.bass as bass
import concourse.mybir as mybir
from concourse.bass_utils import run_bass_kernel_spmd

F32 = mybir.dt.float32
BF16 = mybir.dt.bfloat16
I32 = mybir.dt.int32
AF = mybir.ActivationFunctionType
ALU = mybir.AluOpType
AX = mybir.AxisListType


class Cfg:
    def __init__(self, D=2048, T=2048, L=2, H=8, QL=512, KVL=512, GH=8, DFF=5632):
        self.D, self.T, self.L, self.H, self.QL, self.KVL, self.GH, self.DFF = D, T, L, H, QL, KVL, GH, DFF
        self.KD = D // 128
        self.TB = min(512, T)
        self.NTB = T // self.TB
        self.NT128 = T // 128
        self.NCH = T // 64
        self.NB = min(256, T)
        self.NLq, self.NLk = QL // 128, KVL // 128
        self.o_cq = 0
        self.o_ckv = QL
        self.o_kpe = QL + KVL
        self.o_q = self.o_kpe + 64
        self.o_k = self.o_q + GH * 128
        self.o_v = self.o_k + GH * 128
        self.o_z = self.o_v + GH * 128
        self.o_b = self.o_z + GH * 128
        self.o_a = self.o_b + GH
        self.o_g = self.o_a + GH
        self.INW = self.o_g + 2 * D
        self.EPS = 1e-6
        self.HG = min(4, GH)
        nff = DFF // 128
        self.FQ = max(d for d in range(1, 12) if nff % d == 0)
        self.NFP = nff // self.FQ


class Ev:
    __slots__ = ("sem", "val")

    def __init__(self, sem, val=None):
        self.sem = sem
        self.val = val


class Buf:
    __slots__ = ("name", "w", "r")

    def __init__(self, name="b"):
        self.name = name
        self.w = None
        self.r = {}


class Eng:
    def __init__(self, name, eng, sem):
        self.name, self.eng, self.sem = name, eng, sem
        self.cnt = 0
        self.seen = {}
        self.pending = []
        self.nwait = 0
        self.nins = 0


class MK:
    def __init__(self, nc, stack, n_dma_sems=24):
        self.nc = nc
        ec = stack.enter_context
        self.pe = Eng("pe", nc.tensor, ec(nc.semaphore("s_pe")))
        self.act = Eng("act", nc.scalar, ec(nc.semaphore("s_act")))
        self.dve = Eng("dve", nc.vector, ec(nc.semaphore("s_dve")))
        self.pool = Eng("pool", nc.gpsimd, ec(nc.semaphore("s_pool")))
        self.sp = Eng("sp", nc.sync, ec(nc.semaphore("s_sp")))
        self.engs = [self.pe, self.act, self.dve, self.pool, self.sp]
        self.dsems = {}
        for q in (self.sp, self.pool):
            self.dsems[q.name] = [[ec(nc.semaphore(f"d_{q.name}{i}")), 0, None] for i in range(n_dma_sems)]
        self.drr = {"sp": 0, "pool": 0}
        self.bar_log = []
        self.bar_names = None

    def _wait(self, E, evs):
        best = {}
        for ev in evs:
            if ev is None:
                continue
            if ev.val is None:
                assert ev.sem is E.sem and E is self.pe, f"unresolved event waited by {E.name}"
                continue
            k = id(ev.sem)
            if k not in best or best[k].val < ev.val:
                best[k] = ev
        for k, ev in best.items():
            if E.seen.get(k, 0) < ev.val:
                E.eng.wait_ge(ev.sem, ev.val)
                E.seen[k] = ev.val
                E.nwait += 1

    @staticmethod
    def _deps(reads, writes):
        need = []
        for b in reads:
            need.append(b.w)
        for b in writes:
            need.append(b.w)
            need.extend(b.r.values())
        return need

    @staticmethod
    def _commit(ev, reads, writes):
        for b in reads:
            b.r[id(ev.sem)] = ev
        for b in writes:
            b.w = ev
            b.r = {}

    def op(self, E, fn, reads=(), writes=(), signal=True):
        self._wait(E, self._deps(reads, writes))
        ins = fn(E.eng)
        E.nins += 1
        ev = Ev(E.sem)
        E.pending.append(ev)
        if signal:
            E.cnt += 1
            ins.then_inc(E.sem, 1)
            for p in E.pending:
                p.val = E.cnt
            E.pending = []
        self._commit(ev, reads, writes)
        return ev

    def dma(self, Q, out_ap, in_ap, reads=(), writes=()):
        pool = self.dsems[Q.name]
        i = self.drr[Q.name]
        self.drr[Q.name] = (i + 1) % len(pool)
        slot = pool[i]
        need = self._deps(reads, writes)
        need.append(slot[2])
        self._wait(Q, need)
        slot[1] += 16
        ev = Ev(slot[0], slot[1])
        Q.eng.dma_start(out=out_ap, in_=in_ap).then_inc(slot[0], 16)
        Q.nins += 1
        slot[2] = ev
        self._commit(ev, reads, writes)
        return ev

    def barrier(self):
        self.bar_log.append(getattr(self, 'cur_phase', '?'))
        for Q in (self.sp, self.pool):
            self._wait(Q, [s[2] for s in self.dsems[Q.name]])
            assert not Q.pending
            Q.cnt += 1
            Q.eng.sem_inc(Q.sem, 1)
        assert not self.pe.pending
        evs = [Ev(E.sem, E.cnt) for E in self.engs if E.cnt > 0]
        for E in self.engs:
            self._wait(E, evs)


class Ring:
    def __init__(self, items):
        self.items = items
        self.i = 0

    def next(self):
        it = self.items[self.i]
        self.i = (self.i + 1) % len(self.items)
        return it


def make_consts():
    c = np.zeros((128, 640), np.float32)
    c[:, 0:128] = np.eye(128)
    k = np.arange(64)
    c[0:64, 128:192] = (k[:, None] <= k[None, :])
    c[0:64, 192:256] = (k[:, None] > k[None, :])
    c[0:64, 256:320] = (k[:, None] >= k[None, :])
    c[:, 320:448] = 1.0
    p = np.arange(128)
    c[:, 448:576] = (p[:, None] <= p[None, :])
    inv = (1.0 / (10000.0 ** (np.arange(0, 64, 2, dtype=np.float32) / 64))).astype(np.float32)
    c[0:64, 576] = np.concatenate([inv, inv])
    c[0:32, 577] = -1.0
    c[32:64, 577] = 1.0
    return c


class Builder:
    def __init__(self, cfg, nlayers=None, debug_dump=False):
        self.cfg = cfg
        self.nl = cfg.L if nlayers is None else nlayers
        self.nc = bass.Bass("TRN2", target_bir_lowering=False)
        self._uid = 0

    def I(self, E, name, *a, R=(), W=(), sig=True, **kw):
        return self.mk.op(E, lambda e: getattr(e, name)(*a, **kw), reads=R, writes=W, signal=sig)

    def MM(self, out, lhsT, rhs, start, stop, R, W, sig):
        return self.mk.op(self.mk.pe, lambda e: e.matmul(out, lhsT=lhsT, rhs=rhs, start=start, stop=stop),
                          reads=R, writes=W, signal=sig)

    def TR(self, out, in_, ident, R, W, sig=True):
        return self.mk.op(self.mk.pe, lambda e: e.transpose(out, in_, ident), reads=R, writes=W, signal=sig)

    def ACT(self, out, in_, func, R, W, bias=None, scale=1.0):
        if bias is None:
            return self.mk.op(self.mk.act, lambda e: e.activation(out, in_, func, scale=scale), reads=R, writes=W)
        return self.mk.op(self.mk.act, lambda e: e.activation(out, in_, func, bias=bias, scale=scale), reads=R, writes=W)

    def sb(self, shape, dtype, name=None):
        self._uid += 1
        t = self.scopes[-1].enter_context(self.nc.sbuf_tensor(f"{name or 't'}_{self._uid}", list(shape), dtype))
        return t

    def push(self):
        st = ExitStack()
        st.__enter__()
        self.scopes.append(st)

    def pop(self):
        self.mk.barrier()
        st = self.scopes.pop()
        st.__exit__(None, None, None)

    def dram(self, name, shape, dtype, kind):
        return self.nc.dram_tensor(name, list(shape), dtype, kind=kind).ap()

    def mark(self, name):
        self.mk.cur_phase = name
        self.marks.append((name, self.mk.pe.nins + self.mk.pe.nwait))

    def build(self):
        cfg, nc = self.cfg, self.nc
        self.marks = []
        D, T, L, KD = cfg.D, cfg.T, cfg.L, cfg.KD
        dr = self.dram
        self.x_d = dr("x", [KD, 128, T], F32, "ExternalInput")
        self.c_d = dr("c", [128, KD], F32, "ExternalInput")
        self.pos_d = dr("pos", [1, T], I32, "ExternalInput")
        self.cst_d = dr("consts", [128, 640], F32, "ExternalInput")
        self.w_ada = dr("w_ada", [L, D, 6 * D], F32, "ExternalInput")
        self.par_d = dr("par", [128, L, 8 * KD + 16], F32, "ExternalInput")
        self.cw_d = dr("cw", [128, L, 3 * cfg.GH, 4], F32, "ExternalInput")
        self.w_in = dr("w_in", [L, D, cfg.INW], F32, "ExternalInput")
        self.w_uq = dr("w_uq", [L, cfg.QL, cfg.H * 192], F32, "ExternalInput")
        self.w_ukv = dr("w_ukv", [L, cfg.KVL, cfg.H * 256], F32, "ExternalInput")
        self.w_o_mla = dr("w_o_mla", [L, cfg.H * 128, D], F32, "ExternalInput")
        self.A_log = dr("A_log", [L, 1, cfg.GH], F32, "ExternalInput")
        self.dt_bias = dr("dt_bias", [L, 1, cfg.GH], F32, "ExternalInput")
        self.w_o_gdn = dr("w_o_gdn", [L, cfg.GH * 128, D], F32, "ExternalInput")
        self.w_o = dr("w_o", [L, D, D], F32, "ExternalInput")
        self.w_gate_up = dr("w_gate_up", [L, D, 2 * cfg.DFF], F32, "ExternalInput")
        self.w_down = dr("w_down", [L, cfg.DFF, D], F32, "ExternalInput")
        self.final_norm = dr("final_norm", [128, KD], F32, "ExternalInput")
        self.out_d = dr("out", [KD, 128, T], F32, "ExternalOutput")
        self.xT = dr("s_xT", [KD, 128, T], F32, "Internal")
        self.s_lat = dr("s_lat", [cfg.NLq + cfg.NLk, 128, T], F32, "Internal")
        self.s_kpe = dr("s_kpe", [2, 64, T], F32, "Internal")
        self.s_qkv = dr("s_qkv", [3 * cfg.GH, 128, T], F32, "Internal")
        self.s_z = dr("s_z", [cfg.GH, 128, T], F32, "Internal")
        self.s_ba = dr("s_ba", [2 * cfg.GH, T], F32, "Internal")
        self.s_gate = dr("s_gate", [2 * KD, 128, T], BF16, "Internal")
        self.s_ma = dr("s_ma", [KD, 128, T], BF16, "Internal")
        self.xTB = {}
        self.outB = Buf("out")

        with ExitStack() as top:
            self.mk = MK(nc, top)
            self.scopes = [top]
            mk = self.mk
            self.ps = [top.enter_context(nc.psum_tensor(f"ps{i}", [128, 512], F32)) for i in range(8)]
            self.pb = [Buf(f"ps{i}") for i in range(8)]
            stop = getattr(self, "stop_after", None)
            order = ["globals", "x0", "ada", "rope", "norm0", "proj", "mla", "gdn", "norm1", "ffn", "final"]
            lim = order.index(stop) if stop else len(order)
            on = lambda nm: order.index(nm) <= lim
            self.setup_globals()
            if on("x0"):
                self.phase_x0()
            if on("ada"):
                self.phase_ada()
            if on("rope"):
                self.phase_rope()
            for l in range(self.nl):
                if on("norm0"):
                    self.push()
                    hT = self.sb([128, KD, T], BF16, "hT")
                    hB = Buf("hT")
                    self.phase_norm(l, 0, hT, hB)
                    if on("proj"):
                        self.phase_proj(l, hT, hB)
                    self.pop()
                if on("mla"):
                    self.phase_mla(l)
                if on("gdn"):
                    self.phase_gdn(l)
                if on("norm1"):
                    self.push()
                    hT = self.sb([128, KD, T], BF16, "hT2")
                    hB = Buf("hT2")
                    self.phase_norm(l, 1, hT, hB)
                    if on("ffn"):
                        self.phase_ffn(l, hT, hB)
                    self.pop()
            if on("final"):
                self.phase_final()
            mk._wait(mk.sp, [self.outB.w] + list(self.outB.r.values()))
            mk.barrier()
            self.mark('end')
            self.stats = {e.name: (e.nins, e.nwait) for e in mk.engs}
        return nc

    def setup_globals(self):
        self.mark('setup_globals')
        cfg, mk = self.cfg, self.mk
        KD, L = cfg.KD, cfg.L
        self.cst = self.sb([128, 640], F32, "cst")
        self.cstB = Buf("cst")
        mk.dma(mk.sp, self.cst[:], self.cst_d, writes=[self.cstB])
        c = self.cst
        self.identF = c[:, 0:128]
        self.U64 = c[0:64, 128:192]
        self.SL64 = c[0:64, 192:256]
        self.LINC = c[0:64, 256:320]
        self.onesF = c[:, 320:448]
        self.invf = c[0:64, 576:577]
        self.sgn = c[0:64, 577:578]
        self.cb = self.sb([128, 384], BF16, "cb")
        self.cbB = Buf("cb")
        self.identB = self.cb[:, 0:128]
        self.onesB = self.cb[:, 128:256]
        self.causB = self.cb[:, 256:384]
        self.I(mk.dve, "tensor_copy", self.cb[:, 0:128], c[:, 0:128], R=[self.cstB], W=[self.cbB])
        self.I(mk.dve, "tensor_copy", self.cb[:, 128:256], c[:, 320:448], R=[self.cstB], W=[self.cbB])
        self.I(mk.dve, "tensor_copy", self.cb[:, 256:384], c[:, 448:576], R=[self.cstB], W=[self.cbB])
        self.cc = self.sb([128, 4], F32, "cc")
        self.ccB = Buf("cc")
        self.I(mk.dve, "memset", self.cc[:, 0:1], cfg.EPS, W=[self.ccB])
        self.I(mk.dve, "memset", self.cc[:, 1:2], 1.0, W=[self.ccB])
        self.I(mk.dve, "memset", self.cc[:, 2:3], 0.0, W=[self.ccB])
        self.epsc = self.cc[:, 0:1]
        self.onec = self.cc[:, 1:2]
        self.par = self.sb([128, L, 8 * KD + 16], F32, "par")
        self.parB = Buf("par")
        self.mod = self.sb([128, L, 8 * KD], F32, "mod")
        self.modB = Buf("mod")
        self.cw = self.sb([128, L, 3 * cfg.GH, 4], F32, "cw")
        self.cwB = Buf("cw")
        self.fn = self.sb([128, KD], F32, "fn")
        self.fnB = Buf("fn")
        self.cact = self.sb([128, KD], BF16, "cact")
        self.cactB = Buf("cact")
        self.grow = self.sb([64, L, 2, cfg.GH], F32, "grow")
        self.growB = Buf("grow")
        mk.dma(mk.sp, self.par[:], self.par_d, writes=[self.parB])
        mk.dma(mk.sp, self.cw[:], self.cw_d, writes=[self.cwB])
        mk.dma(mk.sp, self.fn[:], self.final_norm, writes=[self.fnB])
        for l in range(L):
            mk.dma(mk.sp, self.grow[:, l, 0, :], self.A_log[l].partition_broadcast(64), writes=[self.growB])
            mk.dma(mk.sp, self.grow[:, l, 1, :], self.dt_bias[l].partition_broadcast(64), writes=[self.growB])
            self.ACT(self.grow[:, l, 0, :], self.grow[:, l, 0, :], AF.Exp, R=[self.growB], W=[self.growB])
            self.I(mk.dve, "tensor_scalar", self.grow[:, l, 0, :], self.grow[:, l, 0, :], -1.0, None, op0=ALU.mult,
                   R=[self.growB], W=[self.growB])
        self.push()
        ctmp = self.sb([128, KD], F32, "ctmp")
        ctB = Buf("ctmp")
        mk.dma(mk.sp, ctmp[:], self.c_d, writes=[ctB])
        self.ACT(self.cact[:, :], ctmp[:, :], AF.Silu, R=[ctB], W=[self.cactB])
        self.pop()
        self.o_nmix, self.o_nffn, self.o_bada, self.o_qan, self.o_kvan, self.o_gn = 0, KD, 2 * KD, 8 * KD, 8 * KD + 4, 8 * KD + 8

    def phase_x0(self):
        self.mark('phase_x0')
        cfg, mk = self.cfg, self.mk
        for k in range(cfg.KD):
            mk.dma(mk.sp, self.xT[k], self.x_d[k])
        mk.barrier()

    def phase_ada(self):
        self.mark('phase_ada')
        cfg, mk = self.cfg, self.mk
        KD = cfg.KD
        self.push()
        slots = Ring([(self.sb([128, KD, 512], BF16, "wada"), Buf("wada")) for _ in range(3)])
        psA, psAB = self.ps[0], self.pb[0]
        for l in range(self.nl):
            nblk = 6 * cfg.D // 512
            for jb in range(nblk):
                sl, slB = slots.next()
                mk.dma(mk.pool, sl[:], self.w_ada[l][:, jb * 512:(jb + 1) * 512].rearrange("(k p) n -> p k n", p=128), writes=[slB])
                for jj in range(4):
                    j = jb * 4 + jj
                    for k in range(KD):
                        self.MM(psA[:, j:j + 1], sl[:, k, jj * 128:(jj + 1) * 128], self.cact[:, k:k + 1], k == 0, k == KD - 1,
                                R=[slB, self.cactB], W=[psAB], sig=(k == KD - 1))
            m = self.mod
            self.I(mk.dve, "tensor_tensor", m[:, l, 0:6 * KD], psA[:, 0:6 * KD], self.par[:, l, self.o_bada:self.o_bada + 6 * KD],
                   op=ALU.add, R=[psAB, self.parB], W=[self.modB])
            self.I(mk.dve, "scalar_tensor_tensor", m[:, l, 6 * KD:7 * KD], m[:, l, KD:2 * KD], 1.0, self.par[:, l, 0:KD],
                   op0=ALU.add, op1=ALU.mult, R=[self.modB, self.parB], W=[self.modB])
            self.I(mk.dve, "scalar_tensor_tensor", m[:, l, 7 * KD:8 * KD], m[:, l, 4 * KD:5 * KD], 1.0, self.par[:, l, KD:2 * KD],
                   op0=ALU.add, op1=ALU.mult, R=[self.modB, self.parB], W=[self.modB])
        self.pop()

    def phase_rope(self):
        self.mark('phase_rope')
        cfg, mk = self.cfg, self.mk
        T = cfg.T
        self.COS = self.sb([64, T], F32, "cos")
        self.SINS = self.sb([64, T], F32, "sins")
        self.ropeB = Buf("rope")
        self.push()
        pi_ = self.sb([64, T], I32, "posi")
        t0 = self.sb([64, T], F32, "t0")
        t1 = self.sb([64, T], F32, "t1")
        t2 = self.sb([64, T], F32, "t2")
        B = Buf("ropetmp")
        mk.dma(mk.sp, pi_[:], self.pos_d.partition_broadcast(64), writes=[B])
        V = lambda name, *a, **kw: self.I(mk.dve, name, *a, R=[B, self.cstB], W=[B], **kw)
        V("tensor_copy", t0[:], pi_[:])
        V("tensor_scalar", t0[:], t0[:], self.invf, None, op0=ALU.mult)
        V("tensor_scalar", t0[:], t0[:], float(1.0 / (2 * np.pi)), None, op0=ALU.mult)
        for which in range(2):
            dst = self.SINS if which == 0 else self.COS
            if which == 1:
                V("tensor_scalar", t0[:], t0[:], 0.25, None, op0=ALU.add)
            V("tensor_copy", pi_[:], t0[:])
            V("tensor_copy", t1[:], pi_[:])
            V("tensor_tensor", t1[:], t0[:], t1[:], op=ALU.subtract)
            V("tensor_scalar", t2[:], t1[:], 0.5, None, op0=ALU.is_ge)
            V("tensor_tensor", t1[:], t1[:], t2[:], op=ALU.subtract)
            V("tensor_scalar", t2[:], t1[:], -0.5, None, op0=ALU.is_lt)
            V("tensor_tensor", t1[:], t1[:], t2[:], op=ALU.add)
            self.ACT(dst[:], t1[:], AF.Sin, R=[B], W=[self.ropeB], scale=float(2 * np.pi))
        self.I(mk.dve, "tensor_scalar", self.SINS[:], self.SINS[:], self.sgn, None, op0=ALU.mult, R=[self.ropeB, self.cstB], W=[self.ropeB])
        self.pop()

    def phase_norm(self, l, which, hT, hB):
        self.mark(f'norm{which}_{l}')
        cfg, mk = self.cfg, self.mk
        KD, NB = cfg.KD, cfg.NB
        a_off = (6 * KD) if which == 0 else (7 * KD)
        sh_off = 0 if which == 0 else 3 * KD
        self.push()
        xb = Ring([(self.sb([128, KD, NB], F32, "xb"), Buf("xb")) for _ in range(2)])
        sq = Ring([(self.sb([128, KD, NB], BF16, "sq"), Buf("sq")) for _ in range(2)])
        rs = Ring([(self.sb([128, NB], F32, "rs"), Buf("rs")) for _ in range(2)])
        tmp = Ring([(self.sb([128, NB], F32, "tmp"), Buf("tmp")) for _ in range(4)])
        banks = Ring([0, 1])
        xTv = self.xT.rearrange("k p t -> p k t")
        for blk in range(cfg.T // NB):
            sl = slice(blk * NB, (blk + 1) * NB)
            x_, xB = xb.next()
            rd = [self.xTB[key] for key in self.xTB]
            mk.dma(mk.sp, x_[:], xTv[:, :, sl], reads=rd, writes=[xB])
            s_, sB = sq.next()
            self.ACT(s_[:], x_[:], AF.Square, R=[xB], W=[sB])
            b = banks.next()
            for k in range(KD):
                self.MM(self.ps[b][:, 0:NB], self.onesB, s_[:, k, :], k == 0, k == KD - 1, R=[self.cbB, sB], W=[self.pb[b]], sig=(k == KD - 1))
            r_, rB = rs.next()
            self.ACT(r_[:], self.ps[b][:, 0:NB], AF.Sqrt, R=[self.pb[b], self.ccB], W=[rB], bias=self.epsc, scale=1.0 / cfg.D)
            self.I(mk.dve, "reciprocal", r_[:], r_[:], R=[rB], W=[rB])
            for k in range(KD):
                t_, tB = tmp.next()
                self.I(mk.dve, "scalar_tensor_tensor", t_[:], x_[:, k, :], self.mod[:, l, a_off + k:a_off + k + 1], r_[:],
                       op0=ALU.mult, op1=ALU.mult, R=[xB, self.modB, rB], W=[tB])
                self.ACT(hT[:, k, sl], t_[:], AF.Identity, R=[tB, self.modB], W=[hB], bias=self.mod[:, l, sh_off + k:sh_off + k + 1])
        self.pop()

    def stream_mm(self, W2d, blocks, rhs, rhsB, KC, evac, slots, extraR=(), pre=None):
        cfg, mk = self.cfg, self.mk
        TB, NTB = cfg.TB, cfg.NTB
        nsets = 8 // NTB
        Wv = W2d.rearrange("(k p) n -> p k n", p=128)
        flat = [g[2] for _, groups in blocks for g in groups]
        gi = 0
        if pre is not None and flat:
            pre(flat[0])
        for segs, groups in blocks:
            sl, slB = slots.next()
            off = 0
            for (c0, n) in segs:
                mk.dma(mk.pool, sl[:, 0:KC, off:off + n], Wv[:, :, c0:c0 + n], writes=[slB])
                off += n
            for (goff, M, gid) in groups:
                gi += 1
                if pre is not None and gi < len(flat):
                    pre(flat[gi])
                s = self._pset
                self._pset = (self._pset + 1) % nsets
                for k in range(KC):
                    for tb in range(NTB):
                        b = s * NTB + tb
                        self.MM(self.ps[b][0:M, 0:TB], sl[:, k, goff:goff + M], rhs[:, k, tb * TB:(tb + 1) * TB], k == 0, k == KC - 1,
                                R=[slB, rhsB] + list(extraR), W=[self.pb[b]], sig=(k == KC - 1))
                for tb in range(NTB):
                    b = s * NTB + tb
                    evac(gid, M, tb, self.ps[b][0:M, 0:TB], self.pb[b])

    def col_blocks(self, c_lo, ncols, gid0=0):
        blocks = []
        g = gid0
        for c0 in range(c_lo, c_lo + ncols, 512):
            n = min(512, c_lo + ncols - c0)
            groups = []
            for o in range(0, n, 128):
                groups.append((o, min(128, n - o), g))
                g += 1
            blocks.append(([(c0, n)], groups))
        return blocks

    def phase_proj(self, l, hT, hB):
        self.mark(f'proj_{l}')
        cfg, mk = self.cfg, self.mk
        KD, TB = cfg.KD, cfg.TB
        self._pset = 0
        slots = Ring([(self.sb([128, KD, 512], BF16, "win"), Buf("win")) for _ in range(3)])
        stg = Ring([(self.sb([128, TB], F32, "pstg"), Buf("pstg")) for _ in range(6)])
        stgb = Ring([(self.sb([128, TB], BF16, "pstgb"), Buf("pstgb")) for _ in range(4)])
        flip = [0]

        def mk_evac(dst_fn, sigm=False):
            def evac(gid, M, tb, psap, psB):
                if sigm:
                    st, stB = stgb.next()
                    self.ACT(st[0:M, :], psap, AF.Sigmoid, R=[psB], W=[stB])
                else:
                    st, stB = stg.next()
                    flip[0] ^= 1
                    if flip[0]:
                        self.ACT(st[0:M, :], psap, AF.Copy, R=[psB], W=[stB])
                    else:
                        self.I(mk.dve, "tensor_copy", st[0:M, :], psap, R=[psB], W=[stB])
                mk.dma(mk.sp, dst_fn(gid, M, tb), st[0:M, :], reads=[stB])
            return evac
        W = self.w_in[l]
        tsl = lambda tb: slice(tb * TB, (tb + 1) * TB)
        self.stream_mm(W, self.col_blocks(0, cfg.QL + cfg.KVL), hT, hB, KD,
                       mk_evac(lambda g, M, tb: self.s_lat[g, :, tsl(tb)]), slots)
        o = cfg.o_kpe
        blocks = [([(o, 64), (o + 32, 32), (o, 32)], [(0, 64, 0), (64, 64, 1)])]
        self.stream_mm(W, blocks, hT, hB, KD, mk_evac(lambda g, M, tb: self.s_kpe[g, :, tsl(tb)]), slots)
        self.stream_mm(W, self.col_blocks(cfg.o_q, 3 * cfg.GH * 128), hT, hB, KD,
                       mk_evac(lambda g, M, tb: self.s_qkv[g, :, tsl(tb)]), slots)
        self.stream_mm(W, self.col_blocks(cfg.o_z, cfg.GH * 128), hT, hB, KD,
                       mk_evac(lambda g, M, tb: self.s_z[g, :, tsl(tb)]), slots)
        blocks = [([(cfg.o_b, 2 * cfg.GH)], [(0, 2 * cfg.GH, 0)])]
        self.stream_mm(W, blocks, hT, hB, KD, mk_evac(lambda g, M, tb: self.s_ba[:, tsl(tb)]), slots)
        self.stream_mm(W, self.col_blocks(cfg.o_g, 2 * cfg.D), hT, hB, KD,
                       mk_evac(lambda g, M, tb: self.s_gate[g, :, tsl(tb)], sigm=True), slots)

    def phase_mla(self, l):
        self.mark(f'mla_{l}')
        cfg, mk = self.cfg, self.mk
        T, TB, NTB, H = cfg.T, cfg.TB, cfg.NTB, cfg.H
        NLq, NLk = cfg.NLq, cfg.NLk
        NL = NLq + NLk
        self.push()
        cqn = self.sb([128, NLq, T], BF16, "cqn"); cqnB = Buf("cqn")
        ckvn = self.sb([128, NLk, T], BF16, "ckvn"); ckvnB = Buf("ckvn")
        kpeR = self.sb([64, T], BF16, "kpeR"); kpeB = Buf("kpeR")
        aT = self.sb([128, H, T], BF16, "aT"); aTB = Buf("aT")
        self.push()
        lat = Ring([(self.sb([128, NL, TB], F32, "lat"), Buf("lat")) for _ in range(2)])
        sq = Ring([(self.sb([128, NL, TB], BF16, "lsq"), Buf("lsq")) for _ in range(1)])
        rq = Ring([(self.sb([128, 2, TB], F32, "lrs"), Buf("lrs")) for _ in range(2)])
        latv = self.s_lat.rearrange("k p t -> p k t")
        for tb in range(NTB):
            sl = slice(tb * TB, (tb + 1) * TB)
            la, laB = lat.next()
            mk.dma(mk.sp, la[:], latv[:, :, sl], writes=[laB])
            s_, sB = sq.next()
            self.ACT(s_[:], la[:], AF.Square, R=[laB], W=[sB])
            r_, rB = rq.next()
            for which, (k0, nk, dim) in enumerate([(0, NLq, cfg.QL), (NLq, NLk, cfg.KVL)]):
                b = which
                for k in range(nk):
                    self.MM(self.ps[b][:, 0:TB], self.onesB, s_[:, k0 + k, :], k == 0, k == nk - 1, R=[self.cbB, sB], W=[self.pb[b]], sig=(k == nk - 1))
                self.ACT(r_[:, which, :], self.ps[b][:, 0:TB], AF.Sqrt, R=[self.pb[b], self.ccB], W=[rB], bias=self.epsc, scale=1.0 / dim)
            self.I(mk.dve, "reciprocal", r_[:], r_[:], R=[rB], W=[rB])
            for k in range(NLq):
                self.I(mk.dve, "scalar_tensor_tensor", cqn[:, k, sl], la[:, k, :], self.par[:, l, self.o_qan + k:self.o_qan + k + 1], r_[:, 0, :],
                       op0=ALU.mult, op1=ALU.mult, R=[laB, self.parB, rB], W=[cqnB])
            for k in range(NLk):
                self.I(mk.dve, "scalar_tensor_tensor", ckvn[:, k, sl], la[:, NLq + k, :], self.par[:, l, self.o_kvan + k:self.o_kvan + k + 1], r_[:, 1, :],
                       op0=ALU.mult, op1=ALU.mult, R=[laB, self.parB, rB], W=[ckvnB])
        kp = self.sb([64, 2, T], F32, "kp"); kpB = Buf("kp")
        mk.dma(mk.sp, kp[:], self.s_kpe.rearrange("g p t -> p g t"), writes=[kpB])
        self.I(mk.dve, "tensor_tensor", kp[:, 0, :], kp[:, 0, :], self.COS[:], op=ALU.mult, R=[kpB, self.ropeB], W=[kpB])
        self.I(mk.dve, "tensor_tensor", kp[:, 1, :], kp[:, 1, :], self.SINS[:], op=ALU.mult, R=[kpB, self.ropeB], W=[kpB])
        self.I(mk.dve, "tensor_tensor", kpeR[:], kp[:, 0, :], kp[:, 1, :], op=ALU.add, R=[kpB], W=[kpeB])
        self.pop()
        wq = self.sb([128, NLq, H * 192], BF16, "wq"); wqB = Buf("wq")
        wqs = self.sb([128, NLq, H, 64], BF16, "wqs"); wqsB = Buf("wqs")
        wkv = self.sb([128, NLk, H * 256], BF16, "wkv"); wkvB = Buf("wkv")
        mk.dma(mk.pool, wq[:], self.w_uq[l].rearrange("(k p) n -> p k n", p=128), writes=[wqB])
        wq4 = self.w_uq[l].rearrange("(k p) (h e) -> p k h e", p=128, e=192)
        for k in range(NLq):
            mk.dma(mk.pool, wqs[:, k, :, 0:32], wq4[:, k, :, 160:192], writes=[wqsB])
            mk.dma(mk.pool, wqs[:, k, :, 32:64], wq4[:, k, :, 128:160], writes=[wqsB])
        mk.dma(mk.pool, wkv[:], self.w_ukv[l].rearrange("(k p) n -> p k n", p=128), writes=[wkvB])
        self.mark(f'mla_attn_{l}')
        knT = Ring([(self.sb([128, T], BF16, "knT"), Buf("knT")) for _ in range(2)])
        qnT = Ring([(self.sb([128, T], BF16, "qnT"), Buf("qnT")) for _ in range(2)])
        qpR = Ring([(self.sb([64, T], BF16, "qpR"), Buf("qpR")) for _ in range(2)])
        vtk = Ring([(self.sb([128, cfg.NT128, 128], BF16, "vtk"), Buf("vtk")) for _ in range(2)])
        rt = Ring([(self.sb([64, 2, TB], F32, "rt"), Buf("rt")) for _ in range(2)])
        Pt = Ring([(self.sb([128, TB], BF16, "Pt"), Buf("Pt")) for _ in range(4)])
        rec = Ring([(self.sb([128, TB], F32, "rec"), Buf("rec")) for _ in range(2)])
        pj = Ring([7, 2, 3, 4, 5])
        sc = Ring([0, 1, 6])
        od = Ring([(2, 3), (4, 5)])
        scale = float(192 ** -0.5)
        KT = TB // 128
        for h in range(H):
            kn, knB = knT.next(); qn, qnB = qnT.next(); qp, qpB = qpR.next(); vt, vtB = vtk.next()
            for tb in range(NTB):
                sl = slice(tb * TB, (tb + 1) * TB)
                b = pj.next()
                for k in range(NLk):
                    self.MM(self.ps[b][:, 0:TB], wkv[:, k, h * 256:h * 256 + 128], ckvn[:, k, sl], k == 0, k == NLk - 1, R=[wkvB, ckvnB], W=[self.pb[b]], sig=(k == NLk - 1))
                self.ACT(kn[:, sl], self.ps[b][:, 0:TB], AF.Copy, R=[self.pb[b]], W=[knB])
                b = pj.next()
                for k in range(NLq):
                    self.MM(self.ps[b][:, 0:TB], wq[:, k, h * 192:h * 192 + 128], cqn[:, k, sl], k == 0, k == NLq - 1, R=[wqB, cqnB], W=[self.pb[b]], sig=(k == NLq - 1))
                self.I(mk.dve, "tensor_copy", qn[:, sl], self.ps[b][:, 0:TB], R=[self.pb[b]], W=[qnB])
                b1 = pj.next()
                for k in range(NLq):
                    self.MM(self.ps[b1][0:64, 0:TB], wq[:, k, h * 192 + 128:h * 192 + 192], cqn[:, k, sl], k == 0, k == NLq - 1, R=[wqB, cqnB], W=[self.pb[b1]], sig=(k == NLq - 1))
                b2 = pj.next()
                for k in range(NLq):
                    self.MM(self.ps[b2][0:64, 0:TB], wqs[:, k, h, :], cqn[:, k, sl], k == 0, k == NLq - 1, R=[wqsB, cqnB], W=[self.pb[b2]], sig=(k == NLq - 1))
                r_, rB = rt.next()
                self.I(mk.dve, "tensor_tensor", r_[:, 0, :], self.ps[b1][0:64, 0:TB], self.COS[:, sl], op=ALU.mult, R=[self.pb[b1], self.ropeB], W=[rB])
                self.I(mk.dve, "tensor_tensor", r_[:, 1, :], self.ps[b2][0:64, 0:TB], self.SINS[:, sl], op=ALU.mult, R=[self.pb[b2], self.ropeB], W=[rB])
                self.I(mk.dve, "tensor_tensor", qp[:, sl], r_[:, 0, :], r_[:, 1, :], op=ALU.add, R=[rB], W=[qpB])
            for i0 in range(0, cfg.NT128, 4):
                n4 = min(4, cfg.NT128 - i0)
                b = pj.next()
                for ii in range(n4):
                    i = i0 + ii
                    for k in range(NLk):
                        self.MM(self.ps[b][:, ii * 128:(ii + 1) * 128], ckvn[:, k, i * 128:(i + 1) * 128], wkv[:, k, h * 256 + 128:h * 256 + 256],
                                k == 0, k == NLk - 1, R=[wkvB, ckvnB], W=[self.pb[b]], sig=(ii == n4 - 1 and k == NLk - 1))
                self.ACT(vt[:, i0:i0 + n4, :], self.ps[b][:, 0:n4 * 128].rearrange("p (i d) -> p i d", i=n4), AF.Copy, R=[self.pb[b]], W=[vtB])
            for qb in range(NTB):
                bo, bd = od.next()
                nkt = (qb + 1) * KT
                pend = []

                def emit_pv(item):
                    kt, q0, P_, PB = item
                    self.MM(self.ps[bo][:, q0:TB], vt[:, kt, :], P_[:, q0:TB], kt == 0, kt == nkt - 1, R=[vtB, PB], W=[self.pb[bo]], sig=False)
                    self.MM(self.ps[bd][:, q0:TB], self.onesB, P_[:, q0:TB], kt == 0, kt == nkt - 1, R=[self.cbB, PB], W=[self.pb[bd]], sig=True)
                for kt in range(nkt):
                    j = kt - qb * KT
                    q0 = max(0, j) * 128
                    bs = sc.next()
                    qsl = slice(qb * TB + q0, (qb + 1) * TB)
                    ksl = slice(kt * 128, (kt + 1) * 128)
                    self.MM(self.ps[bs][:, q0:TB], kn[:, ksl], qn[:, qsl], True, False, R=[knB, qnB], W=[self.pb[bs]], sig=False)
                    self.MM(self.ps[bs][:, q0:TB], kpeR[:, ksl], qp[:, qsl], False, True, R=[kpeB, qpB], W=[self.pb[bs]], sig=True)
                    P_, PB = Pt.next()
                    self.ACT(P_[:, q0:TB], self.ps[bs][:, q0:TB], AF.Exp, R=[self.pb[bs]], W=[PB], scale=scale)
                    if j >= 0:
                        self.I(mk.dve, "tensor_tensor", P_[:, q0:q0 + 128], P_[:, q0:q0 + 128], self.causB, op=ALU.mult, R=[PB, self.cbB], W=[PB])
                    pend.append((kt, q0, P_, PB))
                    if len(pend) > 2:
                        emit_pv(pend.pop(0))
                while pend:
                    emit_pv(pend.pop(0))
                rc, rcB = rec.next()
                self.I(mk.dve, "reciprocal", rc[:], self.ps[bd][:, 0:TB], R=[self.pb[bd]], W=[rcB])
                self.I(mk.dve, "tensor_tensor", aT[:, h, qb * TB:(qb + 1) * TB], self.ps[bo][:, 0:TB], rc[:], op=ALU.mult, R=[self.pb[bo], rcB], W=[aTB])
        self.mark(f'mla_out_{l}')
        self._pset = 0
        slots = Ring([(self.sb([128, H, 512], BF16, "womla"), Buf("womla")) for _ in range(2)])
        gt = Ring([(self.sb([128, TB], BF16, "gat"), Buf("gat")) for _ in range(3 * NTB)])
        mo = Ring([(self.sb([128, TB], BF16, "mo"), Buf("mo")) for _ in range(4)])
        pend5 = {}

        def pre5(g):
            for tb in range(NTB):
                g_, gB = gt.next()
                mk.dma(mk.sp, g_[:], self.s_gate[g, :, tb * TB:(tb + 1) * TB], writes=[gB])
                pend5[(g, tb)] = (g_, gB)

        def evac(g, M, tb, psap, psB):
            sl = slice(tb * TB, (tb + 1) * TB)
            g_, gB = pend5.pop((g, tb))
            m_, mB = mo.next()
            self.I(mk.dve, "tensor_tensor", m_[:], psap, g_[:], op=ALU.mult, R=[psB, gB], W=[mB])
            mk.dma(mk.sp, self.s_ma[g, :, sl], m_[:], reads=[mB])
        self.stream_mm(self.w_o_mla[l], self.col_blocks(0, cfg.D), aT, aTB, H, evac, slots, pre=pre5)
        self.pop()

    def phase_gdn(self, l):
        self.mark(f'gdn_{l}')
        cfg, mk = self.cfg, self.mk
        T, TB, NTB, GH, HG, NCH, KD = cfg.T, cfg.TB, cfg.NTB, cfg.GH, cfg.HG, cfg.NCH, cfg.KD
        V = mk.dve
        self.push()
        ogT = self.sb([128, GH, T], BF16, "ogT"); ogB = Buf("ogT")
        self.push()
        gsh = [64, NCH, GH]
        beta = self.sb(gsh, F32, "beta"); g_ = self.sb(gsh, F32, "g"); expG = self.sb(gsh, F32, "expG")
        expGLmG = self.sb(gsh, F32, "expGLmG"); c1 = self.sb(gsh, F32, "c1"); nbeta = self.sb(gsh, F32, "nbeta")
        expGL = self.sb([128, NCH, GH], F32, "expGL")
        gB = Buf("gates")
        self.push()
        ba = self.sb([2 * GH, T], F32, "ba"); baB = Buf("ba")
        batok = self.sb([64, NCH, 2 * GH], F32, "batok"); btB = Buf("batok")
        mk.dma(mk.sp, ba[:], self.s_ba, writes=[baB])
        b = 0
        for n in range(NCH):
            self.TR(self.ps[b][0:64, n * 2 * GH:(n + 1) * 2 * GH], ba[:, n * 64:(n + 1) * 64], self.identF[0:2 * GH, 0:2 * GH],
                    R=[baB, self.cstB], W=[self.pb[b]], sig=(n == NCH - 1))
        self.I(V, "tensor_copy", batok[:], self.ps[b][0:64, 0:NCH * 2 * GH].rearrange("p (n g) -> p n g", n=NCH), R=[self.pb[b]], W=[btB])
        self.ACT(beta[:], batok[:, :, 0:GH], AF.Sigmoid, R=[btB], W=[gB])
        self.I(V, "tensor_tensor", g_[:], batok[:, :, GH:2 * GH], self.grow[:, l, 1, :].unsqueeze(1).to_broadcast(gsh), op=ALU.add, R=[btB, self.growB], W=[gB])
        self.ACT(g_[:], g_[:], AF.Exp, R=[gB], W=[gB])
        self.ACT(g_[:], g_[:], AF.Ln, R=[gB, self.ccB], W=[gB], bias=self.onec[0:64, :])
        self.I(V, "tensor_tensor", g_[:], g_[:], self.grow[:, l, 0, :].unsqueeze(1).to_broadcast(gsh), op=ALU.mult, R=[gB, self.growB], W=[gB])
        for (lhs, M, dst, bb) in [(self.U64, 64, expG, 1), (self.SL64, 64, expGLmG, 2), (self.onesF[0:64, :], 128, expGL, 3)]:
            for n in range(NCH):
                self.MM(self.ps[bb][0:M, n * GH:(n + 1) * GH], lhs, g_[:, n, :], True, True, R=[self.cstB, gB], W=[self.pb[bb]], sig=(n == NCH - 1))
            self.ACT(dst[:], self.ps[bb][0:M, 0:NCH * GH].rearrange("p (n g) -> p n g", n=NCH), AF.Exp, R=[self.pb[bb]], W=[gB])
        self.I(V, "tensor_tensor", c1[:], beta[:], expG[:], op=ALU.mult, R=[gB], W=[gB])
        self.I(V, "tensor_scalar", nbeta[:], beta[:], -1.0, None, op0=ALU.mult, R=[gB], W=[gB])
        self.pop()
        gstop = getattr(self, "gstop", 9)
        if gstop == 0:
            self.pop(); self.pop()
            return
        for gp in range(GH // HG):
            hs = slice(gp * HG, (gp + 1) * HG)
            self.push()
            qT = self.sb([128, HG, T], BF16, "gqT"); kT = self.sb([128, HG, T], BF16, "gkT"); vT = self.sb([128, HG, T], BF16, "gvT")
            qkvB = Buf("gqkv")
            self.mark(f'gdn_g1_{l}_{gp}')
            self.push()
            ub = Ring([(self.sb([128, T], F32, "u"), Buf("u")) for _ in range(2)])
            ab = Ring([(self.sb([128, T], F32, "acc"), Buf("acc")) for _ in range(2)])
            sqb = Ring([(self.sb([128, T], BF16, "csq"), Buf("csq")) for _ in range(2)])
            rsb = Ring([(self.sb([128, TB], F32, "crs"), Buf("crs")) for _ in range(3)])
            banks = Ring([0, 1, 2, 3, 4, 5, 6, 7])
            def g1_front(kind, dstT, hh):
                ci = kind * GH + gp * HG + hh
                u, uB = ub.next()
                mk.dma(mk.sp, u[:], self.s_qkv[ci], writes=[uB])
                a, aB = ab.next()
                w = lambda j: self.cw[:, l, ci, j:j + 1]
                self.I(V, "tensor_scalar", a[:], u[:], w(3), None, op0=ALU.mult, R=[uB, self.cwB], W=[aB])
                for d in (1, 2, 3):
                    self.I(V, "scalar_tensor_tensor", a[:, d:T], u[:, 0:T - d], w(3 - d), a[:, d:T], op0=ALU.mult, op1=ALU.add,
                           R=[uB, self.cwB, aB], W=[aB])
                if kind == 2:
                    self.ACT(dstT[:, hh, :], a[:], AF.Silu, R=[aB], W=[qkvB])
                    return None
                self.ACT(a[:], a[:], AF.Silu, R=[aB], W=[aB])
                s_, sB = sqb.next()
                self.ACT(s_[:], a[:], AF.Square, R=[aB], W=[sB])
                bl = []
                for tb in range(NTB):
                    sl = slice(tb * TB, (tb + 1) * TB)
                    b = banks.next()
                    self.MM(self.ps[b][:, 0:TB], self.onesB, s_[:, sl], True, True, R=[self.cbB, sB], W=[self.pb[b]], sig=True)
                    bl.append(b)

                def back():
                    for tb in range(NTB):
                        sl = slice(tb * TB, (tb + 1) * TB)
                        b = bl[tb]
                        r_, rB = rsb.next()
                        self.ACT(r_[:], self.ps[b][:, 0:TB], AF.Sqrt, R=[self.pb[b], self.ccB], W=[rB], bias=self.epsc)
                        self.I(V, "reciprocal", r_[:], r_[:], R=[rB], W=[rB])
                        self.I(V, "scalar_tensor_tensor", dstT[:, hh, sl], a[:, sl], float(128 ** -0.5) if kind == 0 else 1.0, r_[:],
                               op0=ALU.mult, op1=ALU.mult, R=[aB, rB], W=[qkvB])
                return back

            prev_back = None
            for kind, dstT in ((0, qT), (1, kT), (2, vT)):
                for hh in range(HG):
                    bk = g1_front(kind, dstT, hh)
                    if prev_back is not None:
                        prev_back()
                    prev_back = bk
            if prev_back is not None:
                prev_back()
            self.pop()
            if gstop == 1:
                self.pop()
                continue
            self.mark(f'gdn_g2_{l}_{gp}')
            self.push()
            W4 = HG * 64
            W8 = HG * 128
            S = self.sb([128, HG, 128], F32, "S"); SB_ = Buf("S")
            self.I(V, "memset", S[:], 0.0, W=[SB_])
            f4 = lambda nm, n: Ring([(self.sb([64, HG, 64], F32, nm), Buf(nm)) for _ in range(n)])
            isets = []
            for par_ in range(2):
                isets.append(dict(gU=f4("gU", 1), E=f4("E", 1), Es=f4("Es", 1), at=f4("att", 1), A=f4("A", 3), B=f4("Bm", 3), P=f4("P", 2),
                                  ainv=f4("ainv", 1), attT=f4("attT", 1)))
            ibanks = Ring([6, 7])
            f8 = lambda nm, n, dt=F32: Ring([(self.sb([64, HG, 128], dt, nm), Buf(nm)) for _ in range(n)])
            kd_r, vb_r, t_r, r_r, vn_r, o_r, on_r = f8("kdec", 2), f8("vb", 2), f8("t8", 2), f8("r8", 2), f8("vnew", 2), f8("o8", 2), f8("on", 2)
            ss_r = Ring([(self.sb([64, HG], F32, "ss"), Buf("ss")) for _ in range(2)])
            tS = self.sb([128, HG, 128], F32, "tS"); tSB = Buf("tS")
            fb = Ring([0, 1, 2, 3, 4, 5])
            I64 = self.identF[0:64, 0:64]
            bc4 = lambda ap2: ap2.unsqueeze(1).to_broadcast([64, HG, 64])
            results = {}

            cf_r = Ring([(self.sb([128, 3, HG, 64], F32, "cf"), Buf("cf")) for _ in range(3)])
            cf_cache = {}

            def chunk_f32(n):
                if n not in cf_cache:
                    c_, cB_ = cf_r.next()
                    csl_ = slice(n * 64, (n + 1) * 64)
                    self.ACT(c_[:, 0], qT[:, :, csl_], AF.Copy, R=[qkvB], W=[cB_])
                    self.I(V, "tensor_copy", c_[:, 1], kT[:, :, csl_], R=[qkvB], W=[cB_])
                    self.ACT(c_[:, 2], vT[:, :, csl_], AF.Copy, R=[qkvB], W=[cB_]) if True else self.I(V, "tensor_copy", c_[:, 2], vT[:, :, csl_], R=[qkvB], W=[cB_])
                    cf_cache[n] = (c_[:, 1], c_[:, 0], c_[:, 2], cB_)
                    if getattr(self, "debug", False) and n == 0 and l == 0 and gp == 0:
                        mk.dma(mk.sp, self.dram("dbg_cf", [128, 3, HG, 64], F32, "Internal"), c_[:], reads=[cB_])
                    cf_cache.pop(n - 3, None)
                return cf_cache[n]

            def half_(st_):
                b_ = ibanks.next()
                return self.ps[b_][0:64, 0:W4].rearrange("p (h c) -> p h c", h=HG), self.pb[b_]

            def inverse(n):
                csl = slice(n * 64, (n + 1) * 64)
                st_ = isets[n % 2]
                gU_r, E_r, Es_r, at_r, A_r, B_r, P_r, ainv, attT = (st_[k_] for k_ in ("gU", "E", "Es", "at", "A", "B", "P", "ainv", "attT"))
                half = lambda: half_(st_)
                gU, gUB = gU_r.next()
                self.I(V, "tensor_tensor", gU[:], bc4(self.U64), g_[:, n, hs].unsqueeze(2).to_broadcast([64, HG, 64]), op=ALU.mult, R=[self.cstB, gB], W=[gUB])
                pD, pDB = half()
                for hh in range(HG):
                    self.MM(pD[:, hh, :], gU[:, hh, :], self.SL64, True, True, R=[gUB, self.cstB], W=[pDB], sig=(hh == HG - 1))
                E, EB = E_r.next()
                self.ACT(E[:], pD, AF.Exp, R=[pDB], W=[EB])
                yield
                self.I(V, "tensor_tensor", E[:], E[:], bc4(self.LINC), op=ALU.mult, R=[EB, self.cstB], W=[EB])
                Es, EsB = Es_r.next()
                self.I(V, "tensor_tensor", Es[:], E[:], bc4(self.SL64), op=ALU.mult, R=[EB, self.cstB], W=[EsB])
                pK, pKB = half()
                pQ, pQB = half()
                kf, qf, vf, cfB = chunk_f32(n)
                for hh in range(HG):
                    self.MM(pK[:, hh, :], kf[:, hh, :], kf[:, hh, :], True, True, R=[cfB], W=[pKB], sig=(hh == HG - 1))
                for hh in range(HG):
                    self.MM(pQ[:, hh, :], qf[:, hh, :], kf[:, hh, :], True, True, R=[cfB], W=[pQB], sig=(hh == HG - 1))
                yield
                A, AB = A_r.next()
                self.I(V, "tensor_tensor", A[:], pK, Es[:], op=ALU.mult, R=[pKB, EsB], W=[AB])
                self.I(V, "tensor_tensor", A[:], A[:], nbeta[:, n, hs].unsqueeze(2).to_broadcast([64, HG, 64]), op=ALU.mult, R=[AB, gB], W=[AB])
                at, atB = at_r.next()
                self.I(V, "tensor_tensor", at[:], pQ, E[:], op=ALU.mult, R=[pQB, EB], W=[atB])
                if getattr(self, "debug", False) and n == 0 and l == 0 and gp == 0:
                    dk_ = self.sb([64, 2, HG, 64], F32, "dbgk"); dkB = Buf("dbgk")
                    self.I(V, "tensor_copy", dk_[:, 0], pK, R=[pKB], W=[dkB])
                    self.I(V, "tensor_copy", dk_[:, 1], pQ, R=[pQB], W=[dkB])
                    mk.dma(mk.sp, self.dram("dbg_KQ", [64, 2, HG, 64], F32, "Internal"), dk_[:], reads=[dkB])
                    mk.dma(mk.sp, self.dram("dbg_Es", [64, HG, 64], F32, "Internal"), Es[:], reads=[EsB])
                    mk.dma(mk.sp, self.dram("dbg_E", [64, HG, 64], F32, "Internal"), E[:], reads=[EB])
                    mk.dma(mk.sp, self.dram("dbg_N", [64, HG, 64], F32, "Internal"), A[:], reads=[AB])
                    mk.dma(mk.sp, self.dram("dbg_at", [64, HG, 64], F32, "Internal"), at[:], reads=[atB])
                pT1, pT1B = half()
                pT2, pT2B = half()
                for hh in range(HG):
                    self.TR(pT1[:, hh, :], A[:, hh, :], I64, R=[AB, self.cstB], W=[pT1B], sig=(hh == HG - 1))
                for hh in range(HG):
                    self.TR(pT2[:, hh, :], at[:, hh, :], I64, R=[atB, self.cstB], W=[pT2B], sig=(hh == HG - 1))
                yield
                Bm, BB = B_r.next()
                self.ACT(Bm[:], pT1, AF.Copy, R=[pT1B], W=[BB])
                aT_, aTB_ = attT.next()
                self.ACT(aT_[:], pT2, AF.Copy, R=[pT2B], W=[aTB_])
                P, PB = P_r.next()
                self.I(V, "tensor_tensor", P[:], Bm[:], bc4(I64), op=ALU.add, R=[BB, self.cstB], W=[PB])
                for lev in range(1, 6):
                    pB, pBB = half()
                    pA, pAB = half()
                    for hh in range(HG):
                        self.MM(pB[:, hh, :], A[:, hh, :], Bm[:, hh, :], True, True, R=[AB, BB], W=[pBB], sig=(hh == HG - 1))
                    for hh in range(HG):
                        self.MM(pA[:, hh, :], Bm[:, hh, :], A[:, hh, :], True, True, R=[AB, BB], W=[pAB], sig=(hh == HG - 1))
                    yield
                    A2, A2B = A_r.next()
                    B2, B2B = B_r.next()
                    self.ACT(A2[:], pA, AF.Copy, R=[pAB], W=[A2B])
                    if lev < 5:
                        self.I(V, "tensor_copy", B2[:], pB, R=[pBB], W=[B2B])
                    A, AB, Bm, BB = A2, A2B, B2, B2B
                    pP, pPB = half()
                    for hh in range(HG):
                        self.MM(pP[:, hh, :], A[:, hh, :], P[:, hh, :], True, True, R=[AB, PB], W=[pPB], sig=(hh == HG - 1))
                    yield
                    if lev < 5:
                        P2, P2B = P_r.next()
                    else:
                        P2, P2B = ainv.next()
                    self.I(V, "tensor_tensor", P2[:], pP, P[:], op=ALU.add, R=[pPB, PB], W=[P2B])
                    P, PB = P2, P2B
                results[n] = (P, PB, aT_, aTB_)
                if getattr(self, "debug", False) and n == 0 and l == 0 and gp == 0:
                    mk.dma(mk.sp, self.dram("dbg_ainv", [64, HG, 64], F32, "Internal"), P[:], reads=[PB])
                    mk.dma(mk.sp, self.dram("dbg_attT", [64, HG, 64], F32, "Internal"), aT_[:], reads=[aTB_])
                yield

            def full():
                b = fb.next()
                return self.ps[b], self.pb[b]

            def scan(n):
                csl = slice(n * 64, (n + 1) * 64)
                bc8 = lambda t3: t3[:, n, hs].unsqueeze(2).to_broadcast([64, HG, 128])
                kf, qf, vf, cfB = chunk_f32(n)
                v3 = lambda ap: ap[0:64, 0:W8].rearrange("p (h d) -> p h d", h=HG)
                pXk, pXkB = full()
                pXv, pXvB = full()
                for hh in range(HG):
                    self.TR(pXk[0:64, hh * 128:(hh + 1) * 128], kf[:, hh, :], self.identF, R=[cfB, self.cstB], W=[pXkB], sig=(hh == HG - 1))
                for hh in range(HG):
                    self.TR(pXv[0:64, hh * 128:(hh + 1) * 128], vf[:, hh, :], self.identF, R=[cfB, self.cstB], W=[pXvB], sig=(hh == HG - 1))
                kd, kdB = kd_r.next()
                vb, vbB = vb_r.next()
                self.I(V, "tensor_tensor", kd[:], v3(pXk), bc8(expGLmG), op=ALU.mult, R=[pXkB, gB], W=[kdB])
                self.I(V, "tensor_tensor", vb[:], v3(pXv), bc8(beta), op=ALU.mult, R=[pXvB, gB], W=[vbB])
                pKS, pKSB = full()
                pQS, pQSB = full()
                for hh in range(HG):
                    self.MM(pKS[0:64, hh * 128:(hh + 1) * 128], kf[:, hh, :], S[:, hh, :], True, True, R=[cfB, SB_], W=[pKSB], sig=(hh == HG - 1))
                for hh in range(HG):
                    self.MM(pQS[0:64, hh * 128:(hh + 1) * 128], qf[:, hh, :], S[:, hh, :], True, True, R=[cfB, SB_], W=[pQSB], sig=(hh == HG - 1))
                yield
                t8, t8B = t_r.next()
                r8, r8B = r_r.next()
                self.I(V, "tensor_tensor", t8[:], v3(pKS), bc8(c1), op=ALU.mult, R=[pKSB, gB], W=[t8B])
                self.I(V, "tensor_tensor", r8[:], vb[:], t8[:], op=ALU.subtract, R=[vbB, t8B], W=[r8B])
                while n not in results:
                    yield
                AinvT, AiB, aT_, aTB_ = results.pop(n)
                pVN, pVNB = full()
                for hh in range(HG):
                    self.MM(pVN[0:64, hh * 128:(hh + 1) * 128], AinvT[:, hh, :], r8[:, hh, :], True, True, R=[AiB, r8B], W=[pVNB], sig=(hh == HG - 1))
                yield
                vn, vnB = vn_r.next()
                self.ACT(vn[:], v3(pVN), AF.Copy, R=[pVNB], W=[vnB])
                pS, pSB = full()
                for hh in range(HG):
                    self.MM(pS[:, hh * 128:(hh + 1) * 128], kd[:, hh, :], vn[:, hh, :], True, True, R=[kdB, vnB], W=[pSB], sig=(hh == HG - 1))
                pAV, pAVB = full()
                for hh in range(HG):
                    self.MM(pAV[0:64, hh * 128:(hh + 1) * 128], aT_[:, hh, :], vn[:, hh, :], True, True, R=[aTB_, vnB], W=[pAVB], sig=(hh == HG - 1))
                yield
                self.I(V, "tensor_tensor", tS[:], S[:], expGL[:, n, hs].unsqueeze(2).to_broadcast([128, HG, 128]), op=ALU.mult, R=[SB_, gB], W=[tSB])
                self.I(V, "tensor_tensor", S[:], tS[:], pS[:, 0:W8].rearrange("p (h d) -> p h d", h=HG), op=ALU.add, R=[tSB, pSB], W=[SB_])
                o8, o8B = o_r.next()
                t8b, t8bB = t_r.next()
                self.I(V, "tensor_tensor", t8b[:], v3(pQS), bc8(expG), op=ALU.mult, R=[pQSB, gB], W=[t8bB])
                self.I(V, "tensor_tensor", o8[:], t8b[:], v3(pAV), op=ALU.add, R=[t8bB, pAVB], W=[o8B])
                yield
                self.I(V, "tensor_tensor", t8b[:], o8[:], o8[:], op=ALU.mult, R=[o8B], W=[t8bB])
                ss, ssB = ss_r.next()
                self.I(V, "tensor_reduce", ss[:], t8b[:], axis=AX.X, op=ALU.add, R=[t8bB], W=[ssB])
                self.ACT(ss[:], ss[:], AF.Ln, R=[ssB, self.ccB], W=[ssB], bias=self.epsc[0:64, :], scale=1.0 / 128)
                self.ACT(ss[:], ss[:], AF.Exp, R=[ssB], W=[ssB], scale=-0.5)
                on, onB = on_r.next()
                self.I(V, "tensor_tensor", on[:], o8[:], ss[:].unsqueeze(2).to_broadcast([64, HG, 128]), op=ALU.mult, R=[o8B, ssB], W=[onB])
                pO, pOB = full()
                for hh in range(HG):
                    self.TR(pO[:, hh * 64:(hh + 1) * 64], on[:, hh, :], I64, R=[onB, self.cstB], W=[pOB], sig=(hh == HG - 1))
                yield
                self.I(V, "tensor_scalar", ogT[:, hs, csl], pO[:, 0:W4].rearrange("p (h c) -> p h c", h=HG), self.par[:, l, self.o_gn:self.o_gn + 1], None,
                       op0=ALU.mult, R=[pOB, self.parB], W=[ogB])
                yield

            inv_gen = None
            inv_next = 0
            g2mode = getattr(self, "g2mode", "")
            if g2mode.startswith("inv"):
                lim_ = int(g2mode[3:] or 99)
                for n in range(NCH):
                    for i_, _ in enumerate(inverse(n)):
                        if i_ + 1 >= lim_:
                            break
            for n in range(NCH if not g2mode.startswith("inv") else 0):
                sg_ = scan(n)
                done = False
                while not done:
                    if inv_gen is None and inv_next < NCH and inv_next <= n + 1:
                        inv_gen = inverse(inv_next)
                        inv_next += 1
                    if inv_gen is not None:
                        try:
                            next(inv_gen)
                        except StopIteration:
                            inv_gen = None
                    try:
                        next(sg_)
                    except StopIteration:
                        done = True
            assert inv_gen is None or all(False for _ in inv_gen)
            self.pop()
            self.pop()
        self.pop()
        if gstop <= 2:
            self.pop()
            return
        self.mark(f'gdn_g3_{l}')
        self.push()
        zb = Ring([(self.sb([128, TB], F32, "zb"), Buf("zb")) for _ in range(4)])
        for h in range(GH):
            for tb in range(NTB):
                sl = slice(tb * TB, (tb + 1) * TB)
                z_, zB = zb.next()
                mk.dma(mk.sp, z_[:], self.s_z[h, :, sl], writes=[zB])
                self.ACT(z_[:], z_[:], AF.Silu, R=[zB], W=[zB])
                self.I(V, "tensor_tensor", ogT[:, h, sl], ogT[:, h, sl], z_[:], op=ALU.mult, R=[ogB, zB], W=[ogB])
        self.pop()
        if getattr(self, "debug", False):
            dbg = self.dram(f"dbg_og{l}", [GH, 128, T], BF16, "Internal")
            mk.dma(mk.sp, dbg.rearrange("h p t -> p h t"), ogT[:], reads=[ogB])
        mT = self.sb([128, KD, T], BF16, "mT"); mTB = Buf("mT")
        self._pset = 0
        slots = Ring([(self.sb([128, max(GH, KD), 512], BF16, "wog"), Buf("wog")) for _ in range(2)])
        gt = Ring([(self.sb([128, TB], BF16, "gbt"), Buf("gbt")) for _ in range(3 * NTB)])
        mat = Ring([(self.sb([128, TB], BF16, "mat"), Buf("mat")) for _ in range(3 * NTB)])
        tm = Ring([(self.sb([128, TB], F32, "tm"), Buf("tm")) for _ in range(3)])
        pendb = {}

        def pre_b(g):
            for tb in range(NTB):
                sl = slice(tb * TB, (tb + 1) * TB)
                g2, g2B = gt.next()
                mk.dma(mk.sp, g2[:], self.s_gate[KD + g, :, sl], writes=[g2B])
                ma, maB = mat.next()
                mk.dma(mk.sp, ma[:], self.s_ma[g, :, sl], writes=[maB])
                pendb[(g, tb)] = (g2, g2B, ma, maB)

        def evac_b(g, M, tb, psap, psB):
            sl = slice(tb * TB, (tb + 1) * TB)
            g2, g2B, ma, maB = pendb.pop((g, tb))
            t_, tB = tm.next()
            self.I(V, "tensor_tensor", t_[:], psap, g2[:], op=ALU.mult, R=[psB, g2B], W=[tB])
            self.I(V, "tensor_tensor", mT[:, g, sl], t_[:], ma[:], op=ALU.add, R=[tB, maB], W=[mTB])
        self.stream_mm(self.w_o_gdn[l], self.col_blocks(0, cfg.D), ogT, ogB, GH, evac_b, slots, pre=pre_b)
        xt = Ring([(self.sb([128, TB], F32, "xt"), Buf("xt")) for _ in range(3 * NTB)])
        gtc = 2 * KD
        pendo = {}

        def pre_o(g):
            for tb in range(NTB):
                x_, xB = xt.next()
                key = (g, tb)
                xb_ = self.xTB.setdefault(key, Buf(f"xT{key}"))
                mk.dma(mk.sp, x_[:], self.xT[g, :, tb * TB:(tb + 1) * TB], reads=[xb_], writes=[xB])
                pendo[key] = (x_, xB, xb_)

        def evac_o(g, M, tb, psap, psB):
            sl = slice(tb * TB, (tb + 1) * TB)
            x_, xB, xb_ = pendo.pop((g, tb))
            self.I(V, "scalar_tensor_tensor", x_[:], psap, self.mod[:, l, gtc + g:gtc + g + 1], x_[:], op0=ALU.mult, op1=ALU.add,
                   R=[psB, self.modB, xB], W=[xB])
            mk.dma(mk.sp, self.xT[g, :, sl], x_[:], reads=[xB], writes=[xb_])
        self.stream_mm(self.w_o[l], self.col_blocks(0, cfg.D), mT, mTB, KD, evac_o, slots, pre=pre_o)
        self.pop()

    def phase_ffn(self, l, hT, hB):
        self.mark(f'ffn_{l}')
        cfg, mk = self.cfg, self.mk
        T, TB, NTB, KD, FQ, DFF = cfg.T, cfg.TB, cfg.NTB, cfg.KD, cfg.FQ, cfg.DFF
        V = mk.dve
        act = self.sb([128, FQ, T], BF16, "ffact"); actB = Buf("ffact")
        slots = Ring([(self.sb([128, max(KD, FQ), 512], BF16, "wff"), Buf("wff")) for _ in range(2)])
        sg = {}
        sgr = Ring([(self.sb([128, TB], F32, "sg"), Buf("sg")) for _ in range(2 * NTB + 2)])
        xt = Ring([(self.sb([128, TB], F32, "fxt"), Buf("fxt")) for _ in range(2 * NTB + 2)])
        pendd = {}
        gtc = 5 * KD
        self._pset = 0
        for q in range(cfg.NFP):
            blocks = []
            for j0 in range(0, FQ, 2):
                segs, groups = [], []
                off = 0
                for j in range(j0, min(FQ, j0 + 2)):
                    groups += [(off, 128, ("g", j)), (off + 128, 128, ("u", j))]
                    off += 256
                segs = [((q * FQ + j0) * 256, off)]
                blocks.append((segs, groups))

            def evac_gu(gid, M, tb, psap, psB):
                kind, j = gid
                sl = slice(tb * TB, (tb + 1) * TB)
                if kind == "g":
                    s_, sB = sgr.next()
                    self.ACT(s_[:], psap, AF.Silu, R=[psB], W=[sB])
                    sg[(j, tb)] = (s_, sB)
                else:
                    s_, sB = sg.pop((j, tb))
                    self.I(V, "tensor_tensor", act[:, j, sl], psap, s_[:], op=ALU.mult, R=[psB, sB], W=[actB])
            self.stream_mm(self.w_gate_up[l], blocks, hT, hB, KD, evac_gu, slots)

            def pre_d(g):
                for tb in range(NTB):
                    x_, xB = xt.next()
                    key = (g, tb)
                    xb_ = self.xTB.setdefault(key, Buf(f"xT{key}"))
                    mk.dma(mk.sp, x_[:], self.xT[g, :, tb * TB:(tb + 1) * TB], reads=[xb_], writes=[xB])
                    pendd[key] = (x_, xB, xb_)

            def evac_d(g, M, tb, psap, psB):
                sl = slice(tb * TB, (tb + 1) * TB)
                x_, xB, xb_ = pendd.pop((g, tb))
                self.I(V, "scalar_tensor_tensor", x_[:], psap, self.mod[:, l, gtc + g:gtc + g + 1], x_[:], op0=ALU.mult, op1=ALU.add,
                       R=[psB, self.modB, xB], W=[xB])
                mk.dma(mk.sp, self.xT[g, :, sl], x_[:], reads=[xB], writes=[xb_])
            self.stream_mm(self.w_down[l][q * FQ * 128:(q + 1) * FQ * 128, :], self.col_blocks(0, cfg.D), act, actB, FQ, evac_d, slots, pre=pre_d)

    def phase_final(self):
        self.mark('phase_final')
        cfg, mk = self.cfg, self.mk
        KD, T = cfg.KD, cfg.T
        V = mk.dve
        NBF = cfg.NB
        self.push()
        xb = Ring([(self.sb([128, KD, NBF], F32, "fx"), Buf("fx")) for _ in range(3)])
        sq = Ring([(self.sb([128, KD, NBF], BF16, "fsq"), Buf("fsq")) for _ in range(2)])
        rs = Ring([(self.sb([128, NBF], F32, "frs"), Buf("frs")) for _ in range(2)])
        banks = Ring([1, 2, 3, 4, 5, 6, 7])
        xTv = self.xT.rearrange("k p t -> p k t")
        gsz = min(4, KD)
        nblk = T // NBF
        loaded = {}

        def fload(bi):
            if bi < nblk:
                x2, x2B = xb.next()
                mk.dma(mk.sp, x2[:], xTv[:, :, bi * NBF:(bi + 1) * NBF], writes=[x2B])
                loaded[bi] = (x2, x2B)
        fload(0)
        fload(1)
        for blk in range(nblk):
            sl = slice(blk * NBF, (blk + 1) * NBF)
            x_, xB = loaded.pop(blk)
            s_, sB = sq.next()
            self.ACT(s_[:], x_[:], AF.Square, R=[xB], W=[sB])
            for k in range(KD):
                self.MM(self.ps[0][:, 0:NBF], self.onesB, s_[:, k, :], k == 0, k == KD - 1, R=[self.cbB, sB], W=[self.pb[0]], sig=(k == KD - 1))
            r_, rB = rs.next()
            self.ACT(r_[:], self.ps[0][:, 0:NBF], AF.Sqrt, R=[self.pb[0], self.ccB], W=[rB], bias=self.epsc, scale=1.0 / cfg.D)
            self.I(V, "reciprocal", r_[:], r_[:], R=[rB], W=[rB])
            for k in range(KD):
                self.I(V, "scalar_tensor_tensor", x_[:, k, :], x_[:, k, :], self.fn[:, k:k + 1], r_[:], op0=ALU.mult, op1=ALU.mult,
                       R=[xB, self.fnB, rB], W=[xB])
            mk.dma(mk.sp, self.out_d.rearrange("k p t -> p k t")[:, :, sl], x_[:], reads=[xB], writes=[self.outB])
            fload(blk + 2)
        self.pop()


_PROGRAM_CACHE = {}


def make_in_map(cfg, inp, b):
    L, KD, GH = cfg.L, cfg.KD, cfg.GH
    f = lambda a: np.ascontiguousarray(np.asarray(a))
    colT = lambda a, n: np.asarray(a).reshape(L, n, 128).transpose(2, 0, 1)
    par = np.zeros((128, L, 8 * KD + 16), np.float32)
    par[:, :, 0:KD] = colT(inp["norm_mix"], KD)
    par[:, :, KD:2 * KD] = colT(inp["norm_ffn"], KD)
    par[:, :, 2 * KD:8 * KD] = colT(inp["b_ada"], 6 * KD)
    par[:, :, 8 * KD:8 * KD + cfg.NLq] = colT(inp["q_a_norm"], cfg.NLq)
    par[:, :, 8 * KD + 4:8 * KD + 4 + cfg.NLk] = colT(inp["kv_a_norm"], cfg.NLk)
    par[:, :, 8 * KD + 8:8 * KD + 9] = colT(inp["gdn_norm"], 1)
    cw = np.asarray(inp["conv_w"]).reshape(L, 4, 3 * GH, 128).transpose(3, 0, 2, 1)
    nff = cfg.DFF // 128
    wgu = np.asarray(inp["w_gate_up"]).reshape(L, cfg.D, 2, nff, 128).transpose(0, 1, 3, 2, 4).reshape(L, cfg.D, 2 * cfg.DFF)
    m = {
        "x": f(np.asarray(inp["x"][b]).T.reshape(KD, 128, cfg.T)),
        "c": f(np.asarray(inp["c"][b]).reshape(KD, 128).T),
        "pos": f(inp["positions"][b]).reshape(1, cfg.T).astype(np.int32),
        "consts": make_consts(),
        "w_ada": f(inp["w_ada"]),
        "par": f(par),
        "cw": f(cw),
        "w_in": f(inp["w_in"]),
        "w_uq": f(inp["w_uq"]),
        "w_ukv": f(inp["w_ukv"]),
        "w_o_mla": f(inp["w_o_mla"]),
        "A_log": f(inp["A_log"]).reshape(L, 1, GH),
        "dt_bias": f(inp["dt_bias"]).reshape(L, 1, GH),
        "w_o_gdn": f(inp["w_o_gdn"]),
        "w_o": f(inp["w_o"]),
        "w_gate_up": f(wgu),
        "w_down": f(inp["w_down"]),
        "final_norm": f(np.asarray(inp["final_norm"]).reshape(KD, 128).T),
    }
    return m


def kernel(**inputs):
    cfg = Cfg()
    B = inputs["x"].shape[0]
    nc = Builder(cfg).build()
    shared = None
    in_maps = []
    for b in range(B):
        m = make_in_map(cfg, inputs, b) if shared is None else dict(shared)
        if shared is None:
            shared = m
        else:
            m["x"] = np.ascontiguousarray(np.asarray(inputs["x"][b]).T.reshape(cfg.KD, 128, cfg.T))
            m["c"] = np.ascontiguousarray(np.asarray(inputs["c"][b]).reshape(cfg.KD, 128).T)
            m["pos"] = np.ascontiguousarray(inputs["positions"][b]).reshape(1, cfg.T).astype(np.int32)
        in_maps.append(m)
    res = run_bass_kernel_spmd(nc, in_maps, core_ids=list(range(B)))
    out = np.stack([np.asarray(r["out"]).reshape(cfg.D, cfg.T).T for r in res.results], axis=0)
    return np.ascontiguousarray(out.astype(np.float32))
```
